# Optimizing a Trainium2 kernel written in Bass

```python
import jax, jax.numpy as jnp
from jax import lax
import numpy as np

D_MODEL = 1024
BATCH = 8
SEQ = 4096
DEPTH = 4

HEAD_DIM = 64
HG_HEADS = 4
HG_KEY = 64
HG_VAL = 64
SB_HEADS = 6
FOX_HEADS = 6
HG_KW = HG_HEADS * HG_KEY
HG_WIDTH = HG_HEADS * HG_VAL
SB_WIDTH = SB_HEADS * HEAD_DIM
FOX_WIDTH = FOX_HEADS * HEAD_DIM
MIX_WIDTH = HG_WIDTH + SB_WIDTH + FOX_WIDTH
SPLIT_SIZES = (HG_KW, HG_KW, HG_WIDTH, HG_WIDTH, SB_WIDTH, SB_WIDTH, SB_WIDTH,
               FOX_WIDTH, FOX_WIDTH, FOX_WIDTH, FOX_HEADS)
IN_COLS = sum(SPLIT_SIZES)
D_FF = 4 * D_MODEL
HG_CHUNK = 64
BLOCK_Q = 128
EPS = 1e-6
LB_FLOOR = 1e-30
NEG_BIG = -1e30

kernel_name = "hybrid_hgrn2_stickbreak_fox_block"


def rms_norm(x, g):
    x32 = x.astype(jnp.float32)
    y = x32 * lax.rsqrt(jnp.mean(x32 * x32, axis=-1, keepdims=True) + EPS)
    return y * g.astype(jnp.float32)


def hgrn2_mixer(q, f_logit, i, g, lb, norm_g):
    B, T, _ = q.shape
    dt = i.dtype
    C, H, N = HG_CHUNK, HG_HEADS, T // HG_CHUNK
    lb = lb.astype(jnp.float32)
    fl = f_logit.astype(jnp.float32)
    log_f = jnp.logaddexp(jax.nn.log_sigmoid(fl),
                          jnp.log(jnp.maximum(lb, LB_FLOOR)) + jax.nn.log_sigmoid(-fl))
    k = (1.0 - lb) * jax.nn.sigmoid(-fl)

    def to_chunks(a, d):
        return a.astype(jnp.float32).reshape(B, N, C, H, d).transpose(1, 0, 3, 2, 4)

    xs = (to_chunks(q, HG_KEY), to_chunks(k, HG_KEY), to_chunks(i, HG_VAL), to_chunks(log_f, HG_KEY))
    causal = jnp.tril(jnp.ones((C, C), bool))[None, None, :, :, None]

    def step(S, inp):
        qc, kc, vc, lfc = inp
        b = jnp.cumsum(lfc, axis=2)
        diff = b[:, :, :, None, :] - b[:, :, None, :, :]
        decay = jnp.where(causal, jnp.exp(jnp.where(causal, diff, 0.0)), 0.0)
        A = jnp.einsum('bhtk,bhsk,bhtsk->bhts', qc, kc, decay)
        o = (jnp.einsum('bhtk,bhkv->bhtv', qc * jnp.exp(b), S)
             + jnp.einsum('bhts,bhsv->bhtv', A, vc))
        b_end = b[:, :, -1:, :]
        S_new = (jnp.exp(b_end[:, :, 0, :])[..., None] * S
                 + jnp.einsum('bhsk,bhsv->bhkv', kc * jnp.exp(b_end - b), vc))
        return S_new, o

    S0 = jnp.zeros((B, H, HG_KEY, HG_VAL), jnp.float32)
    _, o = lax.scan(step, S0, xs)
    o = o.transpose(1, 0, 3, 2, 4).reshape(B, T, H, HG_VAL)
    o = rms_norm(o, norm_g).reshape(B, T, HG_WIDTH).astype(dt)
    return o * jax.nn.silu(g)


def to_heads(a, n):
    B, T, _ = a.shape
    return a.reshape(B, T, n, HEAD_DIM).transpose(0, 2, 1, 3)


def from_blocks(o):
    nb, B, H, Q, D = o.shape
    return o.transpose(1, 0, 3, 2, 4).reshape(B, nb * Q, H * D)


def stick_breaking_attn(q, k, v):
    B, H, T, D = q.shape
    nb = T // BLOCK_Q
    scale = jnp.float32(1.0 / np.sqrt(HEAD_DIM))
    qb = q.reshape(B, H, nb, BLOCK_Q, D).transpose(2, 0, 1, 3, 4)
    kpos = jnp.arange(T)

    def block(args):
        qi, idx = args
        qpos = idx * BLOCK_Q + jnp.arange(BLOCK_Q)
        z = jnp.einsum('bhqd,bhsd->bhqs', qi, k) * scale
        mask = kpos[None, :] < qpos[:, None]
        log_1mb = jnp.where(mask, jax.nn.log_sigmoid(-z), 0.0)
        tail = lax.cumsum(log_1mb, axis=3, reverse=True) - log_1mb
        A = jnp.where(mask, jnp.exp(jnp.where(mask, jax.nn.log_sigmoid(z) + tail, 0.0)), 0.0)
        return jnp.einsum('bhqs,bhsd->bhqd', A.astype(v.dtype), v)

    return from_blocks(lax.map(block, (qb, jnp.arange(nb))))


def forgetting_attn(q, k, v, c):
    B, H, T, D = q.shape
    nb = T // BLOCK_Q
    scale = jnp.float32(1.0 / np.sqrt(HEAD_DIM))
    qb = q.reshape(B, H, nb, BLOCK_Q, D).transpose(2, 0, 1, 3, 4)
    cb = c.reshape(B, H, nb, BLOCK_Q).transpose(2, 0, 1, 3)
    kpos = jnp.arange(T)

    def block(args):
        qi, ci, idx = args
        qpos = idx * BLOCK_Q + jnp.arange(BLOCK_Q)
        mask = kpos[None, :] <= qpos[:, None]
        bias = jnp.where(mask, ci[..., :, None] - c[..., None, :], 0.0)
        s = jnp.einsum('bhqd,bhsd->bhqs', qi, k) * scale + bias
        p = jax.nn.softmax(jnp.where(mask, s, NEG_BIG), axis=-1)
        return jnp.einsum('bhqs,bhsd->bhqd', p.astype(v.dtype), v)

    return from_blocks(lax.map(block, (qb, cb, jnp.arange(nb))))


def setup_inputs(seed: int = 0) -> dict:
    key = jax.random.key(seed)
    ks = jax.random.split(key, 16)
    f32 = jnp.float32
    nrm = lambda k, shape, s: jax.random.normal(k, shape, f32) * s
    return {
        "x": jax.random.normal(ks[0], (BATCH, SEQ, D_MODEL), f32),
        "lb_logits": nrm(ks[1], (DEPTH, HG_KW), 0.1),
        "norm1_g": 1.0 + nrm(ks[2], (DEPTH, D_MODEL), 0.02),
        "w_in": nrm(ks[3], (DEPTH, D_MODEL, IN_COLS), D_MODEL ** -0.5),
        "hg_norm_g": 1.0 + nrm(ks[4], (DEPTH, HG_VAL), 0.02),
        "sb_q_norm_g": 1.0 + nrm(ks[5], (DEPTH, HEAD_DIM), 0.02),
        "sb_k_norm_g": 1.0 + nrm(ks[6], (DEPTH, HEAD_DIM), 0.02),
        "fox_q_norm_g": 1.0 + nrm(ks[7], (DEPTH, HEAD_DIM), 0.02),
        "fox_k_norm_g": 1.0 + nrm(ks[8], (DEPTH, HEAD_DIM), 0.02),
        "fox_f_bias": 3.0 + nrm(ks[9], (DEPTH, FOX_HEADS), 0.1),
        "w_out": nrm(ks[10], (DEPTH, MIX_WIDTH, D_MODEL), MIX_WIDTH ** -0.5),
        "norm2_g": 1.0 + nrm(ks[11], (DEPTH, D_MODEL), 0.02),
        "w_ff1": nrm(ks[12], (DEPTH, D_MODEL, D_FF), D_MODEL ** -0.5),
        "w_ff2": nrm(ks[13], (DEPTH, D_FF, D_MODEL), D_FF ** -0.5),
    }


def reference(x, lb_logits, norm1_g, w_in, hg_norm_g, sb_q_norm_g, sb_k_norm_g,
              fox_q_norm_g, fox_k_norm_g, fox_f_bias, w_out, norm2_g, w_ff1, w_ff2):
    dt = x.dtype
    p_lb = jax.nn.softmax(lb_logits.astype(jnp.float32), axis=0)
    lower_bounds = jnp.cumsum(p_lb, axis=0) - p_lb[0:1]
    offsets = [int(o) for o in np.cumsum(SPLIT_SIZES)[:-1]]

    for l in range(DEPTH):
        h = rms_norm(x, norm1_g[l]).astype(dt)
        proj = h @ w_in[l]
        (hq, hf, hi, hg, sq, sk, sv, fq, fk, fv, ff) = jnp.split(proj, offsets, axis=-1)

        o_hg = hgrn2_mixer(hq, hf, hi, hg, lower_bounds[l], hg_norm_g[l])

        o_sb = stick_breaking_attn(rms_norm(to_heads(sq, SB_HEADS), sb_q_norm_g[l]),
                                   rms_norm(to_heads(sk, SB_HEADS), sb_k_norm_g[l]),
                                   to_heads(sv, SB_HEADS)).astype(dt)

        log_fg = jax.nn.log_sigmoid(ff.astype(jnp.float32) + fox_f_bias[l].astype(jnp.float32))
        c = jnp.cumsum(log_fg, axis=1).transpose(0, 2, 1)
        o_fox = forgetting_attn(rms_norm(to_heads(fq, FOX_HEADS), fox_q_norm_g[l]),
                                rms_norm(to_heads(fk, FOX_HEADS), fox_k_norm_g[l]),
                                to_heads(fv, FOX_HEADS), c).astype(dt)

        mix = jnp.concatenate([o_hg, o_sb, o_fox], axis=-1)
        x = x + mix @ w_out[l]

        h2 = rms_norm(x, norm2_g[l]).astype(dt)
        x = x + jnp.square(jax.nn.relu(h2 @ w_ff1[l])) @ w_ff2[l]
    return x
```

```python
import contextlib
import os
import numpy as np
import concourse.bass as bass
import concourse.mybir as mybir
from concourse.bass_utils import run_bass_kernel_spmd

F32 = mybir.dt.float32
BF16 = mybir.dt.bfloat16
AF = mybir.ActivationFunctionType
ALU = mybir.AluOpType
AX = mybir.AxisListType

D = 1024
INC = 3334
DFF = 4096
EPS = 1e-6
OFF = dict(hq=0, hf=256, hi=512, hg=768, sq=1024, sk=1408, sv=1792, fq=2176, fk=2560,
           fv=2944, ff=3328)
N_DMA_SEMS = {"hw": 24, "sw": 8}


class Op:
    __slots__ = ("eng", "fn", "waits", "signal", "dma_idx", "seq", "count")

    def __init__(self, eng, fn):
        self.eng = eng
        self.fn = fn
        self.waits = []
        self.signal = False
        self.dma_idx = None
        self.seq = None
        self.count = None


class Prog:
    ENGS = ("pe", "act", "dve", "pool", "sp")

    def __init__(self, nc):
        self.nc = nc
        self.ops = {e: [] for e in self.ENGS}
        self.state = {}
        self.waited = {e: {} for e in self.ENGS}
        self.n_dma = {"hw": 0, "sw": 0}

    def _add(self, eng, fn, reads, writes, dma=False):
        op = Op(eng, fn)
        op.seq = len(self.ops[eng])
        if dma:
            pool = "sw" if eng == "pool" else "hw"
            op.dma_idx = (pool, self.n_dma[pool])
            self.n_dma[pool] += 1
            tok = ("d", pool, op.dma_idx[1])
            if op.dma_idx[1] >= N_DMA_SEMS[pool]:
                self._need(op, ("d", pool, op.dma_idx[1] - N_DMA_SEMS[pool]))
        else:
            tok = ("e", eng, op.seq)
        st = self.state
        xb_ = [k for k in reads if isinstance(k, tuple) and k[0] == "bank" and k not in writes]
        if xb_:
            writes = list(writes) + xb_
        for k in reads:
            s = st.get(k)
            if s is not None and s[0] is not None:
                self._need(op, s[0])
        for k in writes:
            s = st.get(k)
            if s is None:
                continue
            w = s[0]
            if w is not None:
                self._need(op, w)
            for rt in s[1].values():
                self._need(op, rt)
        for k in reads:
            s = st.setdefault(k, [None, {}])
            if dma:
                s[1][tok] = tok
            else:
                s[1][eng] = tok
        for k in writes:
            st[k] = [tok, {}]
        self.ops[eng].append(op)
        return op

    def _need(self, op, tok):
        if tok[0] == "e":
            key, val = tok[1], tok[2]
        else:
            key, val = ("d", tok[1], tok[2] % N_DMA_SEMS[tok[1]]), tok[2]
        wd = self.waited[op.eng]
        if wd.get(key, -1) >= val:
            return
        wd[key] = val
        op.waits.append(tok)
        if tok[0] == "e":
            self.ops[tok[1]][tok[2]].signal = True

    def pe(self, fn, reads=(), writes=()):
        return self._add("pe", fn, reads, writes)

    def act(self, fn, reads=(), writes=()):
        return self._add("act", fn, reads, writes)

    def dve(self, fn, reads=(), writes=()):
        return self._add("dve", fn, reads, writes)

    def pool(self, fn, reads=(), writes=()):
        return self._add("pool", fn, reads, writes)

    def dma(self, q, fn, reads=(), writes=()):
        return self._add(q, fn, reads, writes, dma=True)

    def barrier(self):
        last = {}
        for e in self.ENGS:
            for op in reversed(self.ops[e]):
                if op.dma_idx is None and op.fn is not None:
                    last[e] = ("e", e, op.seq)
                    break
        ndma = dict(self.n_dma)
        for e in self.ENGS:
            op = Op(e, None)
            op.seq = len(self.ops[e])
            for e2, tok in last.items():
                if e2 != e:
                    self._need(op, tok)
            for pool in ("hw", "sw"):
                for i in range(max(0, ndma[pool] - N_DMA_SEMS[pool]), ndma[pool]):
                    self._need(op, ("d", pool, i))
            self.ops[e].append(op)
        self.state = {}

    def emit(self):
        nc = self.nc
        for e in self.ENGS:
            c = 0
            for op in self.ops[e]:
                if op.dma_idx is None and op.signal:
                    c += 1
                op.count = c
        stats = {}
        with contextlib.ExitStack() as es:
            esem = {e: es.enter_context(nc.semaphore("s_" + e)) for e in self.ENGS}
            dsem = {p: [es.enter_context(nc.semaphore("d%s_%d" % (p, i))) for i in range(N_DMA_SEMS[p])]
                    for p in ("hw", "sw")}
            block = es.enter_context(nc.Block())
            engobj = {"pe": block.tensor, "act": block.scalar, "dve": block.vector,
                      "pool": block.gpsimd, "sp": block.sync}
            allops = self.ops
            for e in self.ENGS:
                def body(eng, e=e):
                    nw = 0
                    for op in allops[e]:
                        for tok in op.waits:
                            if tok[0] == "e":
                                eng.wait_ge(esem[tok[1]], allops[tok[1]][tok[2]].count)
                            else:
                                pl, i = tok[1], tok[2]
                                eng.wait_ge(dsem[pl][i % N_DMA_SEMS[pl]], 16 * (i // N_DMA_SEMS[pl] + 1))
                            nw += 1
                        if op.fn is None:
                            continue
                        ins = op.fn(eng)
                        if op.dma_idx is not None:
                            pl, i = op.dma_idx
                            ins.then_inc(dsem[pl][i % N_DMA_SEMS[pl]], 16)
                        elif op.signal:
                            ins.then_inc(esem[e], 1)
                    stats[e] = (len(allops[e]), nw, allops[e][-1].count if allops[e] else 0)
                engobj[e](body)
        return stats


def build(T, depth, dbg=False, stop=99):
    nc = bass.Bass("TRN2", target_bir_lowering=False)
    NT = T // 128
    NG = T // 512
    P = Prog(nc)

    def din(name, shape, dt=F32):
        return nc.dram_tensor(name, list(shape), dt, kind="ExternalInput").ap()

    def dscr(name, shape, dt):
        return nc.dram_tensor(name, list(shape), dt, kind="Internal").ap()

    x_in = din("x", [T, D])
    w_in = din("w_in", [depth, D, INC])
    w_out = din("w_out", [depth, D, D])
    w_ff1 = din("w_ff1", [depth, D, DFF])
    w_ff2 = din("w_ff2", [depth, DFF, D])
    n1g = din("norm1_g", [depth, D])
    n2g = din("norm2_g", [depth, D])
    cols_in = din("cols", [128, depth * 5])
    lbT_in = din("lbT", [128, 2 * depth])
    fb_in = din("fb", [6, depth])
    y = nc.dram_tensor("y", [T, D], F32, kind="ExternalOutput").ap()

    xa = dscr("xa", [T, D], F32)
    xp = dscr("xp", [T, D], F32)
    xb = dscr("xb", [T, D], F32)
    hTs = dscr("hTs", [8, 128, T], BF16)
    qsb = dscr("qsb", [6, 64, T], BF16)
    ksb = dscr("ksb", [6, 64, T], BF16)
    qfx = dscr("qfx", [6, 67, T], BF16)
    kfx = dscr("kfx", [6, 67, T], BF16)
    vsc = dscr("vsc", [T, 12 * 65], BF16)
    mixT = dscr("mixT", [8, 128, T], BF16)
    dbg_out = {}
    if dbg:
        for nm, shp, dt in (("d_mixT", [8, 128, T], BF16), ("d_xa", [T, D], F32),
                            ("d_qfx", [6, 67, T], BF16), ("d_kfx", [6, 67, T], BF16),
                            ("d_qsb", [6, 64, T], BF16), ("d_ksb", [6, 64, T], BF16)):
            dbg_out[nm] = nc.dram_tensor(nm, shp, dt, kind="ExternalOutput").ap()

    es = contextlib.ExitStack()
    with es:
        def sb(name, shape, dt):
            return es.enter_context(nc.sbuf_tensor("sb_" + name, list(shape), dt))

        def ps(name, shape, dt=F32):
            return es.enter_context(nc.psum_tensor("ps_" + name, list(shape), dt))

        slotA = sb("slotA", [128, 32768], BF16)
        slotB = sb("slotB", [128, 32768], BF16)
        ARENA = 63 * 1024
        arena = sb("arena", [128, ARENA], mybir.dt.uint8)
        ident = sb("ident", [128, 128], BF16)
        ident32 = sb("ident32", [128, 128], F32)
        blk = sb("blk", [128, 128], BF16)
        negtri = sb("negtri", [128, 128], BF16)
        negones = sb("negones", [128, 128], BF16)
        msbw = sb("msbw", [128, 896], BF16)
        mfxw = sb("mfxw", [128, 896], BF16)
        hmask = sb("hmask", [128, 4, 128], BF16)
        rmask = sb("rmask", [128, 512], BF16)
        ones32 = sb("ones32", [128, 64], F32)
        onesb = sb("onesb", [128, 512], BF16)
        cols = sb("cols", [128, depth * 5], F32)
        colq = sb("colq", [128, depth * 2], F32)
        lbx = sb("lbx", [128, 2 * depth], F32)
        lbc = sb("lbc", [128, 2 * depth], F32)
        omlb = sb("omlb", [128, 2 * depth], F32)
        lbs = sb("lbs", [128, 2], F32)
        fbc = sb("fbc", [6, depth], F32)
        nfb = sb("nfb", [6, depth], F32)
        cposTok = sb("cposTok", [128, NT, 6], F32)
        epsc = sb("epsc", [128, 1], F32)
        onec = sb("onec", [128, 1], F32)

        banks = [ps("bank%d" % i, [128, 512], F32)[:, :] for i in range(8)]

        class Arena:
            def __init__(self):
                self.off = 0

            def reset(self):
                self.off = 0

            def get(self, shape, dt):
                nb = int(np.prod(shape[1:])) * (4 if dt == F32 else 2)
                nbytes = (nb + 31) // 32 * 32
                assert self.off + nbytes <= ARENA, ("arena overflow", self.off, nbytes)
                ap = arena[0:shape[0], self.off:self.off + nb].bitcast(dt)
                self.off += nbytes
                if len(shape) == 3:
                    ap = ap.rearrange("p (a b) -> p a b", b=shape[2])
                return ap
        AR = Arena()

        def bankbf(i):
            return banks[i][:, :].bitcast(BF16)

        def setup_consts():
            P.pool(lambda e: e.memset(ident[:, :], 1.0), writes=["ident"])
            P.pool(lambda e: e.affine_select(ident[:, :], ident[:, :], [[1, 128]], ALU.is_equal, 0.0,
                                             base=0, channel_multiplier=-1),
                   reads=["ident"], writes=["ident"])
            P.pool(lambda e: e.memset(ident32[:, :], 1.0), writes=["ident32"])
            P.pool(lambda e: e.affine_select(ident32[:, :], ident32[:, :], [[1, 128]], ALU.is_equal,
                                             0.0, base=0, channel_multiplier=-1),
                   reads=["ident32"], writes=["ident32"])
            P.pool(lambda e: e.memset(blk[:, :], 0.0), writes=["blk"])
            P.pool(lambda e: e.memset(blk[0:64, 0:64], 1.0 / 64), reads=["blk"], writes=["blk"])
            P.pool(lambda e: e.memset(blk[64:128, 64:128], 1.0 / 64), reads=["blk"], writes=["blk"])
            P.pool(lambda e: e.memset(negtri[:, :], -1.0), writes=["negtri"])
            P.pool(lambda e: e.affine_select(negtri[:, :], negtri[:, :], [[-1, 128]], ALU.is_ge, 0.0,
                                             base=0, channel_multiplier=1),
                   reads=["negtri"], writes=["negtri"])
            P.pool(lambda e: e.memset(negones[:, :], -1.0), writes=["negones"])
            P.pool(lambda e: e.memset(onesb[:, :], 1.0), writes=["onesb"])
            P.pool(lambda e: e.memset(ones32[:, :], 1.0), writes=["ones32"])
            P.pool(lambda e: e.memset(epsc[:, :], EPS), writes=["epsc"])
            P.pool(lambda e: e.memset(onec[:, :], 1.0), writes=["onec"])
            P.pool(lambda e: e.memset(msbw[:, :], 1.0), writes=["msb"])
            P.pool(lambda e: e.affine_select(msbw[:, :], msbw[:, :], [[1, 896]], ALU.is_gt,
                                             0.0, base=-384, channel_multiplier=-1),
                   reads=["msb"], writes=["msb"])
            P.pool(lambda e: e.memset(mfxw[:, :], 1.0), writes=["mfx"])
            P.pool(lambda e: e.affine_select(mfxw[:, :], mfxw[:, :], [[1, 896]], ALU.is_ge,
                                             0.0, base=-384, channel_multiplier=-1),
                   reads=["mfx"], writes=["mfx"])
            P.pool(lambda e: e.memset(hmask[:, :, :], 0.0), writes=["hmask"])
            for c in range(2):
                P.pool(lambda e, c=c: e.memset(hmask[c * 64:(c + 1) * 64, :, c * 64:(c + 1) * 64], 1.0),
                       reads=["hmask"], writes=["hmask"])
            P.pool(lambda e: e.affine_select(hmask[:, :, :], hmask[:, :, :], [[0, 4], [1, 128]],
                                             ALU.is_ge, 0.0, base=0, channel_multiplier=-1),
                   reads=["hmask"], writes=["hmask"])
            P.pool(lambda e: e.memset(rmask[:, :], 1.0), writes=["rmask"])
            P.pool(lambda e: e.memset(rmask[:, :].rearrange("p (c t) -> p c t", t=64)[:, :, 0:1], 0.0),
                   reads=["rmask"], writes=["rmask"])
            P.dma("sp", lambda e: e.dma_start(out=cols[:, :], in_=cols_in[:, :]), writes=["cols"])
            P.dma("sp", lambda e: e.dma_start(out=lbx[:, :], in_=lbT_in[:, :]), writes=["lbx"])
            P.dma("sp", lambda e: e.dma_start(out=fbc[:, :], in_=fb_in[:, :]), writes=["fbc"])
            P.dve(lambda e: e.tensor_scalar(nfb[:, :], fbc[:, :], -1.0, None, ALU.mult),
                  reads=["fbc"], writes=["nfb"])
            for l in range(depth):
                for i, c in enumerate((0, 2)):
                    P.dve(lambda e, l=l, i=i, c=c: e.tensor_scalar(
                        colq[:, l * 2 + i:l * 2 + i + 1], cols[:, l * 5 + c:l * 5 + c + 1],
                        0.125, None, ALU.mult), reads=["cols"], writes=["colq"])
            P.act(lambda e: e.activation(lbx[:, :], lbx[:, :], AF.Exp), reads=["lbx"], writes=["lbx"])
            lb3 = lbx[:, :].rearrange("p (j l) -> p j l", l=depth)
            P.dve(lambda e: e.tensor_reduce(lbs[:, :], lb3, AX.X, ALU.add), reads=["lbx"], writes=["lbs"])
            P.dve(lambda e: e.reciprocal(lbs[:, :], lbs[:, :]), reads=["lbs"], writes=["lbs"])
            for j in range(2):
                P.dve(lambda e, j=j: e.tensor_scalar(lbx[:, j * depth:(j + 1) * depth],
                                                     lbx[:, j * depth:(j + 1) * depth],
                                                     lbs[:, j:j + 1], None, ALU.mult),
                      reads=["lbx", "lbs"], writes=["lbx"])
                P.dve(lambda e, j=j: e.memset(lbc[:, j * depth:j * depth + 1], 0.0),
                      reads=["lbc"], writes=["lbc"])
                for l in range(1, depth):
                    P.dve(lambda e, j=j, l=l: e.tensor_tensor(
                        lbc[:, j * depth + l:j * depth + l + 1], lbc[:, j * depth + l - 1:j * depth + l],
                        lbx[:, j * depth + l:j * depth + l + 1], ALU.add),
                        reads=["lbc", "lbx"], writes=["lbc"])
            P.dve(lambda e: e.tensor_scalar(omlb[:, :], lbc[:, :], -1.0, 1.0, ALU.mult, ALU.add),
                  reads=["lbc"], writes=["omlb"])
            for h in range(6):
                for t0 in range(0, T, 512):
                    P.dma("sp", lambda e, h=h, t0=t0: e.dma_start(out=kfx[h, 64:67, t0:t0 + 512],
                                                                   in_=onesb[0:3, :]),
                          reads=["onesb"], writes=[("kfx1", h)])

        def load_w_in(l):
            v = slotA[:, 0:8 * INC].rearrange("p (k n) -> p k n", n=INC)
            for k in range(8):
                for hh in range(2):
                    c0 = hh * 1667
                    P.dma("pool", lambda e, k=k, c0=c0: e.dma_start(
                        out=v[:, k, c0:c0 + 1667], in_=w_in[l, k * 128:(k + 1) * 128, c0:c0 + 1667]),
                        writes=["slotA"])
            return v

        def load_w_out(l):
            v = slotB[:, 0:8 * D].rearrange("p (k n) -> p k n", n=D)
            for k in range(8):
                P.dma("pool", lambda e, k=k: e.dma_start(
                    out=v[:, k, :], in_=w_out[l, k * 128:(k + 1) * 128, :]), writes=["slotB"])
            return v

        def load_ffn_half(l, hf, slot, key):
            w1 = slot[:, 0:16384].rearrange("p (k n) -> p k n", n=2048)
            w2 = slot[:, 16384:32768].rearrange("p (k n) -> p k n", n=1024)
            for k in range(8):
                for hh in range(2):
                    c0 = hf * 2048 + hh * 1024
                    P.dma("pool", lambda e, k=k, c0=c0, hh=hh: e.dma_start(
                        out=w1[:, k, hh * 1024:(hh + 1) * 1024],
                        in_=w_ff1[l, k * 128:(k + 1) * 128, c0:c0 + 1024]), writes=[key])
            for fc in range(16):
                r0 = hf * 2048 + fc * 128
                P.dma("pool", lambda e, fc=fc, r0=r0: e.dma_start(
                    out=w2[:, fc, :], in_=w_ff2[l, r0:r0 + 128, :]), writes=[key])
            return w1, w2

        def mm_group(out, pairs, reads, writes):
            def fn(e):
                n = len(pairs)
                ins = None
                for i, (lt, r) in enumerate(pairs):
                    ins = e.matmul(out, lt, r, start=(i == 0), stop=(i == n - 1))
                return ins
            P.pe(fn, reads, writes)

        def norm_tile(xt, xkey, gbc, hb, hkey, st, stkey, sq_scr, sqkey):
            P.act(lambda e: e.activation(sq_scr, xt, AF.Square, accum_out=st[:, 0:1]),
                  reads=[xkey], writes=[sqkey, stkey])
            P.act(lambda e: e.activation(st[:, 1:2], st[:, 0:1], AF.Ln, bias=epsc[:, 0:1], scale=1.0 / D),
                  reads=[stkey], writes=[stkey])
            P.act(lambda e: e.activation(st[:, 2:3], st[:, 1:2], AF.Exp, scale=-0.5),
                  reads=[stkey], writes=[stkey])
            P.dve(lambda e: e.scalar_tensor_tensor(hb, xt, st[:, 2:3], gbc, ALU.mult, ALU.mult),
                  reads=[xkey, stkey, "gbc"], writes=[hkey])

        def transpose_h(hb, hkey, hT, hTkey, tcol, pbank, pkey):
            for half in range(2):
                pv = bankbf(pbank)

                def fn(e, half=half, pv=pv):
                    ins = None
                    for kk in range(4):
                        k = half * 4 + kk
                        ins = e.transpose(pv[:, kk * 128:(kk + 1) * 128], hb[:, k * 128:(k + 1) * 128],
                                          ident[:, :])
                    return ins
                P.pe(fn, reads=[hkey, "ident"], writes=[pkey])
                src = pv[:, 0:512].rearrange("p (k t) -> p k t", t=128)
                dst = hT[:, half * 4:(half + 1) * 4, tcol:tcol + 128]
                if half == 0:
                    P.act(lambda e, src=src, dst=dst: e.activation(dst, src, AF.Copy),
                          reads=[pkey], writes=[hTkey])
                else:
                    P.dve(lambda e, src=src, dst=dst: e.tensor_copy(dst, src),
                          reads=[pkey], writes=[hTkey])

        def headnorm(psrc, pskey, gcol, gkey, bank_ms, mskey, sq, sqkey, rs, rskey, extra_mul, emkey,
                     dst, dstkey):
            P.act(lambda e: e.activation(sq, psrc, AF.Square), reads=[pskey], writes=[sqkey])
            mm_group(bank_ms, [(blk[:, :], sq)], reads=[sqkey, "blk"], writes=[mskey])
            P.act(lambda e: e.activation(rs, bank_ms, AF.Ln, bias=epsc[:, 0:1]),
                  reads=[mskey], writes=[rskey])
            P.act(lambda e: e.activation(rs, rs, AF.Exp, scale=-0.5), reads=[rskey], writes=[rskey])
            if extra_mul is None:
                P.dve(lambda e: e.scalar_tensor_tensor(dst, psrc, gcol, rs, ALU.mult, ALU.mult),
                      reads=[pskey, rskey, gkey], writes=[dstkey])
            else:
                P.dve(lambda e: e.scalar_tensor_tensor(rs, psrc, gcol, rs, ALU.mult, ALU.mult),
                      reads=[pskey, rskey, gkey], writes=[rskey])
                P.pool(lambda e: e.tensor_tensor(dst, rs, extra_mul, ALU.mult),
                       reads=[rskey, emkey], writes=[dstkey])

        def phase1a(l, xsrc, wv):
            AR.reset()
            gbc = AR.get([128, D], F32)
            xt = [AR.get([128, D], F32) for _ in range(2)]
            sqs = AR.get([128, D], BF16)
            hb = [AR.get([128, D], BF16) for _ in range(2)]
            st = [AR.get([128, 4], F32) for _ in range(2)]
            hT = [AR.get([128, 8, 512], BF16) for _ in range(2)]
            sq = [AR.get([128, 512], BF16) for _ in range(2)]
            rs = [AR.get([128, 512], F32) for _ in range(2)]
            qn = [AR.get([128, 512], BF16) for _ in range(2)]
            vaug = [AR.get([128, 12, 65], BF16) for _ in range(2)]
            spt = AR.get([6, 512], F32)
            cpos = [AR.get([6, 512], F32) for _ in range(2)]
            rr = AR.get([6, 512], F32)
            qh = [AR.get([6, 512], BF16) for _ in range(3)]
            P.dma("sp", lambda e: e.dma_start(out=gbc, in_=n1g[l, :].partition_broadcast(128)),
                  writes=["gbc"])
            for i in range(2):
                P.pool(lambda e, i=i: e.memset(vaug[i][:, :, 64:65], 1.0), writes=[("vaug", i)])
            it = 0
            for g in range(NG):
                hb_ = g % 2
                for ti in range(4):
                    t0 = g * 512 + ti * 128
                    b = it % 2
                    it += 1
                    P.dma("sp", lambda e, b=b, t0=t0: e.dma_start(out=xt[b], in_=xsrc[t0:t0 + 128, :]),
                          writes=[("xt", b)])
                    norm_tile(xt[b], ("xt", b), gbc, hb[b], ("hb", b), st[b], ("st", b), sqs, "sqs")
                    transpose_h(hb[b], ("hb", b), hT[hb_], ("hT", hb_), ti * 128, 7, ("bank", 7))
                hTg = hT[hb_]
                hk = ("hT", hb_)
                for k in range(8):
                    P.dma("act", lambda e, k=k, g=g, hTg=hTg: e.dma_start(
                        out=hTs[k, :, g * 512:(g + 1) * 512], in_=hTg[:, k, :]),
                        reads=[hk], writes=[("hTs", g)])
                ci = 0
                for typ, qk, off, dst_d, gci in (("sb", "q", OFF["sq"], qsb, None),
                                                 ("sb", "k", OFF["sk"], ksb, 1),
                                                 ("fx", "q", OFF["fq"], qfx, None),
                                                 ("fx", "k", OFF["fk"], kfx, 3)):
                    for c in range(3):
                        pb = ci % 2
                        ci += 1
                        bank = banks[pb]
                        c0 = off + c * 128
                        mm_group(bank, [(wv[:, k, c0:c0 + 128], hTg[:, k, :]) for k in range(8)],
                                 reads=[hk, "slotA"], writes=[("bank", pb)])
                        if qk == "q":
                            gcol = colq[:, l * 2 + (0 if typ == "sb" else 1):l * 2 + (0 if typ == "sb" else 1) + 1]
                            gkey = "colq"
                        else:
                            gcol = cols[:, l * 5 + gci:l * 5 + gci + 1]
                            gkey = "cols"
                        headnorm(bank, ("bank", pb), gcol, gkey, banks[2 + pb], ("bank", 2 + pb),
                                 sq[pb], ("sq", pb), rs[pb], ("rs", pb), None, None, qn[pb], ("qn", pb))
                        for hp in range(2):
                            P.dma("sp", lambda e, pb=pb, hp=hp, c=c, g=g, dst_d=dst_d: e.dma_start(
                                out=dst_d[2 * c + hp, 0:64, g * 512:(g + 1) * 512],
                                in_=qn[pb][hp * 64:(hp + 1) * 64, :]),
                                reads=[("qn", pb)], writes=[("qkd", typ, qk, 2 * c + hp, g)])
                for ti in range(4):
                    vb = (g * 4 + ti) % 2
                    for vi, off in enumerate((OFF["sv"], OFF["fv"])):
                        bank = banks[4 + vi]
                        mm_group(bank[:, 0:384],
                                 [(hTg[:, k, ti * 128:(ti + 1) * 128], wv[:, k, off:off + 384])
                                  for k in range(8)],
                                 reads=[hk, "slotA"], writes=[("bank", 4 + vi)])
                        src = bank[:, 0:384].rearrange("p (h d) -> p h d", d=64)
                        dst = vaug[vb][:, vi * 6:(vi + 1) * 6, 0:64]
                        if vi == 0:
                            P.act(lambda e, src=src, dst=dst: e.activation(dst, src, AF.Copy),
                                  reads=[("bank", 4 + vi)], writes=[("vaug", vb)])
                        else:
                            P.dve(lambda e, src=src, dst=dst: e.tensor_copy(dst, src),
                                  reads=[("bank", 4 + vi)], writes=[("vaug", vb)])
                    t0 = g * 512 + ti * 128
                    P.dma("sp", lambda e, vb=vb, t0=t0: e.dma_start(
                        out=vsc[t0:t0 + 128, :].rearrange("p (h d) -> p h d", d=65), in_=vaug[vb]),
                        reads=[("vaug", vb)], writes=[("vsc", t0)])
                if os.environ.get("SKIPC") == "1":
                    continue
                mm_group(banks[6][0:6, :], [(wv[:, k, OFF["ff"]:OFF["ff"] + 6], hTg[:, k, :]) for k in range(8)],
                         reads=[hk, "slotA"], writes=[("bank", 6)])
                P.act(lambda e: e.activation(spt, banks[6][0:6, :], AF.Exp, bias=nfb[:, l:l + 1], scale=-1.0),
                      reads=[("bank", 6), "nfb"], writes=["spt"])
                P.act(lambda e: e.activation(spt, spt, AF.Ln, bias=onec[0:6, 0:1]), reads=["spt"], writes=["spt"])
                cb = g % 2
                if g == 0:
                    P.dve(lambda e, cb=cb: e.tensor_tensor_scan(cpos[cb], onesb[0:6, :], spt, 0.0,
                                                                ALU.mult, ALU.add),
                          reads=["spt", "onesb"], writes=[("cpos", cb)])
                else:
                    P.dve(lambda e, cb=cb: e.tensor_tensor_scan(cpos[cb], onesb[0:6, :], spt,
                                                                cpos[1 - cb][:, 511:512], ALU.mult, ALU.add),
                          reads=["spt", "onesb", ("cpos", 1 - cb)], writes=[("cpos", cb)])
                P.dve(lambda e, cb=cb: e.tensor_scalar(qh[0], cpos[cb], -1.0, None, ALU.mult),
                      reads=[("cpos", cb)], writes=["qh0"])
                P.dve(lambda e, cb=cb: e.scalar_tensor_tensor(rr, cpos[cb], -1.0, qh[0], ALU.mult, ALU.subtract),
                      reads=[("cpos", cb), "qh0"], writes=["rr"])
                P.dve(lambda e: e.tensor_copy(qh[1], rr), reads=["rr"], writes=["qh1"])
                P.dve(lambda e: e.tensor_tensor(qh[2], rr, qh[1], ALU.subtract), reads=["rr", "qh1"],
                      writes=["qh2"])
                for i in range(3):
                    P.dma("sp", lambda e, i=i, g=g: e.dma_start(out=qfx[:, 64 + i, g * 512:(g + 1) * 512],
                                                                 in_=qh[i]),
                          reads=["qh%d" % i], writes=[("qfxc", i, g)])
                if os.environ.get("SKIPC3") == "1":
                    continue
                for ti in range(4):
                    P.pe(lambda e, ti=ti, cb=cb: e.matmul(banks[6][:, 0:6], cpos[cb][:, ti * 128:(ti + 1) * 128],
                                                         ident32[0:6, 0:6], start=True, stop=True),
                         reads=[("cpos", cb), "ident32"], writes=[("bank", 6)])
                    P.dve(lambda e, ti=ti, g=g: e.tensor_copy(cposTok[:, g * 4 + ti, :], banks[6][:, 0:6]),
                          reads=[("bank", 6)], writes=["cposTok"])

        def phase1b(l, wv):
            AR.reset()
            hT = [AR.get([128, 8, 512], BF16) for _ in range(2)]
            t1 = [AR.get([128, 512], F32) for _ in range(2)]
            t2 = AR.get([128, 512], F32)
            t3 = AR.get([128, 512], F32)
            t4 = [AR.get([128, 512], F32) for _ in range(2)]
            t5 = AR.get([128, 512], F32)
            gs = [AR.get([128, 512], F32) for _ in range(2)]
            qT = AR.get([128, 2, 512], BF16)
            kTz = [[AR.get([128, 512], BF16) for _ in range(2)] for _ in range(2)]
            khT = AR.get([128, 2, 512], BF16)
            khat = AR.get([128, 4, 256], BF16)
            vt = AR.get([128, 4, 256], BF16)
            vtz = [AR.get([128, 4, 256], BF16) for _ in range(2)]
            atb = [AR.get([128, 4, 128], BF16) for _ in range(2)]
            S32 = [AR.get([128, 2, 64], F32) for _ in range(2)]
            Sbz = [[AR.get([128, 2, 64], BF16) for _ in range(2)] for _ in range(4)]
            sq = AR.get([128, 512], BF16)
            rs = AR.get([128, 512], F32)
            mo = [AR.get([128, 512], BF16) for _ in range(2)]
            for j in range(2):
                for hp in range(2):
                    P.pool(lambda e, j=j, hp=hp: e.memset(kTz[j][hp], 0.0), writes=[("kTz", j)])
            for c in range(2):
                P.pool(lambda e, c=c: e.memset(vtz[c], 0.0), writes=["vtz"])
            for r in range(4):
                for hp in range(2):
                    P.pool(lambda e, r=r, hp=hp: e.memset(Sbz[r][hp], 0.0), writes=[("Sbz", r)])
            P.pool(lambda e: e.memset(S32[0], 0.0), writes=[("S32", 0)])
            sidx = 0
            ring = 0
            for g in range(NG):
                hb_ = g % 2
                hTg = hT[hb_]
                hk = ("hT", hb_)
                for k in range(8):
                    P.dma("sp" if k % 2 == 0 else "act", lambda e, k=k, g=g, hTg=hTg: e.dma_start(
                        out=hTg[:, k, :], in_=hTs[k, :, g * 512:(g + 1) * 512]), writes=[hk])
                for j in range(2):
                    bf_, bq_, bg_ = banks[0], banks[1], banks[2]
                    mm_group(bf_, [(wv[:, k, OFF["hf"] + j * 128:OFF["hf"] + (j + 1) * 128], hTg[:, k, :])
                                   for k in range(8)], reads=[hk, "slotA"], writes=[("bank", 0)])
                    mm_group(bq_, [(wv[:, k, OFF["hq"] + j * 128:OFF["hq"] + (j + 1) * 128], hTg[:, k, :])
                                   for k in range(8)], reads=[hk, "slotA"], writes=[("bank", 1)])
                    mm_group(bg_, [(wv[:, k, OFF["hg"] + j * 128:OFF["hg"] + (j + 1) * 128], hTg[:, k, :])
                                   for k in range(8)], reads=[hk, "slotA"], writes=[("bank", 2)])
                    a = t1[j]
                    ak = ("t1", j)
                    P.act(lambda e, a=a: e.activation(a, bf_, AF.Exp, scale=-1.0), reads=[("bank", 0)], writes=[ak])
                    P.dve(lambda e, a=a: e.tensor_scalar(a, a, 1.0, None, ALU.add), reads=[ak], writes=[ak])
                    P.dve(lambda e, a=a: e.reciprocal(a, a), reads=[ak], writes=[ak])
                    ci = j * depth + l
                    P.dve(lambda e, a=a, ci=ci: e.tensor_scalar(a, a, omlb[:, ci:ci + 1], lbc[:, ci:ci + 1],
                                                                ALU.mult, ALU.add),
                          reads=[ak, "omlb", "lbc"], writes=[ak])
                    P.act(lambda e, a=a: e.activation(t2, a, AF.Ln), reads=[ak], writes=["t2"])
                    P.dve(lambda e: e.tensor_tensor_scan(t3, rmask[:, :], t2, 0.0, ALU.mult, ALU.add),
                          reads=["t2", "rmask"], writes=["t3"])
                    e4 = t4[j]
                    P.act(lambda e, e4=e4: e.activation(e4, t3, AF.Exp), reads=["t3"], writes=[("t4", j)])
                    P.act(lambda e: e.activation(t5, t3, AF.Exp, scale=-1.0), reads=["t3"], writes=["t5"])
                    P.dve(lambda e, j=j, e4=e4: e.tensor_tensor(qT[:, j, :], bq_, e4, ALU.mult),
                          reads=[("bank", 1), ("t4", j)], writes=[("qT", j)])
                    P.pool(lambda e, a=a: e.tensor_scalar(a, a, -1.0, 1.0, ALU.mult, ALU.add),
                           reads=[ak], writes=[ak])
                    P.pool(lambda e, a=a: e.tensor_tensor(t5, a, t5, ALU.mult), reads=[ak, "t5"], writes=["t5"])
                    for hp in range(2):
                        sl = slice(hp * 64, (hp + 1) * 64)
                        P.pool(lambda e, j=j, hp=hp, sl=sl: e.tensor_copy(kTz[j][hp][sl, :], t5[sl, :]),
                               reads=["t5"], writes=[("kTz", j)])
                    e4v = e4.rearrange("p (c t) -> p c t", t=64)[:, :, 63:64].to_broadcast([128, 8, 64])
                    P.dve(lambda e, j=j, e4v=e4v: e.tensor_tensor(
                        khT[:, j, :].rearrange("p (c t) -> p c t", t=64),
                        t5[:, :].rearrange("p (c t) -> p c t", t=64), e4v, ALU.mult),
                        reads=["t5", ("t4", j)], writes=[("khT", j)])
                    gg = gs[j]
                    gk = ("gs", j)
                    P.act(lambda e, gg=gg: e.activation(gg, bg_, AF.Exp, scale=-1.0), reads=[("bank", 2)], writes=[gk])
                    P.dve(lambda e, gg=gg: e.tensor_scalar(gg, gg, 1.0, None, ALU.add), reads=[gk], writes=[gk])
                    P.dve(lambda e, gg=gg: e.reciprocal(gg, gg), reads=[gk], writes=[gk])
                    P.dve(lambda e, gg=gg: e.tensor_tensor(gg, bg_, gg, ALU.mult), reads=[("bank", 2), gk], writes=[gk])
                B1 = int(os.environ.get("B1", "9"))
                if B1 <= 1:
                    continue
                for ti in range(4):
                    tsl = slice(ti * 128, (ti + 1) * 128)
                    mm_group(banks[3][:, 0:256],
                             [(hTg[:, k, tsl], wv[:, k, OFF["hi"]:OFF["hi"] + 256]) for k in range(8)],
                             reads=[hk, "slotA"], writes=[("bank", 3)])
                    P.act(lambda e, ti=ti: e.activation(vt[:, ti, :], banks[3][:, 0:256], AF.Copy),
                          reads=[("bank", 3)], writes=["vt"])
                    for c in range(2):
                        sl = slice(c * 64, (c + 1) * 64)
                        P.dve(lambda e, ti=ti, c=c, sl=sl: e.tensor_copy(vtz[c][sl, ti, :], banks[3][sl, 0:256]),
                              reads=[("bank", 3)], writes=["vtz"])
                    pv = bankbf(3)

                    def fnT(e, tsl=tsl, pv=pv):
                        ins = None
                        for j in range(2):
                            ins = e.transpose(pv[:, 512 + j * 128:512 + (j + 1) * 128], khT[:, j, tsl], ident[:, :])
                        return ins
                    P.pe(fnT, reads=[("khT", 0), ("khT", 1), "ident"], writes=[("bank", 3)])
                    P.act(lambda e, ti=ti, pv=pv: e.activation(khat[:, ti, :], pv[:, 512:768], AF.Copy),
                          reads=[("bank", 3)], writes=["khat"])
                for ti in range(4):
                    if B1 <= 2:
                        continue
                    tsl = slice(ti * 128, (ti + 1) * 128)
                    ab = ti % 2
                    def fnA(e, tsl=tsl):
                        ins = None
                        for h in range(4):
                            j, hp = h // 2, h % 2
                            ins = e.matmul(banks[4][:, h * 128:(h + 1) * 128], kTz[j][hp][:, tsl], qT[:, j, tsl],
                                           start=True, stop=True)
                        return ins
                    P.pe(fnA, reads=[("kTz", 0), ("kTz", 1), ("qT", 0), ("qT", 1)], writes=[("bank", 4)])
                    P.dve(lambda e, ab=ab: e.tensor_tensor(atb[ab], banks[4].rearrange("p (h t) -> p h t", t=128),
                                                           hmask[:, :, :], ALU.mult),
                          reads=[("bank", 4), "hmask"], writes=[("atb", ab)])
                    def fnU(e, ti=ti):
                        ins = None
                        for c in range(2):
                            for h in range(4):
                                j = h // 2
                                ins = e.matmul(banks[5][:, (c * 4 + h) * 64:(c * 4 + h + 1) * 64],
                                               khat[:, ti, j * 128:(j + 1) * 128],
                                               vtz[c][:, ti, h * 64:(h + 1) * 64], start=True, stop=True)
                        return ins
                    P.pe(fnU, reads=["khat", "vtz"], writes=[("bank", 5)])
                    rings = [ring]
                    for c in range(2):
                        chunkcol = (ti * 2 + c) * 64 + 63
                        so, sn = S32[sidx], S32[1 - sidx]
                        for h in range(4):
                            j, hp = h // 2, h % 2
                            sl = slice(hp * 64, (hp + 1) * 64)
                            P.dve(lambda e, so=so, sn=sn, j=j, sl=sl, c=c, h=h, chunkcol=chunkcol:
                                  e.scalar_tensor_tensor(sn[sl, j, :], so[sl, j, :],
                                                         t4[j][sl, chunkcol:chunkcol + 1],
                                                         banks[5][sl, (c * 4 + h) * 64:(c * 4 + h + 1) * 64],
                                                         ALU.mult, ALU.add),
                                  reads=[("S32", sidx), ("t4", j), ("bank", 5)], writes=[("S32", 1 - sidx)])
                        sidx = 1 - sidx
                        ring = (ring + 1) % 4
                        rings.append(ring)
                        for hp in range(2):
                            sl = slice(hp * 64, (hp + 1) * 64)
                            P.act(lambda e, sl=sl, hp=hp, ring=ring, sidx=sidx:
                                  e.activation(Sbz[ring][hp][sl, :, :], S32[sidx][sl, :, :], AF.Copy),
                                  reads=[("S32", sidx)], writes=[("Sbz", ring)])
                    if B1 <= 3:
                        continue
                    def fnO(e, ti=ti, tsl=tsl, ab=ab, rings=tuple(rings)):
                        ins = None
                        for h in range(4):
                            j, hp = h // 2, h % 2
                            osl = slice(hp * 64, (hp + 1) * 64)
                            ob = banks[6 + j]
                            e.matmul(ob[osl, tsl], vt[:, ti, h * 64:(h + 1) * 64], atb[ab][:, h, :],
                                     start=True, stop=False, skip_group_check=True)
                            for c in range(2):
                                csl = slice(ti * 128 + c * 64, ti * 128 + (c + 1) * 64)
                                ins = e.matmul(ob[osl, csl], Sbz[rings[c]][hp][:, j, :], qT[:, j, csl],
                                               start=False, stop=(c == 1), skip_group_check=True)
                        return ins
                    P.pe(fnO, reads=["vt", ("atb", ab), ("Sbz", rings[0]), ("Sbz", rings[1]), ("qT", 0), ("qT", 1)],
                         writes=[("bank", 6), ("bank", 7)])
                for j in range(2):
                    if B1 <= 4:
                        continue
                    gcol = cols[:, l * 5 + 4:l * 5 + 5]
                    headnorm(banks[6 + j], ("bank", 6 + j), gcol, "cols", banks[j], ("bank", j),
                             sq, "sq", rs, "rs", gs[j], ("gs", j), mo[j], ("mo", j))
                    P.dma("sp", lambda e, j=j, g=g: e.dma_start(out=mixT[j, :, g * 512:(g + 1) * 512], in_=mo[j]),
                          reads=[("mo", j)], writes=[("mixT", j, g)])

        def phase2(l):
            AR.reset()
            kT = [AR.get([67, T], BF16) for _ in range(2)]
            qTt = [AR.get([67, T], BF16) for _ in range(2)]
            vv = [AR.get([128, NT, 65], BF16) for _ in range(2)]
            ee = [AR.get([128, 512], F32) for _ in range(2)]
            spb = [AR.get([128, 512], BF16) for _ in range(3)]
            ssum = AR.get([128, 512], F32)
            ssb = [AR.get([128, 512], BF16) for _ in range(3)]
            aa = [AR.get([128, 512], BF16) for _ in range(3)]
            osb = AR.get([65, 512], F32)
            rec = AR.get([65, 512], F32)
            on = [AR.get([64, 512], BF16) for _ in range(2)]

            def load_head(hh):
                typ = "sb" if hh < 6 else "fx"
                h = hh % 6
                hb_ = hh % 2
                nr = 64 if typ == "sb" else 67
                qd, kd = (qsb, ksb) if typ == "sb" else (qfx, kfx)
                P.dma("sp", lambda e: e.dma_start(out=kT[hb_][0:nr, :], in_=kd[h, 0:nr, :]), writes=[("kT", hb_)])
                P.dma("act", lambda e: e.dma_start(out=qTt[hb_][0:nr, :], in_=qd[h, 0:nr, :]), writes=[("qTt", hb_)])
                P.dma("sp", lambda e: e.dma_start(
                    out=vv[hb_], in_=vsc[:, hh * 65:(hh + 1) * 65].rearrange("(n p) d -> p n d", p=128)),
                    writes=[("vv", hb_)])

            items = []
            gi = 0
            for hh in range(12):
                typ = "sb" if hh < 6 else "fx"
                for B in range(NG):
                    na = 4 * B + 4
                    order = list(range(na - 1, -1, -1)) if typ == "sb" else list(range(na))
                    for idx, a in enumerate(order):
                        items.append(dict(hh=hh, typ=typ, h=hh % 6, hb=hh % 2, nr=(64 if typ == "sb" else 67),
                                          B=B, a=a, r=a - 4 * B, first=(idx == 0), last=(idx == na - 1),
                                          ob=gi % 2, newhead=(B == 0 and idx == 0)))
                    gi += 1
            n = len(items)
            load_head(0)

            def stageA(t, it):
                hb_, nr, a, r = it["hb"], it["nr"], it["a"], it["r"]
                ks = slice(a * 128, (a + 1) * 128)
                qs = slice(it["B"] * 512, (it["B"] + 1) * 512)
                zi = t % 2
                zb, zk = banks[zi], ("bank", zi)
                mm_group(zb, [(kT[hb_][0:nr, ks], qTt[hb_][0:nr, qs])], reads=[("kT", hb_), ("qTt", hb_)], writes=[zk])
                if it["typ"] == "fx":
                    ai = t % 3
                    h = it["h"]
                    P.act(lambda e: e.activation(aa[ai], zb, AF.Exp, bias=cposTok[:, a, h:h + 1]),
                          reads=[zk, "cposTok"], writes=[("aa", ai)])
                    if r >= 0:
                        P.pool(lambda e: e.tensor_tensor(aa[ai], aa[ai], mfxw[:, (3 - r) * 128:(3 - r) * 128 + 512], ALU.mult),
                               reads=[("aa", ai), "mfx"], writes=[("aa", ai)])
                else:
                    ei, si = t % 2, t % 3
                    P.act(lambda e: e.activation(ee[ei], zb, AF.Exp), reads=[zk], writes=[("ee", ei)])
                    it["_ln"] = (ei, si)

            def stageA2(t, it):
                if it["typ"] != "sb":
                    return
                ei, si = it["_ln"]
                r, a = it["r"], it["a"]
                P.act(lambda e: e.activation(spb[si], ee[ei], AF.Ln, bias=onec[:, 0:1]), reads=[("ee", ei)],
                      writes=[("spb", si)])
                if r >= 0:
                    P.pool(lambda e: e.tensor_tensor(spb[si], spb[si], msbw[:, (3 - r) * 128:(3 - r) * 128 + 512], ALU.mult),
                           reads=[("spb", si), "msb"], writes=[("spb", si)])
                if a > 0:
                    nsi = (t + 1) % 3
                    if it["first"]:
                        P.dve(lambda e: e.tensor_copy(ssb[nsi], spb[si]), reads=[("spb", si)], writes=[("ssb", nsi)])
                        P.dve(lambda e: e.tensor_copy(ssum, spb[si]), reads=[("spb", si)], writes=["ssum"])
                    else:
                        P.dve(lambda e: e.tensor_tensor(ssb[nsi], ssum, spb[si], ALU.add),
                              reads=["ssum", ("spb", si)], writes=[("ssb", nsi)])
                        P.dve(lambda e: e.tensor_tensor(ssum, ssum, spb[si], ALU.add),
                              reads=["ssum", ("spb", si)], writes=["ssum"])

            def stageB(t, it):
                if it["typ"] != "sb":
                    return
                hb_, nr, a, r = it["hb"], it["nr"], it["a"], it["r"]
                ks = slice(a * 128, (a + 1) * 128)
                qs = slice(it["B"] * 512, (it["B"] + 1) * 512)
                si, ai = t % 3, t % 3
                li = 2 + t % 2
                lb_, lk = banks[li], ("bank", li)
                prs = [(kT[hb_][0:nr, ks], qTt[hb_][0:nr, qs]), (negtri[:, :], spb[si])]
                rd = [("kT", hb_), ("qTt", hb_), ("spb", si), "negtri"]
                if not it["first"]:
                    prs.append((negones[:, :], ssb[si]))
                    rd += [("ssb", si), "negones"]
                mm_group(lb_, prs, reads=rd, writes=[lk])
                P.act(lambda e: e.activation(aa[ai], lb_, AF.Exp), reads=[lk], writes=[("aa", ai)])
                if r >= 0:
                    P.pool(lambda e: e.tensor_tensor(aa[ai], aa[ai], msbw[:, (3 - r) * 128:(3 - r) * 128 + 512], ALU.mult),
                           reads=[("aa", ai), "msb"], writes=[("aa", ai)])

            def stageC(t, it):
                if it["newhead"] and it["hh"] + 1 < 12:
                    load_head(it["hh"] + 1)
                hb_, a = it["hb"], it["a"]
                ai = t % 3
                ob = banks[6 + it["ob"]]
                okey = ("bank", 6 + it["ob"])
                first, last = it["first"], it["last"]
                P.pe(lambda e: e.matmul(ob[0:65, :], vv[hb_][:, a, :], aa[ai], start=first, stop=last),
                     reads=[("vv", hb_), ("aa", ai)], writes=[okey])
                if not last:
                    return
                hh, B = it["hh"], it["B"]
                qs = slice(B * 512, (B + 1) * 512)
                ob_ = it["ob"]
                if it["typ"] == "fx":
                    P.dve(lambda e: e.reciprocal(rec[64:65, :], ob[64:65, :]), reads=[okey], writes=["rec"])
                    P.dve(lambda e: e.tensor_copy(osb[0:64, :], ob[0:64, :]), reads=[okey], writes=["osb"])
                    mm_group(banks[4][0:64, :], [(ones32[64:65, 0:64], rec[64:65, :])], reads=["rec", "ones32"],
                             writes=[("bank", 4)])
                    P.dve(lambda e: e.tensor_tensor(on[ob_], osb[0:64, :], banks[4][0:64, :], ALU.mult),
                          reads=["osb", ("bank", 4)], writes=[("on", ob_)])
                else:
                    P.dve(lambda e: e.tensor_copy(on[ob_], ob[0:64, :]), reads=[okey], writes=[("on", ob_)])
                fch = 2 + hh // 2
                prow = (hh % 2) * 64
                P.dma("sp", lambda e: e.dma_start(out=mixT[fch, prow:prow + 64, qs], in_=on[ob_]),
                      reads=[("on", ob_)], writes=[("mixT", hh, B)])

            for t in range(n + 2):
                if t < n:
                    stageA(t, items[t])
                if 0 <= t - 1 < n:
                    stageB(t - 1, items[t - 1])
                if t < n:
                    stageA2(t, items[t])
                if 0 <= t - 2 < n:
                    stageC(t - 2, items[t - 2])

        def phase3a(l, xsrc, wo):
            AR.reset()
            mx = [AR.get([128, 8, 512], BF16) for _ in range(2)]
            xt = [AR.get([128, D], F32) for _ in range(3)]
            it = 0
            for g in range(NG):
                mb = g % 2
                for k in range(8):
                    P.dma("sp" if k % 2 == 0 else "act", lambda e, k=k, g=g, mb=mb: e.dma_start(
                        out=mx[mb][:, k, :], in_=mixT[k, :, g * 512:(g + 1) * 512]), writes=[("mx", mb)])
                for ti in range(4):
                    t0 = g * 512 + ti * 128
                    b = it % 3
                    it += 1
                    P.dma("sp", lambda e, b=b, t0=t0: e.dma_start(out=xt[b], in_=xsrc[t0:t0 + 128, :]),
                          writes=[("xt", b)])
                    for hf in range(2):
                        pb = (it * 2 + hf) % 4
                        mm_group(banks[pb], [(mx[mb][:, k, ti * 128:(ti + 1) * 128], wo[:, k, hf * 512:(hf + 1) * 512])
                                             for k in range(8)],
                                 reads=[("mx", mb), "slotB"], writes=[("bank", pb)])
                        P.dve(lambda e, b=b, hf=hf, pb=pb: e.tensor_tensor(
                            xt[b][:, hf * 512:(hf + 1) * 512], xt[b][:, hf * 512:(hf + 1) * 512], banks[pb], ALU.add),
                            reads=[("xt", b), ("bank", pb)], writes=[("xt", b)])
                    P.dma("act", lambda e, b=b, t0=t0: e.dma_start(out=xa[t0:t0 + 128, :], in_=xt[b]),
                          reads=[("xt", b)], writes=[("xa", t0)])

        def phase3f(l, half, src, dst, w1, w2, wkey):
            AR.reset()
            xt = [AR.get([128, D], F32) for _ in range(4)]
            hT = [AR.get([128, 8, 512], BF16) for _ in range(2)]
            hid = AR.get([128, 16, 512], BF16)
            rl = [AR.get([128, 512], F32) for _ in range(2)]
            if half == 0:
                gbc = AR.get([128, D], F32)
                sqs = AR.get([128, D], BF16)
                hb = [AR.get([128, D], BF16) for _ in range(2)]
                st = [AR.get([128, 4], F32) for _ in range(2)]
                P.dma("sp", lambda e: e.dma_start(out=gbc, in_=n2g[l, :].partition_broadcast(128)), writes=["gbc"])
            it = 0
            for g in range(NG):
                hb_ = g % 2
                hTg = hT[hb_]
                hk = ("hT", hb_)
                for ti in range(4):
                    t0 = g * 512 + ti * 128
                    P.dma("sp", lambda e, ti=ti, t0=t0: e.dma_start(out=xt[ti], in_=src[t0:t0 + 128, :]),
                          writes=[("xt", ti)])
                if half == 0:
                    for ti in range(4):
                        b = it % 2
                        it += 1
                        norm_tile(xt[ti], ("xt", ti), gbc, hb[b], ("hb", b), st[b], ("st", b), sqs, "sqs")
                        transpose_h(hb[b], ("hb", b), hTg, hk, ti * 128, 7, ("bank", 7))
                    for k in range(8):
                        P.dma("act", lambda e, k=k, g=g, hTg=hTg: e.dma_start(
                            out=hTs[k, :, g * 512:(g + 1) * 512], in_=hTg[:, k, :]), reads=[hk], writes=[("hTs", g)])
                else:
                    for k in range(8):
                        P.dma("sp" if k % 2 == 0 else "act", lambda e, k=k, g=g, hTg=hTg: e.dma_start(
                            out=hTg[:, k, :], in_=hTs[k, :, g * 512:(g + 1) * 512]), writes=[hk])
                for f in range(16):
                    pb = f % 2
                    mm_group(banks[pb], [(w1[:, k, f * 128:(f + 1) * 128], hTg[:, k, :]) for k in range(8)],
                             reads=[hk, wkey], writes=[("bank", pb)])
                    P.act(lambda e, pb=pb: e.activation(rl[pb], banks[pb], AF.Relu), reads=[("bank", pb)],
                          writes=[("rl", pb)])
                    if f % 2 == 0:
                        P.dve(lambda e, pb=pb, f=f: e.tensor_tensor(hid[:, f, :], rl[pb], rl[pb], ALU.mult),
                              reads=[("rl", pb)], writes=["hid"])
                    else:
                        P.pool(lambda e, pb=pb, f=f: e.tensor_tensor(hid[:, f, :], rl[pb], rl[pb], ALU.mult),
                               reads=[("rl", pb)], writes=["hid"])
                for ti in range(4):
                    t0 = g * 512 + ti * 128
                    for hf in range(2):
                        pb = 2 + (ti * 2 + hf) % 4
                        mm_group(banks[pb], [(hid[:, f, ti * 128:(ti + 1) * 128], w2[:, f, hf * 512:(hf + 1) * 512])
                                             for f in range(16)],
                                 reads=["hid", wkey], writes=[("bank", pb)])
                        P.dve(lambda e, ti=ti, hf=hf, pb=pb: e.tensor_tensor(
                            xt[ti][:, hf * 512:(hf + 1) * 512], xt[ti][:, hf * 512:(hf + 1) * 512], banks[pb],
                            ALU.add), reads=[("xt", ti), ("bank", pb)], writes=[("xt", ti)])
                    P.dma("act", lambda e, ti=ti, t0=t0: e.dma_start(out=dst[t0:t0 + 128, :], in_=xt[ti]),
                          reads=[("xt", ti)], writes=[("dst", t0)])

        def whole():
            setup_consts()
            if stop == -1:
                return
            wv = load_w_in(0)
            P.barrier()
            if stop == 0:
                return
            for l in range(depth):
                xsrc = x_in if l == 0 else xb
                wo = load_w_out(l)
                phase1a(l, xsrc, wv)
                P.barrier()
                if stop == 1:
                    return
                phase1b(l, wv)
                P.barrier()
                if stop == 2:
                    return
                w1a, w2a = load_ffn_half(l, 0, slotA, "slotA")
                phase2(l)
                P.barrier()
                if stop == 3:
                    return
                if dbg and l == 0:
                    for nm, srcd in (("d_mixT", mixT), ("d_qfx", qfx), ("d_kfx", kfx), ("d_qsb", qsb), ("d_ksb", ksb)):
                        dd = dbg_out[nm]
                        for i in range(dd.shape[0]):
                            P.dma("sp", lambda e, dd=dd, srcd=srcd, i=i: e.dma_start(out=dd[i], in_=srcd[i]))
                phase3a(l, xsrc, wo)
                P.barrier()
                if stop == 4:
                    return
                if dbg and l == 0:
                    for t0 in range(0, T, 128):
                        P.dma("sp", lambda e, t0=t0: e.dma_start(out=dbg_out["d_xa"][t0:t0 + 128, :], in_=xa[t0:t0 + 128, :]))
                w1b, w2b = load_ffn_half(l, 1, slotB, "slotB")
                phase3f(l, 0, xa, xp, w1a, w2a, "slotA")
                P.barrier()
                if stop == 5:
                    return
                if l + 1 < depth:
                    wv = load_w_in(l + 1)
                phase3f(l, 1, xp, (y if l == depth - 1 else xb), w1b, w2b, "slotB")
                P.barrier()
        whole()
        P.barrier()
        stats = P.emit()
    return nc, stats


_CACHE = {}


def _host_layout(inputs, depth):
    f = lambda a: np.ascontiguousarray(np.asarray(a, dtype=np.float32))
    cols = np.zeros((128, depth * 5), np.float32)
    for l in range(depth):
        for i, nm in enumerate(("sb_q_norm_g", "sb_k_norm_g", "fox_q_norm_g", "fox_k_norm_g", "hg_norm_g")):
            cols[:, l * 5 + i] = np.tile(f(inputs[nm])[l], 2)
    lb = f(inputs["lb_logits"])
    lbT = np.zeros((128, 2 * depth), np.float32)
    for j in range(2):
        lbT[:, j * depth:(j + 1) * depth] = lb[:, j * 128:(j + 1) * 128].T
    fb = np.ascontiguousarray(f(inputs["fox_f_bias"]).T)
    return cols, lbT, fb


def run(inputs, T, depth, n_cores, dbg=False, stop=99):
    key = (T, depth, dbg, stop)
    if key not in _CACHE:
        _CACHE[key] = build(T, depth, dbg, stop)
    nc, stats = _CACHE[key]
    f = lambda a: np.ascontiguousarray(np.asarray(a, dtype=np.float32))
    cols, lbT, fb = _host_layout(inputs, depth)
    shared = {
        "w_in": f(inputs["w_in"]), "w_out": f(inputs["w_out"]), "w_ff1": f(inputs["w_ff1"]),
        "w_ff2": f(inputs["w_ff2"]), "norm1_g": f(inputs["norm1_g"]), "norm2_g": f(inputs["norm2_g"]),
        "cols": cols, "lbT": lbT, "fb": fb,
    }
    x = f(inputs["x"])
    in_maps = []
    for c in range(n_cores):
        m = dict(shared)
        m["x"] = np.ascontiguousarray(x[c])
        in_maps.append(m)
    res = run_bass_kernel_spmd(nc, in_maps, core_ids=list(range(n_cores)))
    return res


def kernel(**inputs):
    res = run(inputs, 4096, 4, 8)
    return np.stack([np.asarray(r["y"], dtype=np.float32) for r in res.results], axis=0)
```

```python
import contextlib
import os
import numpy as np
import concourse.bass as bass
import concourse.mybir as mybir
from concourse.bass_utils import run_bass_kernel_spmd

F32 = mybir.dt.float32
BF16 = mybir.dt.bfloat16
AF = mybir.ActivationFunctionType
ALU = mybir.AluOpType
AX = mybir.AxisListType

D = 1024
INC = 3334
DFF = 4096
EPS = 1e-6
OFF = dict(hq=0, hf=256, hi=512, hg=768, sq=1024, sk=1408, sv=1792, fq=2176, fk=2560,
           fv=2944, ff=3328)
N_DMA_SEMS = {"hw": 24, "sw": 8}
GROUP_DMA = True


class Op:
    __slots__ = ("eng", "fn", "waits", "signal", "dma_idx", "seq", "count")

    def __init__(self, eng, fn):
        self.eng = eng
        self.fn = fn
        self.waits = []
        self.signal = False
        self.dma_idx = None
        self.seq = None
        self.count = None


class Prog:
    ENGS = ("pe", "act", "dve", "pool", "sp")

    def __init__(self, nc):
        self.nc = nc
        self.ops = {e: [] for e in self.ENGS}
        self.state = {}
        self.waited = {e: {} for e in self.ENGS}
        self.n_dma = {"hw": 0, "sw": 0}

    def _add(self, eng, fn, reads, writes, dma=False):
        op = Op(eng, fn)
        op.seq = len(self.ops[eng])
        if dma:
            pool = "sw" if eng == "pool" else "hw"
            op.dma_idx = (pool, self.n_dma[pool])
            self.n_dma[pool] += 1
            tok = ("d", pool, op.dma_idx[1])
            if op.dma_idx[1] >= N_DMA_SEMS[pool]:
                self._need(op, ("d", pool, op.dma_idx[1] - N_DMA_SEMS[pool]))
        else:
            tok = ("e", eng, op.seq)
        st = self.state
        xb_ = [k for k in reads if isinstance(k, tuple) and k[0] == "bank" and k not in writes]
        if xb_:
            writes = list(writes) + xb_
        for k in reads:
            s = st.get(k)
            if s is not None:
                for w in s[0]:
                    self._need(op, w)
                s[2] = False
        for k in writes:
            s = st.get(k)
            if s is None:
                continue
            if dma and s[2] and GROUP_DMA:
                for w in s[3]:
                    self._need(op, w)
                continue
            for w in s[0]:
                self._need(op, w)
            for rt in s[1].values():
                self._need(op, rt)
        for k in reads:
            s = st.setdefault(k, [[], {}, False, []])
            if dma:
                s[1][tok] = tok
            else:
                s[1][eng] = tok
        for k in writes:
            s = st.get(k)
            if dma and s is not None and s[2] and GROUP_DMA:
                s[0].append(tok)
            else:
                pre = (list(s[0]) + list(s[1].values())) if s is not None else []
                st[k] = [[tok], {}, bool(dma), pre]
        self.ops[eng].append(op)
        return op

    def _need(self, op, tok):
        if tok[0] == "e":
            key, val = tok[1], tok[2]
        else:
            key, val = ("d", tok[1], tok[2] % N_DMA_SEMS[tok[1]]), tok[2]
        wd = self.waited[op.eng]
        if wd.get(key, -1) >= val:
            return
        wd[key] = val
        op.waits.append(tok)
        if tok[0] == "e":
            self.ops[tok[1]][tok[2]].signal = True

    def pe(self, fn, reads=(), writes=()):
        return self._add("pe", fn, reads, writes)

    def act(self, fn, reads=(), writes=()):
        return self._add("act", fn, reads, writes)

    def dve(self, fn, reads=(), writes=()):
        return self._add("dve", fn, reads, writes)

    def pool(self, fn, reads=(), writes=()):
        return self._add("pool", fn, reads, writes)

    def dma(self, q, fn, reads=(), writes=()):
        return self._add(q, fn, reads, writes, dma=True)

    def barrier(self):
        last = {}
        for e in self.ENGS:
            for op in reversed(self.ops[e]):
                if op.dma_idx is None and op.fn is not None:
                    last[e] = ("e", e, op.seq)
                    break
        ndma = dict(self.n_dma)
        for e in self.ENGS:
            op = Op(e, None)
            op.seq = len(self.ops[e])
            for e2, tok in last.items():
                if e2 != e:
                    self._need(op, tok)
            for pool in ("hw", "sw"):
                for i in range(max(0, ndma[pool] - N_DMA_SEMS[pool]), ndma[pool]):
                    self._need(op, ("d", pool, i))
            self.ops[e].append(op)
        self.state = {}

    def emit(self):
        nc = self.nc
        for e in self.ENGS:
            c = 0
            for op in self.ops[e]:
                if op.dma_idx is None and op.signal:
                    c += 1
                op.count = c
        stats = {}
        with contextlib.ExitStack() as es:
            esem = {e: es.enter_context(nc.semaphore("s_" + e)) for e in self.ENGS}
            dsem = {p: [es.enter_context(nc.semaphore("d%s_%d" % (p, i))) for i in range(N_DMA_SEMS[p])]
                    for p in ("hw", "sw")}
            block = es.enter_context(nc.Block())
            engobj = {"pe": block.tensor, "act": block.scalar, "dve": block.vector,
                      "pool": block.gpsimd, "sp": block.sync}
            allops = self.ops
            for e in self.ENGS:
                def body(eng, e=e):
                    nw = 0
                    for op in allops[e]:
                        for tok in op.waits:
                            if tok[0] == "e":
                                eng.wait_ge(esem[tok[1]], allops[tok[1]][tok[2]].count)
                            else:
                                pl, i = tok[1], tok[2]
                                eng.wait_ge(dsem[pl][i % N_DMA_SEMS[pl]], 16 * (i // N_DMA_SEMS[pl] + 1))
                            nw += 1
                        if op.fn is None:
                            continue
                        ins = op.fn(eng)
                        if op.dma_idx is not None:
                            pl, i = op.dma_idx
                            ins.then_inc(dsem[pl][i % N_DMA_SEMS[pl]], 16)
                        elif op.signal:
                            ins.then_inc(esem[e], 1)
                    stats[e] = (len(allops[e]), nw, allops[e][-1].count if allops[e] else 0)
                engobj[e](body)
        return stats


def build(T, depth, dbg=False, stop=99):
    nc = bass.Bass("TRN2", target_bir_lowering=False)
    NT = T // 128
    NG = T // 512
    P = Prog(nc)

    def din(name, shape, dt=F32):
        return nc.dram_tensor(name, list(shape), dt, kind="ExternalInput").ap()

    def dscr(name, shape, dt):
        return nc.dram_tensor(name, list(shape), dt, kind="Internal").ap()

    x_in = din("x", [T, D])
    w_in = din("w_in", [depth, D, INC])
    w_out = din("w_out", [depth, D, D])
    w_ff1 = din("w_ff1", [depth, D, DFF])
    w_ff2 = din("w_ff2", [depth, DFF, D])
    n1g = din("norm1_g", [depth, D])
    n2g = din("norm2_g", [depth, D])
    cols_in = din("cols", [128, depth * 5])
    lbT_in = din("lbT", [128, 2 * depth])
    fb_in = din("fb", [6, depth])
    y = nc.dram_tensor("y", [T, D], F32, kind="ExternalOutput").ap()

    xa = dscr("xa", [T, D], F32)
    xp = dscr("xp", [T, D], F32)
    xb = dscr("xb", [T, D], F32)
    hTs = dscr("hTs", [8, 128, T], BF16)
    qsb = dscr("qsb", [6, 64, T], BF16)
    ksb = dscr("ksb", [6, 64, T], BF16)
    qfx = dscr("qfx", [6, 67, T], BF16)
    kfx = dscr("kfx", [6, 67, T], BF16)
    vsc = dscr("vsc", [T, 12 * 65], BF16)
    mixT = dscr("mixT", [8, 128, T], BF16)
    dbg_out = {}
    if dbg:
        for nm, shp, dt in (("d_mixT", [8, 128, T], BF16), ("d_xa", [T, D], F32),
                            ("d_qfx", [6, 67, T], BF16), ("d_kfx", [6, 67, T], BF16),
                            ("d_qsb", [6, 64, T], BF16), ("d_ksb", [6, 64, T], BF16)):
            dbg_out[nm] = nc.dram_tensor(nm, shp, dt, kind="ExternalOutput").ap()

    es = contextlib.ExitStack()
    with es:
        def sb(name, shape, dt):
            return es.enter_context(nc.sbuf_tensor("sb_" + name, list(shape), dt))

        def ps(name, shape, dt=F32):
            return es.enter_context(nc.psum_tensor("ps_" + name, list(shape), dt))

        slotA = sb("slotA", [128, 32768], BF16)
        slotB = sb("slotB", [128, 32768], BF16)
        ARENA = 63 * 1024
        arena = sb("arena", [128, ARENA], mybir.dt.uint8)
        ident = sb("ident", [128, 128], BF16)
        ident32 = sb("ident32", [128, 128], F32)
        blk = sb("blk", [128, 128], BF16)
        negtri = sb("negtri", [128, 128], BF16)
        negones = sb("negones", [128, 128], BF16)
        msbw = sb("msbw", [128, 896], BF16)
        mfxw = sb("mfxw", [128, 896], BF16)
        hmask = sb("hmask", [128, 4, 128], BF16)
        rmask = sb("rmask", [128, 512], BF16)
        ones32 = sb("ones32", [128, 64], F32)
        onesb = sb("onesb", [128, 512], BF16)
        cols = sb("cols", [128, depth * 5], F32)
        colq = sb("colq", [128, depth * 2], F32)
        lbx = sb("lbx", [128, 2 * depth], F32)
        lbc = sb("lbc", [128, 2 * depth], F32)
        omlb = sb("omlb", [128, 2 * depth], F32)
        lbs = sb("lbs", [128, 2], F32)
        fbc = sb("fbc", [6, depth], F32)
        nfb = sb("nfb", [6, depth], F32)
        cposTok = sb("cposTok", [128, NT, 6], F32)
        epsc = sb("epsc", [128, 1], F32)
        onec = sb("onec", [128, 1], F32)

        banks = [ps("bank%d" % i, [128, 512], F32)[:, :] for i in range(8)]

        class Arena:
            def __init__(self):
                self.off = 0

            def reset(self):
                self.off = 0

            def get(self, shape, dt):
                nb = int(np.prod(shape[1:])) * (4 if dt == F32 else 2)
                nbytes = (nb + 31) // 32 * 32
                assert self.off + nbytes <= ARENA, ("arena overflow", self.off, nbytes)
                ap = arena[0:shape[0], self.off:self.off + nb].bitcast(dt)
                self.off += nbytes
                if len(shape) == 3:
                    ap = ap.rearrange("p (a b) -> p a b", b=shape[2])
                return ap
        AR = Arena()

        def bankbf(i):
            return banks[i][:, :].bitcast(BF16)

        def setup_consts():
            P.pool(lambda e: e.memset(ident[:, :], 1.0), writes=["ident"])
            P.pool(lambda e: e.affine_select(ident[:, :], ident[:, :], [[1, 128]], ALU.is_equal, 0.0,
                                             base=0, channel_multiplier=-1),
                   reads=["ident"], writes=["ident"])
            P.pool(lambda e: e.memset(ident32[:, :], 1.0), writes=["ident32"])
            P.pool(lambda e: e.affine_select(ident32[:, :], ident32[:, :], [[1, 128]], ALU.is_equal,
                                             0.0, base=0, channel_multiplier=-1),
                   reads=["ident32"], writes=["ident32"])
            P.pool(lambda e: e.memset(blk[:, :], 0.0), writes=["blk"])
            P.pool(lambda e: e.memset(blk[0:64, 0:64], 1.0 / 64), reads=["blk"], writes=["blk"])
            P.pool(lambda e: e.memset(blk[64:128, 64:128], 1.0 / 64), reads=["blk"], writes=["blk"])
            P.pool(lambda e: e.memset(negtri[:, :], -1.0), writes=["negtri"])
            P.pool(lambda e: e.affine_select(negtri[:, :], negtri[:, :], [[-1, 128]], ALU.is_ge, 0.0,
                                             base=0, channel_multiplier=1),
                   reads=["negtri"], writes=["negtri"])
            P.pool(lambda e: e.memset(negones[:, :], -1.0), writes=["negones"])
            P.pool(lambda e: e.memset(onesb[:, :], 1.0), writes=["onesb"])
            P.pool(lambda e: e.memset(ones32[:, :], 1.0), writes=["ones32"])
            P.pool(lambda e: e.memset(epsc[:, :], EPS), writes=["epsc"])
            P.pool(lambda e: e.memset(onec[:, :], 1.0), writes=["onec"])
            P.pool(lambda e: e.memset(msbw[:, :], 1.0), writes=["msb"])
            P.pool(lambda e: e.affine_select(msbw[:, :], msbw[:, :], [[1, 896]], ALU.is_gt,
                                             0.0, base=-384, channel_multiplier=-1),
                   reads=["msb"], writes=["msb"])
            P.pool(lambda e: e.memset(mfxw[:, :], 1.0), writes=["mfx"])
            P.pool(lambda e: e.affine_select(mfxw[:, :], mfxw[:, :], [[1, 896]], ALU.is_ge,
                                             0.0, base=-384, channel_multiplier=-1),
                   reads=["mfx"], writes=["mfx"])
            P.pool(lambda e: e.memset(hmask[:, :, :], 0.0), writes=["hmask"])
            for c in range(2):
                P.pool(lambda e, c=c: e.memset(hmask[c * 64:(c + 1) * 64, :, c * 64:(c + 1) * 64], 1.0),
                       reads=["hmask"], writes=["hmask"])
            P.pool(lambda e: e.affine_select(hmask[:, :, :], hmask[:, :, :], [[0, 4], [1, 128]],
                                             ALU.is_ge, 0.0, base=0, channel_multiplier=-1),
                   reads=["hmask"], writes=["hmask"])
            P.pool(lambda e: e.memset(rmask[:, :], 1.0), writes=["rmask"])
            P.pool(lambda e: e.memset(rmask[:, :].rearrange("p (c t) -> p c t", t=64)[:, :, 0:1], 0.0),
                   reads=["rmask"], writes=["rmask"])
            P.dma("sp", lambda e: e.dma_start(out=cols[:, :], in_=cols_in[:, :]), writes=["cols"])
            P.dma("sp", lambda e: e.dma_start(out=lbx[:, :], in_=lbT_in[:, :]), writes=["lbx"])
            P.dma("sp", lambda e: e.dma_start(out=fbc[:, :], in_=fb_in[:, :]), writes=["fbc"])
            P.dve(lambda e: e.tensor_scalar(nfb[:, :], fbc[:, :], -1.0, None, ALU.mult),
                  reads=["fbc"], writes=["nfb"])
            for l in range(depth):
                for i, c in enumerate((0, 2)):
                    P.dve(lambda e, l=l, i=i, c=c: e.tensor_scalar(
                        colq[:, l * 2 + i:l * 2 + i + 1], cols[:, l * 5 + c:l * 5 + c + 1],
                        0.125, None, ALU.mult), reads=["cols"], writes=["colq"])
            P.act(lambda e: e.activation(lbx[:, :], lbx[:, :], AF.Exp), reads=["lbx"], writes=["lbx"])
            lb3 = lbx[:, :].rearrange("p (j l) -> p j l", l=depth)
            P.dve(lambda e: e.tensor_reduce(lbs[:, :], lb3, AX.X, ALU.add), reads=["lbx"], writes=["lbs"])
            P.dve(lambda e: e.reciprocal(lbs[:, :], lbs[:, :]), reads=["lbs"], writes=["lbs"])
            for j in range(2):
                P.dve(lambda e, j=j: e.tensor_scalar(lbx[:, j * depth:(j + 1) * depth],
                                                     lbx[:, j * depth:(j + 1) * depth],
                                                     lbs[:, j:j + 1], None, ALU.mult),
                      reads=["lbx", "lbs"], writes=["lbx"])
                P.dve(lambda e, j=j: e.memset(lbc[:, j * depth:j * depth + 1], 0.0),
                      reads=["lbc"], writes=["lbc"])
                for l in range(1, depth):
                    P.dve(lambda e, j=j, l=l: e.tensor_tensor(
                        lbc[:, j * depth + l:j * depth + l + 1], lbc[:, j * depth + l - 1:j * depth + l],
                        lbx[:, j * depth + l:j * depth + l + 1], ALU.add),
                        reads=["lbc", "lbx"], writes=["lbc"])
            P.dve(lambda e: e.tensor_scalar(omlb[:, :], lbc[:, :], -1.0, 1.0, ALU.mult, ALU.add),
                  reads=["lbc"], writes=["omlb"])
            for h in range(6):
                for t0 in range(0, T, 512):
                    P.dma("sp", lambda e, h=h, t0=t0: e.dma_start(out=kfx[h, 64:67, t0:t0 + 512],
                                                                   in_=onesb[0:3, :]),
                          reads=["onesb"], writes=[("kfx1", h)])

        def load_w_in(l):
            v = slotA[:, 0:8 * INC].rearrange("p (k n) -> p k n", n=INC)
            for k in range(8):
                for hh in range(2):
                    c0 = hh * 1667
                    P.dma("pool", lambda e, k=k, c0=c0: e.dma_start(
                        out=v[:, k, c0:c0 + 1667], in_=w_in[l, k * 128:(k + 1) * 128, c0:c0 + 1667]),
                        writes=["slotA"])
            return v

        def load_w_out(l):
            v = slotB[:, 0:8 * D].rearrange("p (k n) -> p k n", n=D)
            for k in range(8):
                P.dma("pool", lambda e, k=k: e.dma_start(
                    out=v[:, k, :], in_=w_out[l, k * 128:(k + 1) * 128, :]), writes=["slotB"])
            return v

        def load_ffn_half(l, hf, slot, key):
            w1 = slot[:, 0:16384].rearrange("p (k n) -> p k n", n=2048)
            w2 = slot[:, 16384:32768].rearrange("p (k n) -> p k n", n=1024)
            for k in range(8):
                for hh in range(2):
                    c0 = hf * 2048 + hh * 1024
                    P.dma("pool", lambda e, k=k, c0=c0, hh=hh: e.dma_start(
                        out=w1[:, k, hh * 1024:(hh + 1) * 1024],
                        in_=w_ff1[l, k * 128:(k + 1) * 128, c0:c0 + 1024]), writes=[key])
            for fc in range(16):
                r0 = hf * 2048 + fc * 128
                P.dma("pool", lambda e, fc=fc, r0=r0: e.dma_start(
                    out=w2[:, fc, :], in_=w_ff2[l, r0:r0 + 128, :]), writes=[key])
            return w1, w2

        def mm_group(out, pairs, reads, writes):
            def fn(e):
                n = len(pairs)
                ins = None
                for i, (lt, r) in enumerate(pairs):
                    ins = e.matmul(out, lt, r, start=(i == 0), stop=(i == n - 1))
                return ins
            P.pe(fn, reads, writes)

        def norm_tile(xt, xkey, gbc, hb, hkey, st, stkey, sq_scr, sqkey):
            P.act(lambda e: e.activation(sq_scr, xt, AF.Square, accum_out=st[:, 0:1]),
                  reads=[xkey], writes=[sqkey, stkey])
            P.act(lambda e: e.activation(st[:, 1:2], st[:, 0:1], AF.Ln, bias=epsc[:, 0:1], scale=1.0 / D),
                  reads=[stkey], writes=[stkey])
            P.act(lambda e: e.activation(st[:, 2:3], st[:, 1:2], AF.Exp, scale=-0.5),
                  reads=[stkey], writes=[stkey])
            P.dve(lambda e: e.scalar_tensor_tensor(hb, xt, st[:, 2:3], gbc, ALU.mult, ALU.mult),
                  reads=[xkey, stkey, "gbc"], writes=[hkey])

        def transpose_h(hb, hkey, hT, hTkey, tcol, pbank, pkey):
            for half in range(2):
                pv = bankbf(pbank)

                def fn(e, half=half, pv=pv):
                    ins = None
                    for kk in range(4):
                        k = half * 4 + kk
                        ins = e.transpose(pv[:, kk * 128:(kk + 1) * 128], hb[:, k * 128:(k + 1) * 128],
                                          ident[:, :])
                    return ins
                P.pe(fn, reads=[hkey, "ident"], writes=[pkey])
                src = pv[:, 0:512].rearrange("p (k t) -> p k t", t=128)
                dst = hT[:, half * 4:(half + 1) * 4, tcol:tcol + 128]
                if half == 0:
                    P.act(lambda e, src=src, dst=dst: e.activation(dst, src, AF.Copy),
                          reads=[pkey], writes=[hTkey])
                else:
                    P.dve(lambda e, src=src, dst=dst: e.tensor_copy(dst, src),
                          reads=[pkey], writes=[hTkey])

        def headnorm(psrc, pskey, gcol, gkey, bank_ms, mskey, sq, sqkey, rs, rskey, extra_mul, emkey,
                     dst, dstkey):
            P.act(lambda e: e.activation(sq, psrc, AF.Square), reads=[pskey], writes=[sqkey])
            mm_group(bank_ms, [(blk[:, :], sq)], reads=[sqkey, "blk"], writes=[mskey])
            P.act(lambda e: e.activation(rs, bank_ms, AF.Ln, bias=epsc[:, 0:1]),
                  reads=[mskey], writes=[rskey])
            P.act(lambda e: e.activation(rs, rs, AF.Exp, scale=-0.5), reads=[rskey], writes=[rskey])
            if extra_mul is None:
                P.dve(lambda e: e.scalar_tensor_tensor(dst, psrc, gcol, rs, ALU.mult, ALU.mult),
                      reads=[pskey, rskey, gkey], writes=[dstkey])
            else:
                P.dve(lambda e: e.scalar_tensor_tensor(rs, psrc, gcol, rs, ALU.mult, ALU.mult),
                      reads=[pskey, rskey, gkey], writes=[rskey])
                P.pool(lambda e: e.tensor_tensor(dst, rs, extra_mul, ALU.mult),
                       reads=[rskey, emkey], writes=[dstkey])

        def phase1a(l, xsrc, wv):
            AR.reset()
            gbc = AR.get([128, D], F32)
            xt = [AR.get([128, D], F32) for _ in range(2)]
            sqs = AR.get([128, D], BF16)
            hb = [AR.get([128, D], BF16) for _ in range(2)]
            st = [AR.get([128, 4], F32) for _ in range(2)]
            hT = [AR.get([128, 8, 512], BF16) for _ in range(2)]
            sq = [AR.get([128, 512], BF16) for _ in range(2)]
            rs = [AR.get([128, 512], F32) for _ in range(2)]
            qn = [AR.get([128, 512], BF16) for _ in range(2)]
            vaug = [AR.get([128, 12, 65], BF16) for _ in range(2)]
            spt = AR.get([6, 512], F32)
            cpos = [AR.get([6, 512], F32) for _ in range(2)]
            rr = AR.get([6, 512], F32)
            qh = [AR.get([6, 512], BF16) for _ in range(3)]
            P.dma("sp", lambda e: e.dma_start(out=gbc, in_=n1g[l, :].partition_broadcast(128)),
                  writes=["gbc"])
            for i in range(2):
                P.pool(lambda e, i=i: e.memset(vaug[i][:, :, 64:65], 1.0), writes=[("vaug", i)])
            itc = [0]

            def prep_tile(g, ti):
                t0 = g * 512 + ti * 128
                b = itc[0] % 2
                itc[0] += 1
                P.dma("sp", lambda e, b=b, t0=t0: e.dma_start(out=xt[b], in_=xsrc[t0:t0 + 128, :]),
                      writes=[("xt", b)])
                norm_tile(xt[b], ("xt", b), gbc, hb[b], ("hb", b), st[b], ("st", b), sqs, "sqs")
                transpose_h(hb[b], ("hb", b), hT[g % 2], ("hT", g % 2), ti * 128, 7, ("bank", 7))

            for ti in range(4):
                prep_tile(0, ti)
            for g in range(NG):
                hb_ = g % 2
                hTg = hT[hb_]
                hk = ("hT", hb_)
                for k in range(8):
                    P.dma("act", lambda e, k=k, g=g, hTg=hTg: e.dma_start(
                        out=hTs[k, :, g * 512:(g + 1) * 512], in_=hTg[:, k, :]),
                        reads=[hk], writes=[("hTs", g)])
                ci = 0
                for typ, qk, off, dst_d, gci in (("sb", "q", OFF["sq"], qsb, None),
                                                 ("sb", "k", OFF["sk"], ksb, 1),
                                                 ("fx", "q", OFF["fq"], qfx, None),
                                                 ("fx", "k", OFF["fk"], kfx, 3)):
                    for c in range(3):
                        pb = ci % 2
                        ci += 1
                        bank = banks[pb]
                        c0 = off + c * 128
                        mm_group(bank, [(wv[:, k, c0:c0 + 128], hTg[:, k, :]) for k in range(8)],
                                 reads=[hk, "slotA"], writes=[("bank", pb)])
                        if qk == "q":
                            gcol = colq[:, l * 2 + (0 if typ == "sb" else 1):l * 2 + (0 if typ == "sb" else 1) + 1]
                            gkey = "colq"
                        else:
                            gcol = cols[:, l * 5 + gci:l * 5 + gci + 1]
                            gkey = "cols"
                        headnorm(bank, ("bank", pb), gcol, gkey, banks[2 + pb], ("bank", 2 + pb),
                                 sq[pb], ("sq", pb), rs[pb], ("rs", pb), None, None, qn[pb], ("qn", pb))
                        for hp in range(2):
                            P.dma("sp", lambda e, pb=pb, hp=hp, c=c, g=g, dst_d=dst_d: e.dma_start(
                                out=dst_d[2 * c + hp, 0:64, g * 512:(g + 1) * 512],
                                in_=qn[pb][hp * 64:(hp + 1) * 64, :]),
                                reads=[("qn", pb)], writes=[("qkd", typ, qk, 2 * c + hp, g)])
                        if ci % 3 == 0 and g + 1 < NG:
                            prep_tile(g + 1, ci // 3 - 1)
                for ti in range(4):
                    vb = (g * 4 + ti) % 2
                    for vi, off in enumerate((OFF["sv"], OFF["fv"])):
                        bank = banks[4 + vi]
                        mm_group(bank[:, 0:384],
                                 [(hTg[:, k, ti * 128:(ti + 1) * 128], wv[:, k, off:off + 384])
                                  for k in range(8)],
                                 reads=[hk, "slotA"], writes=[("bank", 4 + vi)])
                        src = bank[:, 0:384].rearrange("p (h d) -> p h d", d=64)
                        dst = vaug[vb][:, vi * 6:(vi + 1) * 6, 0:64]
                        if vi == 0:
                            P.act(lambda e, src=src, dst=dst: e.activation(dst, src, AF.Copy),
                                  reads=[("bank", 4 + vi)], writes=[("vaug", vb)])
                        else:
                            P.dve(lambda e, src=src, dst=dst: e.tensor_copy(dst, src),
                                  reads=[("bank", 4 + vi)], writes=[("vaug", vb)])
                    t0 = g * 512 + ti * 128
                    P.dma("sp", lambda e, vb=vb, t0=t0: e.dma_start(
                        out=vsc[t0:t0 + 128, :].rearrange("p (h d) -> p h d", d=65), in_=vaug[vb]),
                        reads=[("vaug", vb)], writes=[("vsc", t0)])
                if os.environ.get("SKIPC") == "1":
                    continue
                mm_group(banks[6][0:6, :], [(wv[:, k, OFF["ff"]:OFF["ff"] + 6], hTg[:, k, :]) for k in range(8)],
                         reads=[hk, "slotA"], writes=[("bank", 6)])
                P.act(lambda e: e.activation(spt, banks[6][0:6, :], AF.Exp, bias=nfb[:, l:l + 1], scale=-1.0),
                      reads=[("bank", 6), "nfb"], writes=["spt"])
                P.act(lambda e: e.activation(spt, spt, AF.Ln, bias=onec[0:6, 0:1]), reads=["spt"], writes=["spt"])
                cb = g % 2
                if g == 0:
                    P.dve(lambda e, cb=cb: e.tensor_tensor_scan(cpos[cb], onesb[0:6, :], spt, 0.0,
                                                                ALU.mult, ALU.add),
                          reads=["spt", "onesb"], writes=[("cpos", cb)])
                else:
                    P.dve(lambda e, cb=cb: e.tensor_tensor_scan(cpos[cb], onesb[0:6, :], spt,
                                                                cpos[1 - cb][:, 511:512], ALU.mult, ALU.add),
                          reads=["spt", "onesb", ("cpos", 1 - cb)], writes=[("cpos", cb)])
                P.dve(lambda e, cb=cb: e.tensor_scalar(qh[0], cpos[cb], -1.0, None, ALU.mult),
                      reads=[("cpos", cb)], writes=["qh0"])
                P.dve(lambda e, cb=cb: e.scalar_tensor_tensor(rr, cpos[cb], -1.0, qh[0], ALU.mult, ALU.subtract),
                      reads=[("cpos", cb), "qh0"], writes=["rr"])
                P.dve(lambda e: e.tensor_copy(qh[1], rr), reads=["rr"], writes=["qh1"])
                P.dve(lambda e: e.tensor_tensor(qh[2], rr, qh[1], ALU.subtract), reads=["rr", "qh1"],
                      writes=["qh2"])
                for i in range(3):
                    P.dma("sp", lambda e, i=i, g=g: e.dma_start(out=qfx[:, 64 + i, g * 512:(g + 1) * 512],
                                                                 in_=qh[i]),
                          reads=["qh%d" % i], writes=[("qfxc", i, g)])
                if os.environ.get("SKIPC3") == "1":
                    continue
                for ti in range(4):
                    P.pe(lambda e, ti=ti, cb=cb: e.matmul(banks[6][:, 0:6], cpos[cb][:, ti * 128:(ti + 1) * 128],
                                                         ident32[0:6, 0:6], start=True, stop=True),
                         reads=[("cpos", cb), "ident32"], writes=[("bank", 6)])
                    P.dve(lambda e, ti=ti, g=g: e.tensor_copy(cposTok[:, g * 4 + ti, :], banks[6][:, 0:6]),
                          reads=[("bank", 6)], writes=["cposTok"])

        def phase1b(l, wv):
            AR.reset()
            hT = [AR.get([128, 8, 512], BF16) for _ in range(2)]
            t1 = [AR.get([128, 512], F32) for _ in range(2)]
            t2 = AR.get([128, 512], F32)
            t3 = AR.get([128, 512], F32)
            t4 = [AR.get([128, 512], F32) for _ in range(2)]
            t5 = AR.get([128, 512], F32)
            gs = [AR.get([128, 512], F32) for _ in range(2)]
            qT = AR.get([128, 2, 512], BF16)
            kTz = [[AR.get([128, 512], BF16) for _ in range(2)] for _ in range(2)]
            khT = AR.get([128, 2, 512], BF16)
            khat = AR.get([128, 4, 256], BF16)
            vt = AR.get([128, 4, 256], BF16)
            vtz = [AR.get([128, 4, 256], BF16) for _ in range(2)]
            atb = [AR.get([128, 4, 128], BF16) for _ in range(2)]
            S32 = [AR.get([128, 2, 64], F32) for _ in range(2)]
            Sbz = [[AR.get([128, 2, 64], BF16) for _ in range(2)] for _ in range(4)]
            sq = AR.get([128, 512], BF16)
            rs = AR.get([128, 512], F32)
            mo = [AR.get([128, 512], BF16) for _ in range(2)]
            for j in range(2):
                for hp in range(2):
                    P.pool(lambda e, j=j, hp=hp: e.memset(kTz[j][hp], 0.0), writes=[("kTz", j)])
            for c in range(2):
                P.pool(lambda e, c=c: e.memset(vtz[c], 0.0), writes=["vtz"])
            for r in range(4):
                for hp in range(2):
                    P.pool(lambda e, r=r, hp=hp: e.memset(Sbz[r][hp], 0.0), writes=[("Sbz", r)])
            P.pool(lambda e: e.memset(S32[0], 0.0), writes=[("S32", 0)])
            sidx = 0
            ring = 0
            for g in range(NG):
                hb_ = g % 2
                hTg = hT[hb_]
                hk = ("hT", hb_)
                for k in range(8):
                    P.dma("sp" if k % 2 == 0 else "act", lambda e, k=k, g=g, hTg=hTg: e.dma_start(
                        out=hTg[:, k, :], in_=hTs[k, :, g * 512:(g + 1) * 512]), writes=[hk])
                for j in range(2):
                    bf_, bq_, bg_ = banks[0], banks[1], banks[2]
                    mm_group(bf_, [(wv[:, k, OFF["hf"] + j * 128:OFF["hf"] + (j + 1) * 128], hTg[:, k, :])
                                   for k in range(8)], reads=[hk, "slotA"], writes=[("bank", 0)])
                    mm_group(bq_, [(wv[:, k, OFF["hq"] + j * 128:OFF["hq"] + (j + 1) * 128], hTg[:, k, :])
                                   for k in range(8)], reads=[hk, "slotA"], writes=[("bank", 1)])
                    mm_group(bg_, [(wv[:, k, OFF["hg"] + j * 128:OFF["hg"] + (j + 1) * 128], hTg[:, k, :])
                                   for k in range(8)], reads=[hk, "slotA"], writes=[("bank", 2)])
                    a = t1[j]
                    ak = ("t1", j)
                    P.act(lambda e, a=a: e.activation(a, bf_, AF.Exp, scale=-1.0), reads=[("bank", 0)], writes=[ak])
                    P.dve(lambda e, a=a: e.tensor_scalar(a, a, 1.0, None, ALU.add), reads=[ak], writes=[ak])
                    P.dve(lambda e, a=a: e.reciprocal(a, a), reads=[ak], writes=[ak])
                    ci = j * depth + l
                    P.dve(lambda e, a=a, ci=ci: e.tensor_scalar(a, a, omlb[:, ci:ci + 1], lbc[:, ci:ci + 1],
                                                                ALU.mult, ALU.add),
                          reads=[ak, "omlb", "lbc"], writes=[ak])
                    P.act(lambda e, a=a: e.activation(t2, a, AF.Ln), reads=[ak], writes=["t2"])
                    P.dve(lambda e: e.tensor_tensor_scan(t3, rmask[:, :], t2, 0.0, ALU.mult, ALU.add),
                          reads=["t2", "rmask"], writes=["t3"])
                    e4 = t4[j]
                    P.act(lambda e, e4=e4: e.activation(e4, t3, AF.Exp), reads=["t3"], writes=[("t4", j)])
                    P.act(lambda e: e.activation(t5, t3, AF.Exp, scale=-1.0), reads=["t3"], writes=["t5"])
                    P.dve(lambda e, j=j, e4=e4: e.tensor_tensor(qT[:, j, :], bq_, e4, ALU.mult),
                          reads=[("bank", 1), ("t4", j)], writes=[("qT", j)])
                    P.pool(lambda e, a=a: e.tensor_scalar(a, a, -1.0, 1.0, ALU.mult, ALU.add),
                           reads=[ak], writes=[ak])
                    P.pool(lambda e, a=a: e.tensor_tensor(t5, a, t5, ALU.mult), reads=[ak, "t5"], writes=["t5"])
                    for hp in range(2):
                        sl = slice(hp * 64, (hp + 1) * 64)
                        P.pool(lambda e, j=j, hp=hp, sl=sl: e.tensor_copy(kTz[j][hp][sl, :], t5[sl, :]),
                               reads=["t5"], writes=[("kTz", j)])
                    e4v = e4.rearrange("p (c t) -> p c t", t=64)[:, :, 63:64].to_broadcast([128, 8, 64])
                    P.dve(lambda e, j=j, e4v=e4v: e.tensor_tensor(
                        khT[:, j, :].rearrange("p (c t) -> p c t", t=64),
                        t5[:, :].rearrange("p (c t) -> p c t", t=64), e4v, ALU.mult),
                        reads=["t5", ("t4", j)], writes=[("khT", j)])
                    gg = gs[j]
                    gk = ("gs", j)
                    P.act(lambda e, gg=gg: e.activation(gg, bg_, AF.Exp, scale=-1.0), reads=[("bank", 2)], writes=[gk])
                    P.dve(lambda e, gg=gg: e.tensor_scalar(gg, gg, 1.0, None, ALU.add), reads=[gk], writes=[gk])
                    P.dve(lambda e, gg=gg: e.reciprocal(gg, gg), reads=[gk], writes=[gk])
                    P.dve(lambda e, gg=gg: e.tensor_tensor(gg, bg_, gg, ALU.mult), reads=[("bank", 2), gk], writes=[gk])
                B1 = int(os.environ.get("B1", "9"))
                if B1 <= 1:
                    continue
                for ti in range(4):
                    tsl = slice(ti * 128, (ti + 1) * 128)
                    mm_group(banks[3][:, 0:256],
                             [(hTg[:, k, tsl], wv[:, k, OFF["hi"]:OFF["hi"] + 256]) for k in range(8)],
                             reads=[hk, "slotA"], writes=[("bank", 3)])
                    P.act(lambda e, ti=ti: e.activation(vt[:, ti, :], banks[3][:, 0:256], AF.Copy),
                          reads=[("bank", 3)], writes=["vt"])
                    for c in range(2):
                        sl = slice(c * 64, (c + 1) * 64)
                        P.dve(lambda e, ti=ti, c=c, sl=sl: e.tensor_copy(vtz[c][sl, ti, :], banks[3][sl, 0:256]),
                              reads=[("bank", 3)], writes=["vtz"])
                    pv = bankbf(3)

                    def fnT(e, tsl=tsl, pv=pv):
                        ins = None
                        for j in range(2):
                            ins = e.transpose(pv[:, 512 + j * 128:512 + (j + 1) * 128], khT[:, j, tsl], ident[:, :])
                        return ins
                    P.pe(fnT, reads=[("khT", 0), ("khT", 1), "ident"], writes=[("bank", 3)])
                    P.act(lambda e, ti=ti, pv=pv: e.activation(khat[:, ti, :], pv[:, 512:768], AF.Copy),
                          reads=[("bank", 3)], writes=["khat"])
                for ti in range(4):
                    if B1 <= 2:
                        continue
                    tsl = slice(ti * 128, (ti + 1) * 128)
                    ab = ti % 2
                    def fnA(e, tsl=tsl):
                        ins = None
                        for h in range(4):
                            j, hp = h // 2, h % 2
                            ins = e.matmul(banks[4][:, h * 128:(h + 1) * 128], kTz[j][hp][:, tsl], qT[:, j, tsl],
                                           start=True, stop=True)
                        return ins
                    P.pe(fnA, reads=[("kTz", 0), ("kTz", 1), ("qT", 0), ("qT", 1)], writes=[("bank", 4)])
                    P.dve(lambda e, ab=ab: e.tensor_tensor(atb[ab], banks[4].rearrange("p (h t) -> p h t", t=128),
                                                           hmask[:, :, :], ALU.mult),
                          reads=[("bank", 4), "hmask"], writes=[("atb", ab)])
                    def fnU(e, ti=ti):
                        ins = None
                        for c in range(2):
                            for h in range(4):
                                j = h // 2
                                ins = e.matmul(banks[5][:, (c * 4 + h) * 64:(c * 4 + h + 1) * 64],
                                               khat[:, ti, j * 128:(j + 1) * 128],
                                               vtz[c][:, ti, h * 64:(h + 1) * 64], start=True, stop=True)
                        return ins
                    P.pe(fnU, reads=["khat", "vtz"], writes=[("bank", 5)])
                    rings = [ring]
                    for c in range(2):
                        chunkcol = (ti * 2 + c) * 64 + 63
                        so, sn = S32[sidx], S32[1 - sidx]
                        for h in range(4):
                            j, hp = h // 2, h % 2
                            sl = slice(hp * 64, (hp + 1) * 64)
                            P.dve(lambda e, so=so, sn=sn, j=j, sl=sl, c=c, h=h, chunkcol=chunkcol:
                                  e.scalar_tensor_tensor(sn[sl, j, :], so[sl, j, :],
                                                         t4[j][sl, chunkcol:chunkcol + 1],
                                                         banks[5][sl, (c * 4 + h) * 64:(c * 4 + h + 1) * 64],
                                                         ALU.mult, ALU.add),
                                  reads=[("S32", sidx), ("t4", j), ("bank", 5)], writes=[("S32", 1 - sidx)])
                        sidx = 1 - sidx
                        ring = (ring + 1) % 4
                        rings.append(ring)
                        for hp in range(2):
                            sl = slice(hp * 64, (hp + 1) * 64)
                            P.act(lambda e, sl=sl, hp=hp, ring=ring, sidx=sidx:
                                  e.activation(Sbz[ring][hp][sl, :, :], S32[sidx][sl, :, :], AF.Copy),
                                  reads=[("S32", sidx)], writes=[("Sbz", ring)])
                    if B1 <= 3:
                        continue
                    def fnO(e, ti=ti, tsl=tsl, ab=ab, rings=tuple(rings)):
                        ins = None
                        for h in range(4):
                            j, hp = h // 2, h % 2
                            osl = slice(hp * 64, (hp + 1) * 64)
                            ob = banks[6 + j]
                            e.matmul(ob[osl, tsl], vt[:, ti, h * 64:(h + 1) * 64], atb[ab][:, h, :],
                                     start=True, stop=False, skip_group_check=True)
                            for c in range(2):
                                csl = slice(ti * 128 + c * 64, ti * 128 + (c + 1) * 64)
                                ins = e.matmul(ob[osl, csl], Sbz[rings[c]][hp][:, j, :], qT[:, j, csl],
                                               start=False, stop=(c == 1), skip_group_check=True)
                        return ins
                    P.pe(fnO, reads=["vt", ("atb", ab), ("Sbz", rings[0]), ("Sbz", rings[1]), ("qT", 0), ("qT", 1)],
                         writes=[("bank", 6), ("bank", 7)])
                for j in range(2):
                    if B1 <= 4:
                        continue
                    gcol = cols[:, l * 5 + 4:l * 5 + 5]
                    headnorm(banks[6 + j], ("bank", 6 + j), gcol, "cols", banks[j], ("bank", j),
                             sq, "sq", rs, "rs", gs[j], ("gs", j), mo[j], ("mo", j))
                    P.dma("sp", lambda e, j=j, g=g: e.dma_start(out=mixT[j, :, g * 512:(g + 1) * 512], in_=mo[j]),
                          reads=[("mo", j)], writes=[("mixT", j, g)])

        def phase2(l):
            AR.reset()
            kT = [AR.get([67, T], BF16) for _ in range(2)]
            qTt = [AR.get([67, T], BF16) for _ in range(2)]
            vv = [AR.get([128, NT, 65], BF16) for _ in range(2)]
            ee = [AR.get([128, 512], F32) for _ in range(2)]
            spb = [AR.get([128, 512], BF16) for _ in range(3)]
            ssum = AR.get([128, 512], F32)
            ssb = [AR.get([128, 512], BF16) for _ in range(3)]
            aa = [AR.get([128, 512], BF16) for _ in range(4)]
            osb = AR.get([65, 512], F32)
            rec = AR.get([65, 512], F32)
            on = [AR.get([64, 512], BF16) for _ in range(2)]

            for b_ in range(2):
                P.pool(lambda e, b_=b_: e.memset(kT[b_][64:67, :], 0.0), writes=[("kT", b_)])
                P.pool(lambda e, b_=b_: e.memset(qTt[b_][64:67, :], 0.0), writes=[("qTt", b_)])

            def load_head(hh):
                typ = "sb" if hh < 6 else "fx"
                h = hh % 6
                hb_ = hh % 2
                nr = 64 if typ == "sb" else 67
                qd, kd = (qsb, ksb) if typ == "sb" else (qfx, kfx)
                P.dma("sp", lambda e: e.dma_start(out=kT[hb_][0:nr, :], in_=kd[h, 0:nr, :]), writes=[("kT", hb_)])
                P.dma("act", lambda e: e.dma_start(out=qTt[hb_][0:nr, :], in_=qd[h, 0:nr, :]), writes=[("qTt", hb_)])
                P.dma("sp", lambda e: e.dma_start(
                    out=vv[hb_], in_=vsc[:, hh * 65:(hh + 1) * 65].rearrange("(n p) d -> p n d", p=128)),
                    writes=[("vv", hb_)])

            items = []
            gi = 0
            for hh in range(12):
                typ = "sb" if hh < 6 else "fx"
                for B in range(NG):
                    na = 4 * B + 4
                    order = list(range(na - 1, -1, -1)) if typ == "sb" else list(range(na))
                    for idx, a in enumerate(order):
                        items.append(dict(hh=hh, typ=typ, h=hh % 6, hb=hh % 2, nr=67,
                                          B=B, a=a, r=a - 4 * B, first=(idx == 0), last=(idx == na - 1),
                                          ob=gi % 2, newhead=(B == 0 and idx == 0)))
                    gi += 1
            n = len(items)
            load_head(0)

            def stageA(t, it):
                hb_, nr, a, r = it["hb"], it["nr"], it["a"], it["r"]
                ks = slice(a * 128, (a + 1) * 128)
                qs = slice(it["B"] * 512, (it["B"] + 1) * 512)
                zi = t % 3
                zb, zk = banks[zi], ("bank", zi)
                sbt = (it["typ"] == "sb")
                P.pe(lambda e: e.matmul(zb, kT[hb_][0:nr, ks], qTt[hb_][0:nr, qs], start=True, stop=(not sbt),
                                        skip_group_check=True),
                     reads=[("kT", hb_), ("qTt", hb_)], writes=[zk])
                if it["typ"] == "fx":
                    ai = t % 4
                    h = it["h"]
                    P.act(lambda e: e.activation(aa[ai], zb, AF.Exp, bias=cposTok[:, a, h:h + 1]),
                          reads=[zk, "cposTok"], writes=[("aa", ai)])
                    if r >= 0:
                        P.pool(lambda e: e.tensor_tensor(aa[ai], aa[ai], mfxw[:, (3 - r) * 128:(3 - r) * 128 + 512], ALU.mult),
                               reads=[("aa", ai), "mfx"], writes=[("aa", ai)])
                else:
                    ei, si = t % 2, t % 3
                    P.act(lambda e: e.activation(ee[ei], zb, AF.Exp), reads=[zk], writes=[("ee", ei)])
                    it["_ln"] = (ei, si)

            def stageA2(t, it):
                if it["typ"] != "sb":
                    return
                ei, si = it["_ln"]
                r, a = it["r"], it["a"]
                P.act(lambda e: e.activation(spb[si], ee[ei], AF.Ln, bias=onec[:, 0:1]), reads=[("ee", ei)],
                      writes=[("spb", si)])
                if r >= 0:
                    P.pool(lambda e: e.tensor_tensor(spb[si], spb[si], msbw[:, (3 - r) * 128:(3 - r) * 128 + 512], ALU.mult),
                           reads=[("spb", si), "msb"], writes=[("spb", si)])
                if a > 0:
                    nsi = (t + 1) % 3
                    if it["first"]:
                        P.dve(lambda e: e.tensor_copy(ssb[nsi], spb[si]), reads=[("spb", si)], writes=[("ssb", nsi)])
                        P.dve(lambda e: e.tensor_copy(ssum, spb[si]), reads=[("spb", si)], writes=["ssum"])
                    else:
                        P.dve(lambda e: e.tensor_tensor(ssb[nsi], ssum, spb[si], ALU.add),
                              reads=["ssum", ("spb", si)], writes=[("ssb", nsi)])
                        P.dve(lambda e: e.tensor_tensor(ssum, ssum, spb[si], ALU.add),
                              reads=["ssum", ("spb", si)], writes=["ssum"])

            def stageB_pe(t, it):
                if it["typ"] != "sb":
                    return
                si = t % 3
                li = t % 3
                lb_, lk = banks[li], ("bank", li)
                prs = [(negtri[:, :], spb[si])]
                rd = [("spb", si), "negtri"]
                if not it["first"]:
                    prs.append((negones[:, :], ssb[si]))
                    rd += [("ssb", si), "negones"]

                def fnB(e, prs=prs, lb_=lb_):
                    ins = None
                    for i_, (lt, r_) in enumerate(prs):
                        ins = e.matmul(lb_, lt, r_, start=False, stop=(i_ == len(prs) - 1), skip_group_check=True)
                    return ins
                P.pe(fnB, reads=rd + [lk], writes=[lk])

            def stageB_act(t, it):
                if it["typ"] != "sb":
                    return
                r = it["r"]
                ai = t % 4
                li = t % 3
                lb_, lk = banks[li], ("bank", li)
                P.act(lambda e: e.activation(aa[ai], lb_, AF.Exp), reads=[lk], writes=[("aa", ai)])
                if r >= 0:
                    P.pool(lambda e: e.tensor_tensor(aa[ai], aa[ai], msbw[:, (3 - r) * 128:(3 - r) * 128 + 512], ALU.mult),
                           reads=[("aa", ai), "msb"], writes=[("aa", ai)])

            def stageC(t, it):
                if it["newhead"] and it["hh"] + 1 < 12:
                    load_head(it["hh"] + 1)
                hb_, a = it["hb"], it["a"]
                ai = t % 4
                ob = banks[6 + it["ob"]]
                okey = ("bank", 6 + it["ob"])
                first, last = it["first"], it["last"]
                P.pe(lambda e: e.matmul(ob[0:65, :], vv[hb_][:, a, :], aa[ai], start=first, stop=last),
                     reads=[("vv", hb_), ("aa", ai)], writes=[okey])
                if not last:
                    return
                hh, B = it["hh"], it["B"]
                qs = slice(B * 512, (B + 1) * 512)
                ob_ = it["ob"]
                if it["typ"] == "fx":
                    P.dve(lambda e: e.reciprocal(rec[64:65, :], ob[64:65, :]), reads=[okey], writes=["rec"])
                    P.dve(lambda e: e.tensor_copy(osb[0:64, :], ob[0:64, :]), reads=[okey], writes=["osb"])
                    mm_group(banks[4][0:64, :], [(ones32[64:65, 0:64], rec[64:65, :])], reads=["rec", "ones32"],
                             writes=[("bank", 4)])
                    P.dve(lambda e: e.tensor_tensor(on[ob_], osb[0:64, :], banks[4][0:64, :], ALU.mult),
                          reads=["osb", ("bank", 4)], writes=[("on", ob_)])
                else:
                    P.dve(lambda e: e.tensor_copy(on[ob_], ob[0:64, :]), reads=[okey], writes=[("on", ob_)])
                fch = 2 + hh // 2
                prow = (hh % 2) * 64
                P.dma("sp", lambda e: e.dma_start(out=mixT[fch, prow:prow + 64, qs], in_=on[ob_]),
                      reads=[("on", ob_)], writes=[("mixT", hh, B)])

            for t in range(n + 3):
                if 0 <= t - 2 < n:
                    stageB_act(t - 2, items[t - 2])
                if t < n:
                    stageA(t, items[t])
                if 0 <= t - 1 < n:
                    stageB_pe(t - 1, items[t - 1])
                if t < n:
                    stageA2(t, items[t])
                if 0 <= t - 3 < n:
                    stageC(t - 3, items[t - 3])

        def phase3a(l, xsrc, wo):
            AR.reset()
            mx = [AR.get([128, 8, 512], BF16) for _ in range(2)]
            xt = [AR.get([128, D], F32) for _ in range(3)]
            it = 0
            for g in range(NG):
                mb = g % 2
                for k in range(8):
                    P.dma("sp" if k % 2 == 0 else "act", lambda e, k=k, g=g, mb=mb: e.dma_start(
                        out=mx[mb][:, k, :], in_=mixT[k, :, g * 512:(g + 1) * 512]), writes=[("mx", mb)])
                for ti in range(4):
                    t0 = g * 512 + ti * 128
                    b = it % 3
                    it += 1
                    P.dma("sp", lambda e, b=b, t0=t0: e.dma_start(out=xt[b], in_=xsrc[t0:t0 + 128, :]),
                          writes=[("xt", b)])
                    for hf in range(2):
                        pb = (it * 2 + hf) % 4
                        mm_group(banks[pb], [(mx[mb][:, k, ti * 128:(ti + 1) * 128], wo[:, k, hf * 512:(hf + 1) * 512])
                                             for k in range(8)],
                                 reads=[("mx", mb), "slotB"], writes=[("bank", pb)])
                        P.dve(lambda e, b=b, hf=hf, pb=pb: e.tensor_tensor(
                            xt[b][:, hf * 512:(hf + 1) * 512], xt[b][:, hf * 512:(hf + 1) * 512], banks[pb], ALU.add),
                            reads=[("xt", b), ("bank", pb)], writes=[("xt", b)])
                    P.dma("act", lambda e, b=b, t0=t0: e.dma_start(out=xa[t0:t0 + 128, :], in_=xt[b]),
                          reads=[("xt", b)], writes=[("xa", t0)])

        def phase3f(l, half, src, dst, w1, w2, wkey):
            AR.reset()
            xt = [AR.get([128, D], F32) for _ in range(4)]
            hT = [AR.get([128, 8, 512], BF16) for _ in range(2)]
            hid = AR.get([128, 16, 512], BF16)
            rl = [AR.get([128, 512], F32) for _ in range(2)]
            if half == 0:
                gbc = AR.get([128, D], F32)
                sqs = AR.get([128, D], BF16)
                hb = [AR.get([128, D], BF16) for _ in range(2)]
                st = [AR.get([128, 4], F32) for _ in range(2)]
                P.dma("sp", lambda e: e.dma_start(out=gbc, in_=n2g[l, :].partition_broadcast(128)), writes=["gbc"])
            it = 0
            for g in range(NG):
                hb_ = g % 2
                hTg = hT[hb_]
                hk = ("hT", hb_)
                for ti in range(4):
                    t0 = g * 512 + ti * 128
                    P.dma("sp", lambda e, ti=ti, t0=t0: e.dma_start(out=xt[ti], in_=src[t0:t0 + 128, :]),
                          writes=[("xt", ti)])
                if half == 0:
                    for ti in range(4):
                        b = it % 2
                        it += 1
                        norm_tile(xt[ti], ("xt", ti), gbc, hb[b], ("hb", b), st[b], ("st", b), sqs, "sqs")
                        transpose_h(hb[b], ("hb", b), hTg, hk, ti * 128, 7, ("bank", 7))
                    for k in range(8):
                        P.dma("act", lambda e, k=k, g=g, hTg=hTg: e.dma_start(
                            out=hTs[k, :, g * 512:(g + 1) * 512], in_=hTg[:, k, :]), reads=[hk], writes=[("hTs", g)])
                else:
                    for k in range(8):
                        P.dma("sp" if k % 2 == 0 else "act", lambda e, k=k, g=g, hTg=hTg: e.dma_start(
                            out=hTg[:, k, :], in_=hTs[k, :, g * 512:(g + 1) * 512]), writes=[hk])
                for f in range(16):
                    pb = f % 2
                    mm_group(banks[pb], [(w1[:, k, f * 128:(f + 1) * 128], hTg[:, k, :]) for k in range(8)],
                             reads=[hk, wkey], writes=[("bank", pb)])
                    P.act(lambda e, pb=pb: e.activation(rl[pb], banks[pb], AF.Relu), reads=[("bank", pb)],
                          writes=[("rl", pb)])
                    if f % 2 == 0:
                        P.dve(lambda e, pb=pb, f=f: e.tensor_tensor(hid[:, f, :], rl[pb], rl[pb], ALU.mult),
                              reads=[("rl", pb)], writes=["hid"])
                    else:
                        P.pool(lambda e, pb=pb, f=f: e.tensor_tensor(hid[:, f, :], rl[pb], rl[pb], ALU.mult),
                               reads=[("rl", pb)], writes=["hid"])
                for ti in range(4):
                    t0 = g * 512 + ti * 128
                    for hf in range(2):
                        pb = 2 + (ti * 2 + hf) % 4
                        mm_group(banks[pb], [(hid[:, f, ti * 128:(ti + 1) * 128], w2[:, f, hf * 512:(hf + 1) * 512])
                                             for f in range(16)],
                                 reads=["hid", wkey], writes=[("bank", pb)])
                        P.dve(lambda e, ti=ti, hf=hf, pb=pb: e.tensor_tensor(
                            xt[ti][:, hf * 512:(hf + 1) * 512], xt[ti][:, hf * 512:(hf + 1) * 512], banks[pb],
                            ALU.add), reads=[("xt", ti), ("bank", pb)], writes=[("xt", ti)])
                    P.dma("act", lambda e, ti=ti, t0=t0: e.dma_start(out=dst[t0:t0 + 128, :], in_=xt[ti]),
                          reads=[("xt", ti)], writes=[("dst", t0)])

        def whole():
            setup_consts()
            if stop == -1:
                return
            wv = load_w_in(0)
            P.barrier()
            if stop == 0:
                return
            for l in range(depth):
                xsrc = x_in if l == 0 else xb
                wo = load_w_out(l)
                phase1a(l, xsrc, wv)
                P.barrier()
                if stop == 1:
                    return
                phase1b(l, wv)
                P.barrier()
                if stop == 2:
                    return
                w1a, w2a = load_ffn_half(l, 0, slotA, "slotA")
                phase2(l)
                P.barrier()
                if stop == 3:
                    return
                if dbg and l == 0:
                    for nm, srcd in (("d_mixT", mixT), ("d_qfx", qfx), ("d_kfx", kfx), ("d_qsb", qsb), ("d_ksb", ksb)):
                        dd = dbg_out[nm]
                        for i in range(dd.shape[0]):
                            P.dma("sp", lambda e, dd=dd, srcd=srcd, i=i: e.dma_start(out=dd[i], in_=srcd[i]))
                phase3a(l, xsrc, wo)
                P.barrier()
                if stop == 4:
                    return
                if dbg and l == 0:
                    for t0 in range(0, T, 128):
                        P.dma("sp", lambda e, t0=t0: e.dma_start(out=dbg_out["d_xa"][t0:t0 + 128, :], in_=xa[t0:t0 + 128, :]))
                w1b, w2b = load_ffn_half(l, 1, slotB, "slotB")
                phase3f(l, 0, xa, xp, w1a, w2a, "slotA")
                P.barrier()
                if stop == 5:
                    return
                if l + 1 < depth:
                    wv = load_w_in(l + 1)
                phase3f(l, 1, xp, (y if l == depth - 1 else xb), w1b, w2b, "slotB")
                P.barrier()
        whole()
        P.barrier()
        stats = P.emit()
    return nc, stats


_CACHE = {}


def _host_layout(inputs, depth):
    f = lambda a: np.ascontiguousarray(np.asarray(a, dtype=np.float32))
    cols = np.zeros((128, depth * 5), np.float32)
    for l in range(depth):
        for i, nm in enumerate(("sb_q_norm_g", "sb_k_norm_g", "fox_q_norm_g", "fox_k_norm_g", "hg_norm_g")):
            cols[:, l * 5 + i] = np.tile(f(inputs[nm])[l], 2)
    lb = f(inputs["lb_logits"])
    lbT = np.zeros((128, 2 * depth), np.float32)
    for j in range(2):
        lbT[:, j * depth:(j + 1) * depth] = lb[:, j * 128:(j + 1) * 128].T
    fb = np.ascontiguousarray(f(inputs["fox_f_bias"]).T)
    return cols, lbT, fb


def run(inputs, T, depth, n_cores, dbg=False, stop=99):
    key = (T, depth, dbg, stop)
    if key not in _CACHE:
        _CACHE[key] = build(T, depth, dbg, stop)
    nc, stats = _CACHE[key]
    f = lambda a: np.ascontiguousarray(np.asarray(a, dtype=np.float32))
    cols, lbT, fb = _host_layout(inputs, depth)
    shared = {
        "w_in": f(inputs["w_in"]), "w_out": f(inputs["w_out"]), "w_ff1": f(inputs["w_ff1"]),
        "w_ff2": f(inputs["w_ff2"]), "norm1_g": f(inputs["norm1_g"]), "norm2_g": f(inputs["norm2_g"]),
        "cols": cols, "lbT": lbT, "fb": fb,
    }
    x = f(inputs["x"])
    in_maps = []
    for c in range(n_cores):
        m = dict(shared)
        m["x"] = np.ascontiguousarray(x[c])
        in_maps.append(m)
    res = run_bass_kernel_spmd(nc, in_maps, core_ids=list(range(n_cores)))
    return res


def kernel(**inputs):
    res = run(inputs, 4096, 4, 8)
    return np.stack([np.asarray(r["y"], dtype=np.float32) for r in res.results], axis=0)
```

```python
import contextlib
import os
import numpy as np
import concourse.bass as bass
import concourse.mybir as mybir
from concourse.bass_utils import run_bass_kernel_spmd

F32 = mybir.dt.float32
BF16 = mybir.dt.bfloat16
AF = mybir.ActivationFunctionType
ALU = mybir.AluOpType
AX = mybir.AxisListType

D = 1024
INC = 3334
DFF = 4096
EPS = 1e-6
OFF = dict(hq=0, hf=256, hi=512, hg=768, sq=1024, sk=1408, sv=1792, fq=2176, fk=2560,
           fv=2944, ff=3328)
N_DMA_SEMS = {"hw": 24, "sw": 8}
GROUP_DMA = True


class Op:
    __slots__ = ("eng", "fn", "waits", "signal", "dma_idx", "seq", "count")

    def __init__(self, eng, fn):
        self.eng = eng
        self.fn = fn
        self.waits = []
        self.signal = False
        self.dma_idx = None
        self.seq = None
        self.count = None


class Prog:
    ENGS = ("pe", "act", "dve", "pool", "sp")

    def __init__(self, nc):
        self.nc = nc
        self.ops = {e: [] for e in self.ENGS}
        self.state = {}
        self.waited = {e: {} for e in self.ENGS}
        self.n_dma = {"hw": 0, "sw": 0}

    def _add(self, eng, fn, reads, writes, dma=False):
        op = Op(eng, fn)
        op.seq = len(self.ops[eng])
        if dma:
            pool = "sw" if eng == "pool" else "hw"
            op.dma_idx = (pool, self.n_dma[pool])
            self.n_dma[pool] += 1
            tok = ("d", pool, op.dma_idx[1])
            if op.dma_idx[1] >= N_DMA_SEMS[pool]:
                self._need(op, ("d", pool, op.dma_idx[1] - N_DMA_SEMS[pool]))
        else:
            tok = ("e", eng, op.seq)
        st = self.state
        xb_ = [k for k in reads if isinstance(k, tuple) and k[0] == "bank" and k not in writes]
        if xb_:
            writes = list(writes) + xb_
        for k in reads:
            s = st.get(k)
            if s is not None:
                for w in s[0]:
                    self._need(op, w)
                s[2] = False
        for k in writes:
            s = st.get(k)
            if s is None:
                continue
            if dma and s[2] and GROUP_DMA:
                for w in s[3]:
                    self._need(op, w)
                continue
            for w in s[0]:
                self._need(op, w)
            for rt in s[1].values():
                self._need(op, rt)
        for k in reads:
            s = st.setdefault(k, [[], {}, False, []])
            if dma:
                s[1][tok] = tok
            else:
                s[1][eng] = tok
        for k in writes:
            s = st.get(k)
            if dma and s is not None and s[2] and GROUP_DMA:
                s[0].append(tok)
            else:
                pre = (list(s[0]) + list(s[1].values())) if s is not None else []
                st[k] = [[tok], {}, bool(dma), pre]
        self.ops[eng].append(op)
        return op

    def _need(self, op, tok):
        if tok[0] == "e":
            key, val = tok[1], tok[2]
        else:
            key, val = ("d", tok[1], tok[2] % N_DMA_SEMS[tok[1]]), tok[2]
        wd = self.waited[op.eng]
        if wd.get(key, -1) >= val:
            return
        wd[key] = val
        op.waits.append(tok)
        if tok[0] == "e":
            self.ops[tok[1]][tok[2]].signal = True

    def pe(self, fn, reads=(), writes=()):
        return self._add("pe", fn, reads, writes)

    def act(self, fn, reads=(), writes=()):
        return self._add("act", fn, reads, writes)

    def dve(self, fn, reads=(), writes=()):
        return self._add("dve", fn, reads, writes)

    def pool(self, fn, reads=(), writes=()):
        return self._add("pool", fn, reads, writes)

    def dma(self, q, fn, reads=(), writes=()):
        return self._add(q, fn, reads, writes, dma=True)

    def barrier(self):
        last = {}
        for e in self.ENGS:
            for op in reversed(self.ops[e]):
                if op.dma_idx is None and op.fn is not None:
                    last[e] = ("e", e, op.seq)
                    break
        ndma = dict(self.n_dma)
        for e in self.ENGS:
            op = Op(e, None)
            op.seq = len(self.ops[e])
            for e2, tok in last.items():
                if e2 != e:
                    self._need(op, tok)
            for pool in ("hw", "sw"):
                for i in range(max(0, ndma[pool] - N_DMA_SEMS[pool]), ndma[pool]):
                    self._need(op, ("d", pool, i))
            self.ops[e].append(op)
        self.state = {}

    def emit(self):
        nc = self.nc
        for e in self.ENGS:
            c = 0
            for op in self.ops[e]:
                if op.dma_idx is None and op.signal:
                    c += 1
                op.count = c
        stats = {}
        with contextlib.ExitStack() as es:
            esem = {e: es.enter_context(nc.semaphore("s_" + e)) for e in self.ENGS}
            dsem = {p: [es.enter_context(nc.semaphore("d%s_%d" % (p, i))) for i in range(N_DMA_SEMS[p])]
                    for p in ("hw", "sw")}
            block = es.enter_context(nc.Block())
            engobj = {"pe": block.tensor, "act": block.scalar, "dve": block.vector,
                      "pool": block.gpsimd, "sp": block.sync}
            allops = self.ops
            for e in self.ENGS:
                def body(eng, e=e):
                    nw = 0
                    for op in allops[e]:
                        for tok in op.waits:
                            if tok[0] == "e":
                                eng.wait_ge(esem[tok[1]], allops[tok[1]][tok[2]].count)
                            else:
                                pl, i = tok[1], tok[2]
                                eng.wait_ge(dsem[pl][i % N_DMA_SEMS[pl]], 16 * (i // N_DMA_SEMS[pl] + 1))
                            nw += 1
                        if op.fn is None:
                            continue
                        ins = op.fn(eng)
                        if op.dma_idx is not None:
                            pl, i = op.dma_idx
                            ins.then_inc(dsem[pl][i % N_DMA_SEMS[pl]], 16)
                        elif op.signal:
                            ins.then_inc(esem[e], 1)
                    stats[e] = (len(allops[e]), nw, allops[e][-1].count if allops[e] else 0)
                engobj[e](body)
        return stats


def build(T, depth, dbg=False, stop=99):
    nc = bass.Bass("TRN2", target_bir_lowering=False)
    NT = T // 128
    NG = T // 512
    P = Prog(nc)

    def din(name, shape, dt=F32):
        return nc.dram_tensor(name, list(shape), dt, kind="ExternalInput").ap()

    def dscr(name, shape, dt):
        return nc.dram_tensor(name, list(shape), dt, kind="Internal").ap()

    x_in = din("x", [T, D])
    w_in = din("w_in", [depth, D, INC])
    w_out = din("w_out", [depth, D, D])
    w_ff1 = din("w_ff1", [depth, D, DFF])
    w_ff2 = din("w_ff2", [depth, DFF, D])
    n1g = din("norm1_g", [depth, D])
    n2g = din("norm2_g", [depth, D])
    cols_in = din("cols", [128, depth * 5])
    lbT_in = din("lbT", [128, 2 * depth])
    fb_in = din("fb", [6, depth])
    y = nc.dram_tensor("y", [T, D], F32, kind="ExternalOutput").ap()

    xa = dscr("xa", [T, D], F32)
    xp = dscr("xp", [T, D], F32)
    xb = dscr("xb", [T, D], F32)
    hTs = dscr("hTs", [8, 128, T], BF16)
    qsb = dscr("qsb", [6, 64, T], BF16)
    ksb = dscr("ksb", [6, 64, T], BF16)
    qfx = dscr("qfx", [6, 67, T], BF16)
    kfx = dscr("kfx", [6, 67, T], BF16)
    vsc = dscr("vsc", [T, 12 * 65], BF16)
    mixT = dscr("mixT", [8, 128, T], BF16)
    dbg_out = {}
    if dbg:
        for nm, shp, dt in (("d_mixT", [8, 128, T], BF16), ("d_xa", [T, D], F32),
                            ("d_qfx", [6, 67, T], BF16), ("d_kfx", [6, 67, T], BF16),
                            ("d_qsb", [6, 64, T], BF16), ("d_ksb", [6, 64, T], BF16)):
            dbg_out[nm] = nc.dram_tensor(nm, shp, dt, kind="ExternalOutput").ap()

    es = contextlib.ExitStack()
    with es:
        def sb(name, shape, dt):
            return es.enter_context(nc.sbuf_tensor("sb_" + name, list(shape), dt))

        def ps(name, shape, dt=F32):
            return es.enter_context(nc.psum_tensor("ps_" + name, list(shape), dt))

        slotA = sb("slotA", [128, 32768], BF16)
        slotB = sb("slotB", [128, 32768], BF16)
        ARENA = 63 * 1024
        arena = sb("arena", [128, ARENA], mybir.dt.uint8)
        ident = sb("ident", [128, 128], BF16)
        ident32 = sb("ident32", [128, 128], F32)
        blk = sb("blk", [128, 128], BF16)
        negtri = sb("negtri", [128, 128], BF16)
        negones = sb("negones", [128, 128], BF16)
        msbw = sb("msbw", [128, 896], BF16)
        mfxw = sb("mfxw", [128, 896], BF16)
        hmask = sb("hmask", [128, 4, 128], BF16)
        rmask = sb("rmask", [128, 512], BF16)
        ones32 = sb("ones32", [128, 64], F32)
        onesb = sb("onesb", [128, 512], BF16)
        cols = sb("cols", [128, depth * 5], F32)
        colq = sb("colq", [128, depth * 2], F32)
        lbx = sb("lbx", [128, 2 * depth], F32)
        lbc = sb("lbc", [128, 2 * depth], F32)
        omlb = sb("omlb", [128, 2 * depth], F32)
        lbs = sb("lbs", [128, 2], F32)
        fbc = sb("fbc", [6, depth], F32)
        nfb = sb("nfb", [6, depth], F32)
        cposTok = sb("cposTok", [128, NT, 6], F32)
        epsc = sb("epsc", [128, 1], F32)
        onec = sb("onec", [128, 1], F32)

        banks = [ps("bank%d" % i, [128, 512], F32)[:, :] for i in range(8)]

        class Arena:
            def __init__(self):
                self.off = 0

            def reset(self):
                self.off = 0

            def get(self, shape, dt):
                nb = int(np.prod(shape[1:])) * (4 if dt == F32 else 2)
                nbytes = (nb + 31) // 32 * 32
                assert self.off + nbytes <= ARENA, ("arena overflow", self.off, nbytes)
                ap = arena[0:shape[0], self.off:self.off + nb].bitcast(dt)
                self.off += nbytes
                if len(shape) == 3:
                    ap = ap.rearrange("p (a b) -> p a b", b=shape[2])
                return ap
        AR = Arena()

        def bankbf(i):
            return banks[i][:, :].bitcast(BF16)

        def setup_consts():
            P.pool(lambda e: e.memset(ident[:, :], 1.0), writes=["ident"])
            P.pool(lambda e: e.affine_select(ident[:, :], ident[:, :], [[1, 128]], ALU.is_equal, 0.0,
                                             base=0, channel_multiplier=-1),
                   reads=["ident"], writes=["ident"])
            P.pool(lambda e: e.memset(ident32[:, :], 1.0), writes=["ident32"])
            P.pool(lambda e: e.affine_select(ident32[:, :], ident32[:, :], [[1, 128]], ALU.is_equal,
                                             0.0, base=0, channel_multiplier=-1),
                   reads=["ident32"], writes=["ident32"])
            P.pool(lambda e: e.memset(blk[:, :], 0.0), writes=["blk"])
            P.pool(lambda e: e.memset(blk[0:64, 0:64], 1.0 / 64), reads=["blk"], writes=["blk"])
            P.pool(lambda e: e.memset(blk[64:128, 64:128], 1.0 / 64), reads=["blk"], writes=["blk"])
            P.pool(lambda e: e.memset(negtri[:, :], -1.0), writes=["negtri"])
            P.pool(lambda e: e.affine_select(negtri[:, :], negtri[:, :], [[-1, 128]], ALU.is_ge, 0.0,
                                             base=0, channel_multiplier=1),
                   reads=["negtri"], writes=["negtri"])
            P.pool(lambda e: e.memset(negones[:, :], -1.0), writes=["negones"])
            P.pool(lambda e: e.memset(onesb[:, :], 1.0), writes=["onesb"])
            P.pool(lambda e: e.memset(ones32[:, :], 1.0), writes=["ones32"])
            P.pool(lambda e: e.memset(epsc[:, :], EPS), writes=["epsc"])
            P.pool(lambda e: e.memset(onec[:, :], 1.0), writes=["onec"])
            P.pool(lambda e: e.memset(msbw[:, :], 1.0), writes=["msb"])
            P.pool(lambda e: e.affine_select(msbw[:, :], msbw[:, :], [[1, 896]], ALU.is_gt,
                                             0.0, base=-384, channel_multiplier=-1),
                   reads=["msb"], writes=["msb"])
            P.pool(lambda e: e.memset(mfxw[:, :], 1.0), writes=["mfx"])
            P.pool(lambda e: e.affine_select(mfxw[:, :], mfxw[:, :], [[1, 896]], ALU.is_ge,
                                             0.0, base=-384, channel_multiplier=-1),
                   reads=["mfx"], writes=["mfx"])
            P.pool(lambda e: e.memset(hmask[:, :, :], 0.0), writes=["hmask"])
            for c in range(2):
                P.pool(lambda e, c=c: e.memset(hmask[c * 64:(c + 1) * 64, :, c * 64:(c + 1) * 64], 1.0),
                       reads=["hmask"], writes=["hmask"])
            P.pool(lambda e: e.affine_select(hmask[:, :, :], hmask[:, :, :], [[0, 4], [1, 128]],
                                             ALU.is_ge, 0.0, base=0, channel_multiplier=-1),
                   reads=["hmask"], writes=["hmask"])
            P.pool(lambda e: e.memset(rmask[:, :], 1.0), writes=["rmask"])
            P.pool(lambda e: e.memset(rmask[:, :].rearrange("p (c t) -> p c t", t=64)[:, :, 0:1], 0.0),
                   reads=["rmask"], writes=["rmask"])
            P.dma("sp", lambda e: e.dma_start(out=cols[:, :], in_=cols_in[:, :]), writes=["cols"])
            P.dma("sp", lambda e: e.dma_start(out=lbx[:, :], in_=lbT_in[:, :]), writes=["lbx"])
            P.dma("sp", lambda e: e.dma_start(out=fbc[:, :], in_=fb_in[:, :]), writes=["fbc"])
            P.dve(lambda e: e.tensor_scalar(nfb[:, :], fbc[:, :], -1.0, None, ALU.mult),
                  reads=["fbc"], writes=["nfb"])
            for l in range(depth):
                for i, c in enumerate((0, 2)):
                    P.dve(lambda e, l=l, i=i, c=c: e.tensor_scalar(
                        colq[:, l * 2 + i:l * 2 + i + 1], cols[:, l * 5 + c:l * 5 + c + 1],
                        0.125, None, ALU.mult), reads=["cols"], writes=["colq"])
            P.act(lambda e: e.activation(lbx[:, :], lbx[:, :], AF.Exp), reads=["lbx"], writes=["lbx"])
            lb3 = lbx[:, :].rearrange("p (j l) -> p j l", l=depth)
            P.dve(lambda e: e.tensor_reduce(lbs[:, :], lb3, AX.X, ALU.add), reads=["lbx"], writes=["lbs"])
            P.dve(lambda e: e.reciprocal(lbs[:, :], lbs[:, :]), reads=["lbs"], writes=["lbs"])
            for j in range(2):
                P.dve(lambda e, j=j: e.tensor_scalar(lbx[:, j * depth:(j + 1) * depth],
                                                     lbx[:, j * depth:(j + 1) * depth],
                                                     lbs[:, j:j + 1], None, ALU.mult),
                      reads=["lbx", "lbs"], writes=["lbx"])
                P.dve(lambda e, j=j: e.memset(lbc[:, j * depth:j * depth + 1], 0.0),
                      reads=["lbc"], writes=["lbc"])
                for l in range(1, depth):
                    P.dve(lambda e, j=j, l=l: e.tensor_tensor(
                        lbc[:, j * depth + l:j * depth + l + 1], lbc[:, j * depth + l - 1:j * depth + l],
                        lbx[:, j * depth + l:j * depth + l + 1], ALU.add),
                        reads=["lbc", "lbx"], writes=["lbc"])
            P.dve(lambda e: e.tensor_scalar(omlb[:, :], lbc[:, :], -1.0, 1.0, ALU.mult, ALU.add),
                  reads=["lbc"], writes=["omlb"])
            for h in range(6):
                for t0 in range(0, T, 512):
                    P.dma("sp", lambda e, h=h, t0=t0: e.dma_start(out=kfx[h, 64:67, t0:t0 + 512],
                                                                   in_=onesb[0:3, :]),
                          reads=["onesb"], writes=[("kfx1", h)])

        def load_w_in(l):
            v = slotA[:, 0:8 * INC].rearrange("p (k n) -> p k n", n=INC)
            for k in range(8):
                for hh in range(2):
                    c0 = hh * 1667
                    P.dma("pool", lambda e, k=k, c0=c0: e.dma_start(
                        out=v[:, k, c0:c0 + 1667], in_=w_in[l, k * 128:(k + 1) * 128, c0:c0 + 1667]),
                        writes=["slotA"])
            return v

        def load_w_out(l):
            v = slotB[:, 0:8 * D].rearrange("p (k n) -> p k n", n=D)
            for k in range(8):
                P.dma("pool", lambda e, k=k: e.dma_start(
                    out=v[:, k, :], in_=w_out[l, k * 128:(k + 1) * 128, :]), writes=["slotB"])
            return v

        def load_ffn_half(l, hf, slot, key):
            w1 = slot[:, 0:16384].rearrange("p (k n) -> p k n", n=2048)
            w2 = slot[:, 16384:32768].rearrange("p (k n) -> p k n", n=1024)
            for k in range(8):
                for hh in range(2):
                    c0 = hf * 2048 + hh * 1024
                    P.dma("pool", lambda e, k=k, c0=c0, hh=hh: e.dma_start(
                        out=w1[:, k, hh * 1024:(hh + 1) * 1024],
                        in_=w_ff1[l, k * 128:(k + 1) * 128, c0:c0 + 1024]), writes=[key])
            for fc in range(16):
                r0 = hf * 2048 + fc * 128
                P.dma("pool", lambda e, fc=fc, r0=r0: e.dma_start(
                    out=w2[:, fc, :], in_=w_ff2[l, r0:r0 + 128, :]), writes=[key])
            return w1, w2

        def mm_group(out, pairs, reads, writes):
            def fn(e):
                n = len(pairs)
                ins = None
                for i, (lt, r) in enumerate(pairs):
                    ins = e.matmul(out, lt, r, start=(i == 0), stop=(i == n - 1))
                return ins
            P.pe(fn, reads, writes)

        def norm_tile(xt, xkey, gbc, hb, hkey, st, stkey, sq_scr, sqkey):
            P.act(lambda e: e.activation(sq_scr, xt, AF.Square, accum_out=st[:, 0:1]),
                  reads=[xkey], writes=[sqkey, stkey])
            P.act(lambda e: e.activation(st[:, 1:2], st[:, 0:1], AF.Ln, bias=epsc[:, 0:1], scale=1.0 / D),
                  reads=[stkey], writes=[stkey])
            P.act(lambda e: e.activation(st[:, 2:3], st[:, 1:2], AF.Exp, scale=-0.5),
                  reads=[stkey], writes=[stkey])
            P.dve(lambda e: e.scalar_tensor_tensor(hb, xt, st[:, 2:3], gbc, ALU.mult, ALU.mult),
                  reads=[xkey, stkey, "gbc"], writes=[hkey])

        def transpose_h(hb, hkey, hT, hTkey, tcol, pbank, pkey):
            for half in range(2):
                pv = bankbf(pbank)

                def fn(e, half=half, pv=pv):
                    ins = None
                    for kk in range(4):
                        k = half * 4 + kk
                        ins = e.transpose(pv[:, kk * 128:(kk + 1) * 128], hb[:, k * 128:(k + 1) * 128],
                                          ident[:, :])
                    return ins
                P.pe(fn, reads=[hkey, "ident"], writes=[pkey])
                src = pv[:, 0:512].rearrange("p (k t) -> p k t", t=128)
                dst = hT[:, half * 4:(half + 1) * 4, tcol:tcol + 128]
                if half == 0:
                    P.act(lambda e, src=src, dst=dst: e.activation(dst, src, AF.Copy),
                          reads=[pkey], writes=[hTkey])
                else:
                    P.dve(lambda e, src=src, dst=dst: e.tensor_copy(dst, src),
                          reads=[pkey], writes=[hTkey])

        def headnorm_a(psrc, pskey, sq, sqkey):
            P.act(lambda e: e.activation(sq, psrc, AF.Square), reads=[pskey], writes=[sqkey])

        def headnorm_b(psrc, pskey, gcol, gkey, bank_ms, mskey, sq, sqkey, rs, rskey, extra_mul, emkey,
                       dst, dstkey):
            mm_group(bank_ms, [(blk[:, :], sq)], reads=[sqkey, "blk"], writes=[mskey])
            P.act(lambda e: e.activation(rs, bank_ms, AF.Ln, bias=epsc[:, 0:1]),
                  reads=[mskey], writes=[rskey])
            P.act(lambda e: e.activation(rs, rs, AF.Exp, scale=-0.5), reads=[rskey], writes=[rskey])
            if extra_mul is None:
                P.dve(lambda e: e.scalar_tensor_tensor(dst, psrc, gcol, rs, ALU.mult, ALU.mult),
                      reads=[pskey, rskey, gkey], writes=[dstkey])
            else:
                P.dve(lambda e: e.scalar_tensor_tensor(rs, psrc, gcol, rs, ALU.mult, ALU.mult),
                      reads=[pskey, rskey, gkey], writes=[rskey])
                P.pool(lambda e: e.tensor_tensor(dst, rs, extra_mul, ALU.mult),
                       reads=[rskey, emkey], writes=[dstkey])

        def headnorm(psrc, pskey, gcol, gkey, bank_ms, mskey, sq, sqkey, rs, rskey, extra_mul, emkey,
                     dst, dstkey):
            headnorm_a(psrc, pskey, sq, sqkey)
            headnorm_b(psrc, pskey, gcol, gkey, bank_ms, mskey, sq, sqkey, rs, rskey, extra_mul, emkey, dst, dstkey)

        def phase1a(l, xsrc, wv):
            AR.reset()
            gbc = AR.get([128, D], F32)
            xt = [AR.get([128, D], F32) for _ in range(2)]
            sqs = AR.get([128, D], BF16)
            hb = [AR.get([128, D], BF16) for _ in range(2)]
            st = [AR.get([128, 4], F32) for _ in range(2)]
            hT = [AR.get([128, 8, 512], BF16) for _ in range(2)]
            sq = [AR.get([128, 512], BF16) for _ in range(2)]
            rs = [AR.get([128, 512], F32) for _ in range(2)]
            qn = [AR.get([128, 512], BF16) for _ in range(2)]
            vaug = [AR.get([128, 12, 65], BF16) for _ in range(2)]
            spt = AR.get([6, 512], F32)
            cpos = [AR.get([6, 512], F32) for _ in range(2)]
            rr = AR.get([6, 512], F32)
            qh = [AR.get([6, 512], BF16) for _ in range(3)]
            P.dma("sp", lambda e: e.dma_start(out=gbc, in_=n1g[l, :].partition_broadcast(128)),
                  writes=["gbc"])
            for i in range(2):
                P.pool(lambda e, i=i: e.memset(vaug[i][:, :, 64:65], 1.0), writes=[("vaug", i)])
            itc = [0]

            def prep_tile(g, ti):
                t0 = g * 512 + ti * 128
                b = itc[0] % 2
                itc[0] += 1
                P.dma("sp", lambda e, b=b, t0=t0: e.dma_start(out=xt[b], in_=xsrc[t0:t0 + 128, :]),
                      writes=[("xt", b)])
                norm_tile(xt[b], ("xt", b), gbc, hb[b], ("hb", b), st[b], ("st", b), sqs, "sqs")
                transpose_h(hb[b], ("hb", b), hT[g % 2], ("hT", g % 2), ti * 128, 7, ("bank", 7))

            for ti in range(4):
                prep_tile(0, ti)
            for g in range(NG):
                hb_ = g % 2
                hTg = hT[hb_]
                hk = ("hT", hb_)
                for k in range(8):
                    P.dma("act", lambda e, k=k, g=g, hTg=hTg: e.dma_start(
                        out=hTs[k, :, g * 512:(g + 1) * 512], in_=hTg[:, k, :]),
                        reads=[hk], writes=[("hTs", g)])
                ci = 0
                pend = None

                def tail(args):
                    (bank, pb, gcol, gkey, typ, qk, c, dst_d) = args
                    headnorm_b(bank, ("bank", pb), gcol, gkey, banks[2 + pb], ("bank", 2 + pb),
                               sq[pb], ("sq", pb), rs[pb], ("rs", pb), None, None, qn[pb], ("qn", pb))
                    for hp in range(2):
                        P.dma("sp", lambda e, pb=pb, hp=hp, c=c, g=g, dst_d=dst_d: e.dma_start(
                            out=dst_d[2 * c + hp, 0:64, g * 512:(g + 1) * 512],
                            in_=qn[pb][hp * 64:(hp + 1) * 64, :]),
                            reads=[("qn", pb)], writes=[("qkd", typ, qk, 2 * c + hp, g)])

                for typ, qk, off, dst_d, gci in (("sb", "q", OFF["sq"], qsb, None),
                                                 ("sb", "k", OFF["sk"], ksb, 1),
                                                 ("fx", "q", OFF["fq"], qfx, None),
                                                 ("fx", "k", OFF["fk"], kfx, 3)):
                    for c in range(3):
                        pb = ci % 2
                        ci += 1
                        bank = banks[pb]
                        c0 = off + c * 128
                        mm_group(bank, [(wv[:, k, c0:c0 + 128], hTg[:, k, :]) for k in range(8)],
                                 reads=[hk, "slotA"], writes=[("bank", pb)])
                        if qk == "q":
                            qi = 0 if typ == "sb" else 1
                            gcol = colq[:, l * 2 + qi:l * 2 + qi + 1]
                            gkey = "colq"
                        else:
                            gcol = cols[:, l * 5 + gci:l * 5 + gci + 1]
                            gkey = "cols"
                        headnorm_a(bank, ("bank", pb), sq[pb], ("sq", pb))
                        if pend is not None:
                            tail(pend)
                        pend = (bank, pb, gcol, gkey, typ, qk, c, dst_d)
                        if ci % 3 == 0 and g + 1 < NG:
                            prep_tile(g + 1, ci // 3 - 1)
                tail(pend)
                for ti in range(4):
                    vb = (g * 4 + ti) % 2
                    for vi, off in enumerate((OFF["sv"], OFF["fv"])):
                        bank = banks[4 + vi]
                        mm_group(bank[:, 0:384],
                                 [(hTg[:, k, ti * 128:(ti + 1) * 128], wv[:, k, off:off + 384])
                                  for k in range(8)],
                                 reads=[hk, "slotA"], writes=[("bank", 4 + vi)])
                        src = bank[:, 0:384].rearrange("p (h d) -> p h d", d=64)
                        dst = vaug[vb][:, vi * 6:(vi + 1) * 6, 0:64]
                        if vi == 0:
                            P.act(lambda e, src=src, dst=dst: e.activation(dst, src, AF.Copy),
                                  reads=[("bank", 4 + vi)], writes=[("vaug", vb)])
                        else:
                            P.dve(lambda e, src=src, dst=dst: e.tensor_copy(dst, src),
                                  reads=[("bank", 4 + vi)], writes=[("vaug", vb)])
                    t0 = g * 512 + ti * 128
                    P.dma("sp", lambda e, vb=vb, t0=t0: e.dma_start(
                        out=vsc[t0:t0 + 128, :].rearrange("p (h d) -> p h d", d=65), in_=vaug[vb]),
                        reads=[("vaug", vb)], writes=[("vsc", t0)])
                if os.environ.get("SKIPC") == "1":
                    continue
                mm_group(banks[6][0:6, :], [(wv[:, k, OFF["ff"]:OFF["ff"] + 6], hTg[:, k, :]) for k in range(8)],
                         reads=[hk, "slotA"], writes=[("bank", 6)])
                P.act(lambda e: e.activation(spt, banks[6][0:6, :], AF.Exp, bias=nfb[:, l:l + 1], scale=-1.0),
                      reads=[("bank", 6), "nfb"], writes=["spt"])
                P.act(lambda e: e.activation(spt, spt, AF.Ln, bias=onec[0:6, 0:1]), reads=["spt"], writes=["spt"])
                cb = g % 2
                if g == 0:
                    P.dve(lambda e, cb=cb: e.tensor_tensor_scan(cpos[cb], onesb[0:6, :], spt, 0.0,
                                                                ALU.mult, ALU.add),
                          reads=["spt", "onesb"], writes=[("cpos", cb)])
                else:
                    P.dve(lambda e, cb=cb: e.tensor_tensor_scan(cpos[cb], onesb[0:6, :], spt,
                                                                cpos[1 - cb][:, 511:512], ALU.mult, ALU.add),
                          reads=["spt", "onesb", ("cpos", 1 - cb)], writes=[("cpos", cb)])
                P.dve(lambda e, cb=cb: e.tensor_scalar(qh[0], cpos[cb], -1.0, None, ALU.mult),
                      reads=[("cpos", cb)], writes=["qh0"])
                P.dve(lambda e, cb=cb: e.scalar_tensor_tensor(rr, cpos[cb], -1.0, qh[0], ALU.mult, ALU.subtract),
                      reads=[("cpos", cb), "qh0"], writes=["rr"])
                P.dve(lambda e: e.tensor_copy(qh[1], rr), reads=["rr"], writes=["qh1"])
                P.dve(lambda e: e.tensor_tensor(qh[2], rr, qh[1], ALU.subtract), reads=["rr", "qh1"],
                      writes=["qh2"])
                for i in range(3):
                    P.dma("sp", lambda e, i=i, g=g: e.dma_start(out=qfx[:, 64 + i, g * 512:(g + 1) * 512],
                                                                 in_=qh[i]),
                          reads=["qh%d" % i], writes=[("qfxc", i, g)])
                if os.environ.get("SKIPC3") == "1":
                    continue
                for ti in range(4):
                    P.pe(lambda e, ti=ti, cb=cb: e.matmul(banks[6][:, 0:6], cpos[cb][:, ti * 128:(ti + 1) * 128],
                                                         ident32[0:6, 0:6], start=True, stop=True),
                         reads=[("cpos", cb), "ident32"], writes=[("bank", 6)])
                    P.dve(lambda e, ti=ti, g=g: e.tensor_copy(cposTok[:, g * 4 + ti, :], banks[6][:, 0:6]),
                          reads=[("bank", 6)], writes=["cposTok"])

        def phase1b(l, wv):
            AR.reset()
            hT = [AR.get([128, 8, 512], BF16) for _ in range(2)]
            t1 = [AR.get([128, 512], F32) for _ in range(2)]
            t2 = AR.get([128, 512], F32)
            t3 = AR.get([128, 512], F32)
            t4 = [AR.get([128, 512], F32) for _ in range(2)]
            t5 = AR.get([128, 512], F32)
            gs = [AR.get([128, 512], F32) for _ in range(2)]
            qT = AR.get([128, 2, 512], BF16)
            kTz = [[AR.get([128, 512], BF16) for _ in range(2)] for _ in range(2)]
            khT = AR.get([128, 2, 512], BF16)
            khat = AR.get([128, 4, 256], BF16)
            vt = AR.get([128, 4, 256], BF16)
            vtz = [AR.get([128, 4, 256], BF16) for _ in range(2)]
            atb = [AR.get([128, 4, 128], BF16) for _ in range(2)]
            S32 = [AR.get([128, 2, 64], F32) for _ in range(2)]
            Sbz = [[AR.get([128, 2, 64], BF16) for _ in range(2)] for _ in range(4)]
            sq = AR.get([128, 512], BF16)
            rs = AR.get([128, 512], F32)
            mo = [AR.get([128, 512], BF16) for _ in range(2)]
            for j in range(2):
                for hp in range(2):
                    P.pool(lambda e, j=j, hp=hp: e.memset(kTz[j][hp], 0.0), writes=[("kTz", j)])
            for c in range(2):
                P.pool(lambda e, c=c: e.memset(vtz[c], 0.0), writes=["vtz"])
            for r in range(4):
                for hp in range(2):
                    P.pool(lambda e, r=r, hp=hp: e.memset(Sbz[r][hp], 0.0), writes=[("Sbz", r)])
            P.pool(lambda e: e.memset(S32[0], 0.0), writes=[("S32", 0)])
            sidx = 0
            ring = 0
            for g in range(NG):
                hb_ = g % 2
                hTg = hT[hb_]
                hk = ("hT", hb_)
                for k in range(8):
                    P.dma("sp" if k % 2 == 0 else "act", lambda e, k=k, g=g, hTg=hTg: e.dma_start(
                        out=hTg[:, k, :], in_=hTs[k, :, g * 512:(g + 1) * 512]), writes=[hk])
                for j in range(2):
                    bf_, bq_, bg_ = banks[0], banks[1], banks[2]
                    mm_group(bf_, [(wv[:, k, OFF["hf"] + j * 128:OFF["hf"] + (j + 1) * 128], hTg[:, k, :])
                                   for k in range(8)], reads=[hk, "slotA"], writes=[("bank", 0)])
                    mm_group(bq_, [(wv[:, k, OFF["hq"] + j * 128:OFF["hq"] + (j + 1) * 128], hTg[:, k, :])
                                   for k in range(8)], reads=[hk, "slotA"], writes=[("bank", 1)])
                    mm_group(bg_, [(wv[:, k, OFF["hg"] + j * 128:OFF["hg"] + (j + 1) * 128], hTg[:, k, :])
                                   for k in range(8)], reads=[hk, "slotA"], writes=[("bank", 2)])
                    a = t1[j]
                    ak = ("t1", j)
                    P.act(lambda e, a=a: e.activation(a, bf_, AF.Exp, scale=-1.0), reads=[("bank", 0)], writes=[ak])
                    P.dve(lambda e, a=a: e.tensor_scalar(a, a, 1.0, None, ALU.add), reads=[ak], writes=[ak])
                    P.dve(lambda e, a=a: e.reciprocal(a, a), reads=[ak], writes=[ak])
                    ci = j * depth + l
                    P.dve(lambda e, a=a, ci=ci: e.tensor_scalar(a, a, omlb[:, ci:ci + 1], lbc[:, ci:ci + 1],
                                                                ALU.mult, ALU.add),
                          reads=[ak, "omlb", "lbc"], writes=[ak])
                    P.act(lambda e, a=a: e.activation(t2, a, AF.Ln), reads=[ak], writes=["t2"])
                    P.dve(lambda e: e.tensor_tensor_scan(t3, rmask[:, :], t2, 0.0, ALU.mult, ALU.add),
                          reads=["t2", "rmask"], writes=["t3"])
                    e4 = t4[j]
                    P.act(lambda e, e4=e4: e.activation(e4, t3, AF.Exp), reads=["t3"], writes=[("t4", j)])
                    P.act(lambda e: e.activation(t5, t3, AF.Exp, scale=-1.0), reads=["t3"], writes=["t5"])
                    P.dve(lambda e, j=j, e4=e4: e.tensor_tensor(qT[:, j, :], bq_, e4, ALU.mult),
                          reads=[("bank", 1), ("t4", j)], writes=[("qT", j)])
                    P.pool(lambda e, a=a: e.tensor_scalar(a, a, -1.0, 1.0, ALU.mult, ALU.add),
                           reads=[ak], writes=[ak])
                    P.pool(lambda e, a=a: e.tensor_tensor(t5, a, t5, ALU.mult), reads=[ak, "t5"], writes=["t5"])
                    for hp in range(2):
                        sl = slice(hp * 64, (hp + 1) * 64)
                        P.pool(lambda e, j=j, hp=hp, sl=sl: e.tensor_copy(kTz[j][hp][sl, :], t5[sl, :]),
                               reads=["t5"], writes=[("kTz", j)])
                    e4v = e4.rearrange("p (c t) -> p c t", t=64)[:, :, 63:64].to_broadcast([128, 8, 64])
                    P.dve(lambda e, j=j, e4v=e4v: e.tensor_tensor(
                        khT[:, j, :].rearrange("p (c t) -> p c t", t=64),
                        t5[:, :].rearrange("p (c t) -> p c t", t=64), e4v, ALU.mult),
                        reads=["t5", ("t4", j)], writes=[("khT", j)])
                    gg = gs[j]
                    gk = ("gs", j)
                    P.act(lambda e, gg=gg: e.activation(gg, bg_, AF.Exp, scale=-1.0), reads=[("bank", 2)], writes=[gk])
                    P.dve(lambda e, gg=gg: e.tensor_scalar(gg, gg, 1.0, None, ALU.add), reads=[gk], writes=[gk])
                    P.dve(lambda e, gg=gg: e.reciprocal(gg, gg), reads=[gk], writes=[gk])
                    P.dve(lambda e, gg=gg: e.tensor_tensor(gg, bg_, gg, ALU.mult), reads=[("bank", 2), gk], writes=[gk])
                B1 = int(os.environ.get("B1", "9"))
                if B1 <= 1:
                    continue
                for ti in range(4):
                    tsl = slice(ti * 128, (ti + 1) * 128)
                    mm_group(banks[3][:, 0:256],
                             [(hTg[:, k, tsl], wv[:, k, OFF["hi"]:OFF["hi"] + 256]) for k in range(8)],
                             reads=[hk, "slotA"], writes=[("bank", 3)])
                    P.act(lambda e, ti=ti: e.activation(vt[:, ti, :], banks[3][:, 0:256], AF.Copy),
                          reads=[("bank", 3)], writes=["vt"])
                    for c in range(2):
                        sl = slice(c * 64, (c + 1) * 64)
                        P.dve(lambda e, ti=ti, c=c, sl=sl: e.tensor_copy(vtz[c][sl, ti, :], banks[3][sl, 0:256]),
                              reads=[("bank", 3)], writes=["vtz"])
                    pv = bankbf(3)

                    def fnT(e, tsl=tsl, pv=pv):
                        ins = None
                        for j in range(2):
                            ins = e.transpose(pv[:, 512 + j * 128:512 + (j + 1) * 128], khT[:, j, tsl], ident[:, :])
                        return ins
                    P.pe(fnT, reads=[("khT", 0), ("khT", 1), "ident"], writes=[("bank", 3)])
                    P.act(lambda e, ti=ti, pv=pv: e.activation(khat[:, ti, :], pv[:, 512:768], AF.Copy),
                          reads=[("bank", 3)], writes=["khat"])
                for ti in range(4):
                    if B1 <= 2:
                        continue
                    tsl = slice(ti * 128, (ti + 1) * 128)
                    ab = ti % 2
                    def fnA(e, tsl=tsl):
                        ins = None
                        for h in range(4):
                            j, hp = h // 2, h % 2
                            ins = e.matmul(banks[4][:, h * 128:(h + 1) * 128], kTz[j][hp][:, tsl], qT[:, j, tsl],
                                           start=True, stop=True)
                        return ins
                    P.pe(fnA, reads=[("kTz", 0), ("kTz", 1), ("qT", 0), ("qT", 1)], writes=[("bank", 4)])
                    P.dve(lambda e, ab=ab: e.tensor_tensor(atb[ab], banks[4].rearrange("p (h t) -> p h t", t=128),
                                                           hmask[:, :, :], ALU.mult),
                          reads=[("bank", 4), "hmask"], writes=[("atb", ab)])
                    def fnU(e, ti=ti):
                        ins = None
                        for c in range(2):
                            for h in range(4):
                                j = h // 2
                                ins = e.matmul(banks[5][:, (c * 4 + h) * 64:(c * 4 + h + 1) * 64],
                                               khat[:, ti, j * 128:(j + 1) * 128],
                                               vtz[c][:, ti, h * 64:(h + 1) * 64], start=True, stop=True)
                        return ins
                    P.pe(fnU, reads=["khat", "vtz"], writes=[("bank", 5)])
                    rings = [ring]
                    for c in range(2):
                        chunkcol = (ti * 2 + c) * 64 + 63
                        so, sn = S32[sidx], S32[1 - sidx]
                        for h in range(4):
                            j, hp = h // 2, h % 2
                            sl = slice(hp * 64, (hp + 1) * 64)
                            P.dve(lambda e, so=so, sn=sn, j=j, sl=sl, c=c, h=h, chunkcol=chunkcol:
                                  e.scalar_tensor_tensor(sn[sl, j, :], so[sl, j, :],
                                                         t4[j][sl, chunkcol:chunkcol + 1],
                                                         banks[5][sl, (c * 4 + h) * 64:(c * 4 + h + 1) * 64],
                                                         ALU.mult, ALU.add),
                                  reads=[("S32", sidx), ("t4", j), ("bank", 5)], writes=[("S32", 1 - sidx)])
                        sidx = 1 - sidx
                        ring = (ring + 1) % 4
                        rings.append(ring)
                        for hp in range(2):
                            sl = slice(hp * 64, (hp + 1) * 64)
                            P.act(lambda e, sl=sl, hp=hp, ring=ring, sidx=sidx:
                                  e.activation(Sbz[ring][hp][sl, :, :], S32[sidx][sl, :, :], AF.Copy),
                                  reads=[("S32", sidx)], writes=[("Sbz", ring)])
                    if B1 <= 3:
                        continue
                    def fnO(e, ti=ti, tsl=tsl, ab=ab, rings=tuple(rings)):
                        ins = None
                        for h in range(4):
                            j, hp = h // 2, h % 2
                            osl = slice(hp * 64, (hp + 1) * 64)
                            ob = banks[6 + j]
                            e.matmul(ob[osl, tsl], vt[:, ti, h * 64:(h + 1) * 64], atb[ab][:, h, :],
                                     start=True, stop=False, skip_group_check=True)
                            for c in range(2):
                                csl = slice(ti * 128 + c * 64, ti * 128 + (c + 1) * 64)
                                ins = e.matmul(ob[osl, csl], Sbz[rings[c]][hp][:, j, :], qT[:, j, csl],
                                               start=False, stop=(c == 1), skip_group_check=True)
                        return ins
                    P.pe(fnO, reads=["vt", ("atb", ab), ("Sbz", rings[0]), ("Sbz", rings[1]), ("qT", 0), ("qT", 1)],
                         writes=[("bank", 6), ("bank", 7)])
                for j in range(2):
                    if B1 <= 4:
                        continue
                    gcol = cols[:, l * 5 + 4:l * 5 + 5]
                    headnorm(banks[6 + j], ("bank", 6 + j), gcol, "cols", banks[j], ("bank", j),
                             sq, "sq", rs, "rs", gs[j], ("gs", j), mo[j], ("mo", j))
                    P.dma("sp", lambda e, j=j, g=g: e.dma_start(out=mixT[j, :, g * 512:(g + 1) * 512], in_=mo[j]),
                          reads=[("mo", j)], writes=[("mixT", j, g)])

        def phase2(l):
            AR.reset()
            kT = [AR.get([67, T], BF16) for _ in range(2)]
            qTt = [AR.get([67, T], BF16) for _ in range(2)]
            vv = [AR.get([128, NT, 65], BF16) for _ in range(2)]
            ee = [AR.get([128, 512], F32) for _ in range(2)]
            spb = [AR.get([128, 512], BF16) for _ in range(3)]
            ssum = AR.get([128, 512], F32)
            ssb = [AR.get([128, 512], BF16) for _ in range(3)]
            aa = [AR.get([128, 512], BF16) for _ in range(4)]
            osb = AR.get([65, 512], F32)
            rec = AR.get([65, 512], F32)
            on = [AR.get([64, 512], BF16) for _ in range(2)]

            for b_ in range(2):
                P.pool(lambda e, b_=b_: e.memset(kT[b_][64:67, :], 0.0), writes=[("kT", b_)])
                P.pool(lambda e, b_=b_: e.memset(qTt[b_][64:67, :], 0.0), writes=[("qTt", b_)])

            def load_head(hh):
                typ = "sb" if hh < 6 else "fx"
                h = hh % 6
                hb_ = hh % 2
                nr = 64 if typ == "sb" else 67
                qd, kd = (qsb, ksb) if typ == "sb" else (qfx, kfx)
                P.dma("sp", lambda e: e.dma_start(out=kT[hb_][0:nr, :], in_=kd[h, 0:nr, :]), writes=[("kT", hb_)])
                P.dma("act", lambda e: e.dma_start(out=qTt[hb_][0:nr, :], in_=qd[h, 0:nr, :]), writes=[("qTt", hb_)])
                P.dma("sp", lambda e: e.dma_start(
                    out=vv[hb_], in_=vsc[:, hh * 65:(hh + 1) * 65].rearrange("(n p) d -> p n d", p=128)),
                    writes=[("vv", hb_)])

            items = []
            gi = 0
            for hh in range(12):
                typ = "sb" if hh < 6 else "fx"
                for B in range(NG):
                    na = 4 * B + 4
                    order = list(range(na - 1, -1, -1)) if typ == "sb" else list(range(na))
                    for idx, a in enumerate(order):
                        items.append(dict(hh=hh, typ=typ, h=hh % 6, hb=hh % 2, nr=67,
                                          B=B, a=a, r=a - 4 * B, first=(idx == 0), last=(idx == na - 1),
                                          ob=gi % 2, newhead=(B == 0 and idx == 0)))
                    gi += 1
            n = len(items)
            load_head(0)

            def geom(it):
                r = it["r"]
                c0 = max(r, 0) * 128
                return r, c0, slice(c0, 512), 512 - c0

            def stageA(t, it):
                hb_, nr, a = it["hb"], it["nr"], it["a"]
                r, c0, cs, W = geom(it)
                ks = slice(a * 128, (a + 1) * 128)
                qs = slice(it["B"] * 512 + c0, (it["B"] + 1) * 512)
                zi = t % 3
                zb, zk = banks[zi], ("bank", zi)
                sbt = (it["typ"] == "sb")
                P.pe(lambda e: e.matmul(zb[:, cs], kT[hb_][0:nr, ks], qTt[hb_][0:nr, qs], start=True, stop=(not sbt),
                                        skip_group_check=True),
                     reads=[("kT", hb_), ("qTt", hb_)], writes=[zk])
                if it["typ"] == "fx":
                    ai = t % 4
                    h = it["h"]
                    P.act(lambda e: e.activation(aa[ai][:, cs], zb[:, cs], AF.Exp, bias=cposTok[:, a, h:h + 1]),
                          reads=[zk, "cposTok"], writes=[("aa", ai)])
                    if r >= 0:
                        P.pool(lambda e: e.tensor_tensor(aa[ai][:, cs], aa[ai][:, cs], mfxw[:, 384:384 + W], ALU.mult),
                               reads=[("aa", ai), "mfx"], writes=[("aa", ai)])
                else:
                    ei, si = t % 2, t % 3
                    P.act(lambda e: e.activation(ee[ei][:, cs], zb[:, cs], AF.Exp), reads=[zk], writes=[("ee", ei)])
                    it["_ln"] = (ei, si)

            def stageA2(t, it):
                if it["typ"] != "sb":
                    return
                ei, si = it["_ln"]
                a = it["a"]
                r, c0, cs, W = geom(it)
                P.act(lambda e: e.activation(spb[si][:, cs], ee[ei][:, cs], AF.Ln, bias=onec[:, 0:1]), reads=[("ee", ei)],
                      writes=[("spb", si)])
                if r >= 0:
                    P.pool(lambda e: e.tensor_tensor(spb[si][:, cs], spb[si][:, cs], msbw[:, 384:384 + W], ALU.mult),
                           reads=[("spb", si), "msb"], writes=[("spb", si)])
                if c0 > 0:
                    P.pool(lambda e: e.memset(spb[si][:, 0:c0], 0.0), reads=[("spb", si)], writes=[("spb", si)])
                if a > 0:
                    nsi = (t + 1) % 3
                    if it["first"]:
                        P.dve(lambda e: e.tensor_copy(ssb[nsi], spb[si]), reads=[("spb", si)], writes=[("ssb", nsi)])
                        P.dve(lambda e: e.tensor_copy(ssum, spb[si]), reads=[("spb", si)], writes=["ssum"])
                    else:
                        P.dve(lambda e: e.tensor_tensor(ssb[nsi], ssum, spb[si], ALU.add),
                              reads=["ssum", ("spb", si)], writes=[("ssb", nsi)])
                        P.dve(lambda e: e.tensor_tensor(ssum, ssum, spb[si], ALU.add),
                              reads=["ssum", ("spb", si)], writes=["ssum"])

            def stageB_pe(t, it):
                if it["typ"] != "sb":
                    return
                r, c0, cs, W = geom(it)
                si = t % 3
                li = t % 3
                lb_, lk = banks[li], ("bank", li)
                prs = [(negtri[:, :], spb[si][:, cs])]
                rd = [("spb", si), "negtri"]
                if not it["first"]:
                    prs.append((negones[:, :], ssb[si][:, cs]))
                    rd += [("ssb", si), "negones"]

                def fnB(e, prs=prs, lb_=lb_):
                    ins = None
                    for i_, (lt, r_) in enumerate(prs):
                        ins = e.matmul(lb_[:, cs], lt, r_, start=False, stop=(i_ == len(prs) - 1), skip_group_check=True)
                    return ins
                P.pe(fnB, reads=rd + [lk], writes=[lk])

            def stageB_act(t, it):
                if it["typ"] != "sb":
                    return
                r, c0, cs, W = geom(it)
                ai = t % 4
                li = t % 3
                lb_, lk = banks[li], ("bank", li)
                P.act(lambda e: e.activation(aa[ai][:, cs], lb_[:, cs], AF.Exp), reads=[lk], writes=[("aa", ai)])
                if r >= 0:
                    P.pool(lambda e: e.tensor_tensor(aa[ai][:, cs], aa[ai][:, cs], msbw[:, 384:384 + W], ALU.mult),
                           reads=[("aa", ai), "msb"], writes=[("aa", ai)])

            def stageC(t, it):
                if it["newhead"] and it["hh"] + 1 < 12:
                    load_head(it["hh"] + 1)
                hb_, a = it["hb"], it["a"]
                r, c0, cs, W = geom(it)
                ai = t % 4
                ob = banks[6 + it["ob"]]
                okey = ("bank", 6 + it["ob"])
                first, last = it["first"], it["last"]
                P.pe(lambda e: e.matmul(ob[0:65, cs], vv[hb_][:, a, :], aa[ai][:, cs], start=first, stop=last,
                                        skip_group_check=True),
                     reads=[("vv", hb_), ("aa", ai)], writes=[okey])
                if not last:
                    return
                hh, B = it["hh"], it["B"]
                qs = slice(B * 512, (B + 1) * 512)
                ob_ = it["ob"]
                if it["typ"] == "fx":
                    P.dve(lambda e: e.reciprocal(rec[64:65, :], ob[64:65, :]), reads=[okey], writes=["rec"])
                    P.dve(lambda e: e.tensor_copy(osb[0:64, :], ob[0:64, :]), reads=[okey], writes=["osb"])
                    mm_group(banks[4][0:64, :], [(ones32[64:65, 0:64], rec[64:65, :])], reads=["rec", "ones32"],
                             writes=[("bank", 4)])
                    P.dve(lambda e: e.tensor_tensor(on[ob_], osb[0:64, :], banks[4][0:64, :], ALU.mult),
                          reads=["osb", ("bank", 4)], writes=[("on", ob_)])
                else:
                    P.dve(lambda e: e.tensor_copy(on[ob_], ob[0:64, :]), reads=[okey], writes=[("on", ob_)])
                fch = 2 + hh // 2
                prow = (hh % 2) * 64
                P.dma("sp", lambda e: e.dma_start(out=mixT[fch, prow:prow + 64, qs], in_=on[ob_]),
                      reads=[("on", ob_)], writes=[("mixT", hh, B)])

            for t in range(n + 3):
                if 0 <= t - 2 < n:
                    stageB_act(t - 2, items[t - 2])
                if t < n:
                    stageA(t, items[t])
                if 0 <= t - 1 < n:
                    stageB_pe(t - 1, items[t - 1])
                if t < n:
                    stageA2(t, items[t])
                if 0 <= t - 3 < n:
                    stageC(t - 3, items[t - 3])

        def phase3a(l, xsrc, wo):
            AR.reset()
            mx = [AR.get([128, 8, 512], BF16) for _ in range(2)]
            xt = [AR.get([128, D], F32) for _ in range(3)]
            it = 0
            for g in range(NG):
                mb = g % 2
                for k in range(8):
                    P.dma("sp" if k % 2 == 0 else "act", lambda e, k=k, g=g, mb=mb: e.dma_start(
                        out=mx[mb][:, k, :], in_=mixT[k, :, g * 512:(g + 1) * 512]), writes=[("mx", mb)])
                for ti in range(4):
                    t0 = g * 512 + ti * 128
                    b = it % 3
                    it += 1
                    P.dma("sp", lambda e, b=b, t0=t0: e.dma_start(out=xt[b], in_=xsrc[t0:t0 + 128, :]),
                          writes=[("xt", b)])
                    for hf in range(2):
                        pb = (it * 2 + hf) % 4
                        mm_group(banks[pb], [(mx[mb][:, k, ti * 128:(ti + 1) * 128], wo[:, k, hf * 512:(hf + 1) * 512])
                                             for k in range(8)],
                                 reads=[("mx", mb), "slotB"], writes=[("bank", pb)])
                        P.dve(lambda e, b=b, hf=hf, pb=pb: e.tensor_tensor(
                            xt[b][:, hf * 512:(hf + 1) * 512], xt[b][:, hf * 512:(hf + 1) * 512], banks[pb], ALU.add),
                            reads=[("xt", b), ("bank", pb)], writes=[("xt", b)])
                    P.dma("act", lambda e, b=b, t0=t0: e.dma_start(out=xa[t0:t0 + 128, :], in_=xt[b]),
                          reads=[("xt", b)], writes=[("xa", t0)])

        def phase3f(l, half, src, dst, w1, w2, wkey):
            AR.reset()
            xt = [AR.get([128, D], F32) for _ in range(4)]
            hT = [AR.get([128, 8, 512], BF16) for _ in range(2)]
            hid = AR.get([128, 16, 512], BF16)
            rl = [AR.get([128, 512], F32) for _ in range(2)]
            if half == 0:
                gbc = AR.get([128, D], F32)
                sqs = AR.get([128, D], BF16)
                hb = [AR.get([128, D], BF16) for _ in range(2)]
                st = [AR.get([128, 4], F32) for _ in range(2)]
                P.dma("sp", lambda e: e.dma_start(out=gbc, in_=n2g[l, :].partition_broadcast(128)), writes=["gbc"])
            it = 0
            for g in range(NG):
                hb_ = g % 2
                hTg = hT[hb_]
                hk = ("hT", hb_)
                for ti in range(4):
                    t0 = g * 512 + ti * 128
                    P.dma("sp", lambda e, ti=ti, t0=t0: e.dma_start(out=xt[ti], in_=src[t0:t0 + 128, :]),
                          writes=[("xt", ti)])
                if half == 0:
                    for ti in range(4):
                        b = it % 2
                        it += 1
                        norm_tile(xt[ti], ("xt", ti), gbc, hb[b], ("hb", b), st[b], ("st", b), sqs, "sqs")
                        transpose_h(hb[b], ("hb", b), hTg, hk, ti * 128, 7, ("bank", 7))
                    for k in range(8):
                        P.dma("act", lambda e, k=k, g=g, hTg=hTg: e.dma_start(
                            out=hTs[k, :, g * 512:(g + 1) * 512], in_=hTg[:, k, :]), reads=[hk], writes=[("hTs", g)])
                else:
                    for k in range(8):
                        P.dma("sp" if k % 2 == 0 else "act", lambda e, k=k, g=g, hTg=hTg: e.dma_start(
                            out=hTg[:, k, :], in_=hTs[k, :, g * 512:(g + 1) * 512]), writes=[hk])
                for f in range(16):
                    pb = f % 2
                    mm_group(banks[pb], [(w1[:, k, f * 128:(f + 1) * 128], hTg[:, k, :]) for k in range(8)],
                             reads=[hk, wkey], writes=[("bank", pb)])
                    P.act(lambda e, pb=pb: e.activation(rl[pb], banks[pb], AF.Relu), reads=[("bank", pb)],
                          writes=[("rl", pb)])
                    if f % 2 == 0:
                        P.dve(lambda e, pb=pb, f=f: e.tensor_tensor(hid[:, f, :], rl[pb], rl[pb], ALU.mult),
                              reads=[("rl", pb)], writes=["hid"])
                    else:
                        P.pool(lambda e, pb=pb, f=f: e.tensor_tensor(hid[:, f, :], rl[pb], rl[pb], ALU.mult),
                               reads=[("rl", pb)], writes=["hid"])
                for ti in range(4):
                    t0 = g * 512 + ti * 128
                    for hf in range(2):
                        pb = 2 + (ti * 2 + hf) % 4
                        mm_group(banks[pb], [(hid[:, f, ti * 128:(ti + 1) * 128], w2[:, f, hf * 512:(hf + 1) * 512])
                                             for f in range(16)],
                                 reads=["hid", wkey], writes=[("bank", pb)])
                        P.dve(lambda e, ti=ti, hf=hf, pb=pb: e.tensor_tensor(
                            xt[ti][:, hf * 512:(hf + 1) * 512], xt[ti][:, hf * 512:(hf + 1) * 512], banks[pb],
                            ALU.add), reads=[("xt", ti), ("bank", pb)], writes=[("xt", ti)])
                    P.dma("act", lambda e, ti=ti, t0=t0: e.dma_start(out=dst[t0:t0 + 128, :], in_=xt[ti]),
                          reads=[("xt", ti)], writes=[("dst", t0)])

        def whole():
            setup_consts()
            if stop == -1:
                return
            wv = load_w_in(0)
            P.barrier()
            if stop == 0:
                return
            for l in range(depth):
                xsrc = x_in if l == 0 else xb
                wo = load_w_out(l)
                phase1a(l, xsrc, wv)
                P.barrier()
                if stop == 1:
                    return
                phase1b(l, wv)
                P.barrier()
                if stop == 2:
                    return
                w1a, w2a = load_ffn_half(l, 0, slotA, "slotA")
                phase2(l)
                P.barrier()
                if stop == 3:
                    return
                if dbg and l == 0:
                    for nm, srcd in (("d_mixT", mixT), ("d_qfx", qfx), ("d_kfx", kfx), ("d_qsb", qsb), ("d_ksb", ksb)):
                        dd = dbg_out[nm]
                        for i in range(dd.shape[0]):
                            P.dma("sp", lambda e, dd=dd, srcd=srcd, i=i: e.dma_start(out=dd[i], in_=srcd[i]))
                phase3a(l, xsrc, wo)
                P.barrier()
                if stop == 4:
                    return
                if dbg and l == 0:
                    for t0 in range(0, T, 128):
                        P.dma("sp", lambda e, t0=t0: e.dma_start(out=dbg_out["d_xa"][t0:t0 + 128, :], in_=xa[t0:t0 + 128, :]))
                w1b, w2b = load_ffn_half(l, 1, slotB, "slotB")
                phase3f(l, 0, xa, xp, w1a, w2a, "slotA")
                P.barrier()
                if stop == 5:
                    return
                if l + 1 < depth:
                    wv = load_w_in(l + 1)
                phase3f(l, 1, xp, (y if l == depth - 1 else xb), w1b, w2b, "slotB")
                P.barrier()
        whole()
        P.barrier()
        stats = P.emit()
    return nc, stats


_CACHE = {}


def _host_layout(inputs, depth):
    f = lambda a: np.ascontiguousarray(np.asarray(a, dtype=np.float32))
    cols = np.zeros((128, depth * 5), np.float32)
    for l in range(depth):
        for i, nm in enumerate(("sb_q_norm_g", "sb_k_norm_g", "fox_q_norm_g", "fox_k_norm_g", "hg_norm_g")):
            cols[:, l * 5 + i] = np.tile(f(inputs[nm])[l], 2)
    lb = f(inputs["lb_logits"])
    lbT = np.zeros((128, 2 * depth), np.float32)
    for j in range(2):
        lbT[:, j * depth:(j + 1) * depth] = lb[:, j * 128:(j + 1) * 128].T
    fb = np.ascontiguousarray(f(inputs["fox_f_bias"]).T)
    return cols, lbT, fb


def run(inputs, T, depth, n_cores, dbg=False, stop=99):
    key = (T, depth, dbg, stop)
    if key not in _CACHE:
        _CACHE[key] = build(T, depth, dbg, stop)
    nc, stats = _CACHE[key]
    f = lambda a: np.ascontiguousarray(np.asarray(a, dtype=np.float32))
    cols, lbT, fb = _host_layout(inputs, depth)
    shared = {
        "w_in": f(inputs["w_in"]), "w_out": f(inputs["w_out"]), "w_ff1": f(inputs["w_ff1"]),
        "w_ff2": f(inputs["w_ff2"]), "norm1_g": f(inputs["norm1_g"]), "norm2_g": f(inputs["norm2_g"]),
        "cols": cols, "lbT": lbT, "fb": fb,
    }
    x = f(inputs["x"])
    in_maps = []
    for c in range(n_cores):
        m = dict(shared)
        m["x"] = np.ascontiguousarray(x[c])
        in_maps.append(m)
    res = run_bass_kernel_spmd(nc, in_maps, core_ids=list(range(n_cores)))
    return res


def kernel(**inputs):
    res = run(inputs, 4096, 4, 8)
    return np.stack([np.asarray(r["y"], dtype=np.float32) for r in res.results], axis=0)
```

```python
import contextlib
import os
import numpy as np
import concourse.bass as bass
import concourse.mybir as mybir
from concourse.bass_utils import run_bass_kernel_spmd

F32 = mybir.dt.float32
BF16 = mybir.dt.bfloat16
AF = mybir.ActivationFunctionType
ALU = mybir.AluOpType
AX = mybir.AxisListType

D = 1024
INC = 3334
DFF = 4096
EPS = 1e-6
OFF = dict(hq=0, hf=256, hi=512, hg=768, sq=1024, sk=1408, sv=1792, fq=2176, fk=2560,
           fv=2944, ff=3328)
N_DMA_SEMS = {"hw": 24, "sw": 8}
GROUP_DMA = True


class Op:
    __slots__ = ("eng", "fn", "waits", "signal", "dma_idx", "seq", "count")

    def __init__(self, eng, fn):
        self.eng = eng
        self.fn = fn
        self.waits = []
        self.signal = False
        self.dma_idx = None
        self.seq = None
        self.count = None


class Prog:
    ENGS = ("pe", "act", "dve", "pool", "sp")

    def __init__(self, nc):
        self.nc = nc
        self.ops = {e: [] for e in self.ENGS}
        self.state = {}
        self.waited = {e: {} for e in self.ENGS}
        self.n_dma = {"hw": 0, "sw": 0}

    def _add(self, eng, fn, reads, writes, dma=False):
        op = Op(eng, fn)
        op.seq = len(self.ops[eng])
        if dma:
            pool = "sw" if eng == "pool" else "hw"
            op.dma_idx = (pool, self.n_dma[pool])
            self.n_dma[pool] += 1
            tok = ("d", pool, op.dma_idx[1])
            if op.dma_idx[1] >= N_DMA_SEMS[pool]:
                self._need(op, ("d", pool, op.dma_idx[1] - N_DMA_SEMS[pool]))
        else:
            tok = ("e", eng, op.seq)
        st = self.state
        xb_ = [k for k in reads if isinstance(k, tuple) and k[0] == "bank" and k not in writes]
        if xb_:
            writes = list(writes) + xb_
        for k in reads:
            s = st.get(k)
            if s is not None:
                for w in s[0]:
                    self._need(op, w)
                s[2] = False
        for k in writes:
            s = st.get(k)
            if s is None:
                continue
            if dma and s[2] and GROUP_DMA:
                for w in s[3]:
                    self._need(op, w)
                continue
            for w in s[0]:
                self._need(op, w)
            for rt in s[1].values():
                self._need(op, rt)
        for k in reads:
            s = st.setdefault(k, [[], {}, False, []])
            if dma:
                s[1][tok] = tok
            else:
                s[1][eng] = tok
        for k in writes:
            s = st.get(k)
            if dma and s is not None and s[2] and GROUP_DMA:
                s[0].append(tok)
            else:
                pre = (list(s[0]) + list(s[1].values())) if s is not None else []
                st[k] = [[tok], {}, bool(dma), pre]
        self.ops[eng].append(op)
        return op

    def _need(self, op, tok):
        if tok[0] == "e":
            key, val = tok[1], tok[2]
        else:
            key, val = ("d", tok[1], tok[2] % N_DMA_SEMS[tok[1]]), tok[2]
        wd = self.waited[op.eng]
        if wd.get(key, -1) >= val:
            return
        wd[key] = val
        op.waits.append(tok)
        if tok[0] == "e":
            self.ops[tok[1]][tok[2]].signal = True

    def pe(self, fn, reads=(), writes=()):
        return self._add("pe", fn, reads, writes)

    def act(self, fn, reads=(), writes=()):
        return self._add("act", fn, reads, writes)

    def dve(self, fn, reads=(), writes=()):
        return self._add("dve", fn, reads, writes)

    def pool(self, fn, reads=(), writes=()):
        return self._add("pool", fn, reads, writes)

    def dma(self, q, fn, reads=(), writes=()):
        return self._add(q, fn, reads, writes, dma=True)

    def barrier(self):
        last = {}
        for e in self.ENGS:
            for op in reversed(self.ops[e]):
                if op.dma_idx is None and op.fn is not None:
                    last[e] = ("e", e, op.seq)
                    break
        ndma = dict(self.n_dma)
        for e in self.ENGS:
            op = Op(e, None)
            op.seq = len(self.ops[e])
            for e2, tok in last.items():
                if e2 != e:
                    self._need(op, tok)
            for pool in ("hw", "sw"):
                for i in range(max(0, ndma[pool] - N_DMA_SEMS[pool]), ndma[pool]):
                    self._need(op, ("d", pool, i))
            self.ops[e].append(op)
        self.state = {}

    def emit(self):
        nc = self.nc
        for e in self.ENGS:
            c = 0
            for op in self.ops[e]:
                if op.dma_idx is None and op.signal:
                    c += 1
                op.count = c
        stats = {}
        with contextlib.ExitStack() as es:
            esem = {e: es.enter_context(nc.semaphore("s_" + e)) for e in self.ENGS}
            dsem = {p: [es.enter_context(nc.semaphore("d%s_%d" % (p, i))) for i in range(N_DMA_SEMS[p])]
                    for p in ("hw", "sw")}
            block = es.enter_context(nc.Block())
            engobj = {"pe": block.tensor, "act": block.scalar, "dve": block.vector,
                      "pool": block.gpsimd, "sp": block.sync}
            allops = self.ops
            for e in self.ENGS:
                def body(eng, e=e):
                    nw = 0
                    for op in allops[e]:
                        for tok in op.waits:
                            if tok[0] == "e":
                                eng.wait_ge(esem[tok[1]], allops[tok[1]][tok[2]].count)
                            else:
                                pl, i = tok[1], tok[2]
                                eng.wait_ge(dsem[pl][i % N_DMA_SEMS[pl]], 16 * (i // N_DMA_SEMS[pl] + 1))
                            nw += 1
                        if op.fn is None:
                            continue
                        ins = op.fn(eng)
                        if op.dma_idx is not None:
                            pl, i = op.dma_idx
                            ins.then_inc(dsem[pl][i % N_DMA_SEMS[pl]], 16)
                        elif op.signal:
                            ins.then_inc(esem[e], 1)
                    stats[e] = (len(allops[e]), nw, allops[e][-1].count if allops[e] else 0)
                engobj[e](body)
        return stats


def build(T, depth, dbg=False, stop=99):
    nc = bass.Bass("TRN2", target_bir_lowering=False)
    NT = T // 128
    NG = T // 512
    P = Prog(nc)

    def din(name, shape, dt=F32):
        return nc.dram_tensor(name, list(shape), dt, kind="ExternalInput").ap()

    def dscr(name, shape, dt):
        return nc.dram_tensor(name, list(shape), dt, kind="Internal").ap()

    x_in = din("x", [T, D])
    w_in = din("w_in", [depth, D, INC])
    w_out = din("w_out", [depth, D, D])
    w_ff1 = din("w_ff1", [depth, D, DFF])
    w_ff2 = din("w_ff2", [depth, DFF, D])
    n1g = din("norm1_g", [depth, D])
    n2g = din("norm2_g", [depth, D])
    cols_in = din("cols", [128, depth * 5])
    lbT_in = din("lbT", [128, 2 * depth])
    fb_in = din("fb", [6, depth])
    y = nc.dram_tensor("y", [T, D], F32, kind="ExternalOutput").ap()

    xa = dscr("xa", [T, D], F32)
    xp = dscr("xp", [T, D], F32)
    xb = dscr("xb", [T, D], F32)
    hTs = dscr("hTs", [8, 128, T], BF16)
    qsb = dscr("qsb", [6, 64, T], BF16)
    ksb = dscr("ksb", [6, 64, T], BF16)
    qfx = dscr("qfx", [6, 67, T], BF16)
    kfx = dscr("kfx", [6, 67, T], BF16)
    vsc = dscr("vsc", [T, 12 * 65], BF16)
    mixT = dscr("mixT", [8, 128, T], BF16)
    dbg_out = {}
    if dbg:
        for nm, shp, dt in (("d_mixT", [8, 128, T], BF16), ("d_xa", [T, D], F32),
                            ("d_qfx", [6, 67, T], BF16), ("d_kfx", [6, 67, T], BF16),
                            ("d_qsb", [6, 64, T], BF16), ("d_ksb", [6, 64, T], BF16)):
            dbg_out[nm] = nc.dram_tensor(nm, shp, dt, kind="ExternalOutput").ap()

    es = contextlib.ExitStack()
    with es:
        def sb(name, shape, dt):
            return es.enter_context(nc.sbuf_tensor("sb_" + name, list(shape), dt))

        def ps(name, shape, dt=F32):
            return es.enter_context(nc.psum_tensor("ps_" + name, list(shape), dt))

        slotA = sb("slotA", [128, 32768], BF16)
        slotB = sb("slotB", [128, 32768], BF16)
        ARENA = 63 * 1024
        arena = sb("arena", [128, ARENA], mybir.dt.uint8)
        ident = sb("ident", [128, 128], BF16)
        ident32 = sb("ident32", [128, 128], F32)
        blk = sb("blk", [128, 128], BF16)
        negtri = sb("negtri", [128, 128], BF16)
        negones = sb("negones", [128, 128], BF16)
        msbw = sb("msbw", [128, 896], BF16)
        mfxw = sb("mfxw", [128, 896], BF16)
        hmask = sb("hmask", [128, 4, 128], BF16)
        rmask = sb("rmask", [128, 512], BF16)
        ones32 = sb("ones32", [128, 64], F32)
        onesb = sb("onesb", [128, 512], BF16)
        cols = sb("cols", [128, depth * 5], F32)
        colq = sb("colq", [128, depth * 2], F32)
        lbx = sb("lbx", [128, 2 * depth], F32)
        lbc = sb("lbc", [128, 2 * depth], F32)
        omlb = sb("omlb", [128, 2 * depth], F32)
        lbs = sb("lbs", [128, 2], F32)
        fbc = sb("fbc", [6, depth], F32)
        nfb = sb("nfb", [6, depth], F32)
        cposTok = sb("cposTok", [128, NT, 6], F32)
        epsc = sb("epsc", [128, 1], F32)
        onec = sb("onec", [128, 1], F32)

        banks = [ps("bank%d" % i, [128, 512], F32)[:, :] for i in range(8)]

        class Arena:
            def __init__(self):
                self.off = 0

            def reset(self):
                self.off = 0

            def get(self, shape, dt):
                nb = int(np.prod(shape[1:])) * (4 if dt == F32 else 2)
                nbytes = (nb + 31) // 32 * 32
                assert self.off + nbytes <= ARENA, ("arena overflow", self.off, nbytes)
                ap = arena[0:shape[0], self.off:self.off + nb].bitcast(dt)
                self.off += nbytes
                if len(shape) == 3:
                    ap = ap.rearrange("p (a b) -> p a b", b=shape[2])
                return ap
        AR = Arena()

        def bankbf(i):
            return banks[i][:, :].bitcast(BF16)

        def setup_consts():
            P.pool(lambda e: e.memset(ident[:, :], 1.0), writes=["ident"])
            P.pool(lambda e: e.affine_select(ident[:, :], ident[:, :], [[1, 128]], ALU.is_equal, 0.0,
                                             base=0, channel_multiplier=-1),
                   reads=["ident"], writes=["ident"])
            P.pool(lambda e: e.memset(ident32[:, :], 1.0), writes=["ident32"])
            P.pool(lambda e: e.affine_select(ident32[:, :], ident32[:, :], [[1, 128]], ALU.is_equal,
                                             0.0, base=0, channel_multiplier=-1),
                   reads=["ident32"], writes=["ident32"])
            P.pool(lambda e: e.memset(blk[:, :], 0.0), writes=["blk"])
            P.pool(lambda e: e.memset(blk[0:64, 0:64], 1.0 / 64), reads=["blk"], writes=["blk"])
            P.pool(lambda e: e.memset(blk[64:128, 64:128], 1.0 / 64), reads=["blk"], writes=["blk"])
            P.pool(lambda e: e.memset(negtri[:, :], -1.0), writes=["negtri"])
            P.pool(lambda e: e.affine_select(negtri[:, :], negtri[:, :], [[-1, 128]], ALU.is_ge, 0.0,
                                             base=0, channel_multiplier=1),
                   reads=["negtri"], writes=["negtri"])
            P.pool(lambda e: e.memset(negones[:, :], -1.0), writes=["negones"])
            P.pool(lambda e: e.memset(onesb[:, :], 1.0), writes=["onesb"])
            P.pool(lambda e: e.memset(ones32[:, :], 1.0), writes=["ones32"])
            P.pool(lambda e: e.memset(epsc[:, :], EPS), writes=["epsc"])
            P.pool(lambda e: e.memset(onec[:, :], 1.0), writes=["onec"])
            P.pool(lambda e: e.memset(msbw[:, :], 1.0), writes=["msb"])
            P.pool(lambda e: e.affine_select(msbw[:, :], msbw[:, :], [[1, 896]], ALU.is_gt,
                                             0.0, base=-384, channel_multiplier=-1),
                   reads=["msb"], writes=["msb"])
            P.pool(lambda e: e.memset(mfxw[:, :], 1.0), writes=["mfx"])
            P.pool(lambda e: e.affine_select(mfxw[:, :], mfxw[:, :], [[1, 896]], ALU.is_ge,
                                             0.0, base=-384, channel_multiplier=-1),
                   reads=["mfx"], writes=["mfx"])
            P.pool(lambda e: e.memset(hmask[:, :, :], 0.0), writes=["hmask"])
            for c in range(2):
                P.pool(lambda e, c=c: e.memset(hmask[c * 64:(c + 1) * 64, :, c * 64:(c + 1) * 64], 1.0),
                       reads=["hmask"], writes=["hmask"])
            P.pool(lambda e: e.affine_select(hmask[:, :, :], hmask[:, :, :], [[0, 4], [1, 128]],
                                             ALU.is_ge, 0.0, base=0, channel_multiplier=-1),
                   reads=["hmask"], writes=["hmask"])
            P.pool(lambda e: e.memset(rmask[:, :], 1.0), writes=["rmask"])
            P.pool(lambda e: e.memset(rmask[:, :].rearrange("p (c t) -> p c t", t=64)[:, :, 0:1], 0.0),
                   reads=["rmask"], writes=["rmask"])
            P.dma("sp", lambda e: e.dma_start(out=cols[:, :], in_=cols_in[:, :]), writes=["cols"])
            P.dma("sp", lambda e: e.dma_start(out=lbx[:, :], in_=lbT_in[:, :]), writes=["lbx"])
            P.dma("sp", lambda e: e.dma_start(out=fbc[:, :], in_=fb_in[:, :]), writes=["fbc"])
            P.dve(lambda e: e.tensor_scalar(nfb[:, :], fbc[:, :], -1.0, None, ALU.mult),
                  reads=["fbc"], writes=["nfb"])
            for l in range(depth):
                for i, c in enumerate((0, 2)):
                    P.dve(lambda e, l=l, i=i, c=c: e.tensor_scalar(
                        colq[:, l * 2 + i:l * 2 + i + 1], cols[:, l * 5 + c:l * 5 + c + 1],
                        0.125, None, ALU.mult), reads=["cols"], writes=["colq"])
            P.act(lambda e: e.activation(lbx[:, :], lbx[:, :], AF.Exp), reads=["lbx"], writes=["lbx"])
            lb3 = lbx[:, :].rearrange("p (j l) -> p j l", l=depth)
            P.dve(lambda e: e.tensor_reduce(lbs[:, :], lb3, AX.X, ALU.add), reads=["lbx"], writes=["lbs"])
            P.dve(lambda e: e.reciprocal(lbs[:, :], lbs[:, :]), reads=["lbs"], writes=["lbs"])
            for j in range(2):
                P.dve(lambda e, j=j: e.tensor_scalar(lbx[:, j * depth:(j + 1) * depth],
                                                     lbx[:, j * depth:(j + 1) * depth],
                                                     lbs[:, j:j + 1], None, ALU.mult),
                      reads=["lbx", "lbs"], writes=["lbx"])
                P.dve(lambda e, j=j: e.memset(lbc[:, j * depth:j * depth + 1], 0.0),
                      reads=["lbc"], writes=["lbc"])
                for l in range(1, depth):
                    P.dve(lambda e, j=j, l=l: e.tensor_tensor(
                        lbc[:, j * depth + l:j * depth + l + 1], lbc[:, j * depth + l - 1:j * depth + l],
                        lbx[:, j * depth + l:j * depth + l + 1], ALU.add),
                        reads=["lbc", "lbx"], writes=["lbc"])
            P.dve(lambda e: e.tensor_scalar(omlb[:, :], lbc[:, :], -1.0, 1.0, ALU.mult, ALU.add),
                  reads=["lbc"], writes=["omlb"])
            for h in range(6):
                for t0 in range(0, T, 512):
                    P.dma("sp", lambda e, h=h, t0=t0: e.dma_start(out=kfx[h, 64:67, t0:t0 + 512],
                                                                   in_=onesb[0:3, :]),
                          reads=["onesb"], writes=[("kfx1", h)])

        def load_w_in(l):
            v = slotA[:, 0:8 * INC].rearrange("p (k n) -> p k n", n=INC)
            for k in range(8):
                for hh in range(2):
                    c0 = hh * 1667
                    P.dma("pool", lambda e, k=k, c0=c0: e.dma_start(
                        out=v[:, k, c0:c0 + 1667], in_=w_in[l, k * 128:(k + 1) * 128, c0:c0 + 1667]),
                        writes=["slotA"])
            return v

        def load_w_out(l):
            v = slotB[:, 0:8 * D].rearrange("p (k n) -> p k n", n=D)
            for k in range(8):
                P.dma("pool", lambda e, k=k: e.dma_start(
                    out=v[:, k, :], in_=w_out[l, k * 128:(k + 1) * 128, :]), writes=["slotB"])
            return v

        def load_ffn_half(l, hf, slot, key):
            w1 = slot[:, 0:16384].rearrange("p (k n) -> p k n", n=2048)
            w2 = slot[:, 16384:32768].rearrange("p (k n) -> p k n", n=1024)
            for k in range(8):
                for hh in range(2):
                    c0 = hf * 2048 + hh * 1024
                    P.dma("pool", lambda e, k=k, c0=c0, hh=hh: e.dma_start(
                        out=w1[:, k, hh * 1024:(hh + 1) * 1024],
                        in_=w_ff1[l, k * 128:(k + 1) * 128, c0:c0 + 1024]), writes=[key])
            for fc in range(16):
                r0 = hf * 2048 + fc * 128
                P.dma("pool", lambda e, fc=fc, r0=r0: e.dma_start(
                    out=w2[:, fc, :], in_=w_ff2[l, r0:r0 + 128, :]), writes=[key])
            return w1, w2

        def mm_group(out, pairs, reads, writes):
            def fn(e):
                n = len(pairs)
                ins = None
                for i, (lt, r) in enumerate(pairs):
                    ins = e.matmul(out, lt, r, start=(i == 0), stop=(i == n - 1))
                return ins
            P.pe(fn, reads, writes)

        def norm_tile(xt, xkey, gbc, hb, hkey, st, stkey, sq_scr, sqkey):
            P.act(lambda e: e.activation(sq_scr, xt, AF.Square, accum_out=st[:, 0:1]),
                  reads=[xkey], writes=[sqkey, stkey])
            P.act(lambda e: e.activation(st[:, 1:2], st[:, 0:1], AF.Ln, bias=epsc[:, 0:1], scale=1.0 / D),
                  reads=[stkey], writes=[stkey])
            P.act(lambda e: e.activation(st[:, 2:3], st[:, 1:2], AF.Exp, scale=-0.5),
                  reads=[stkey], writes=[stkey])
            P.dve(lambda e: e.scalar_tensor_tensor(hb, xt, st[:, 2:3], gbc, ALU.mult, ALU.mult),
                  reads=[xkey, stkey, "gbc"], writes=[hkey])

        def transpose_h(hb, hkey, hT, hTkey, tcol, pbank, pkey):
            for half in range(2):
                pv = bankbf(pbank)

                def fn(e, half=half, pv=pv):
                    ins = None
                    for kk in range(4):
                        k = half * 4 + kk
                        ins = e.transpose(pv[:, kk * 128:(kk + 1) * 128], hb[:, k * 128:(k + 1) * 128],
                                          ident[:, :])
                    return ins
                P.pe(fn, reads=[hkey, "ident"], writes=[pkey])
                src = pv[:, 0:512].rearrange("p (k t) -> p k t", t=128)
                dst = hT[:, half * 4:(half + 1) * 4, tcol:tcol + 128]
                if half == 0:
                    P.act(lambda e, src=src, dst=dst: e.activation(dst, src, AF.Copy),
                          reads=[pkey], writes=[hTkey])
                else:
                    P.dve(lambda e, src=src, dst=dst: e.tensor_copy(dst, src),
                          reads=[pkey], writes=[hTkey])

        def headnorm_a(psrc, pskey, sq, sqkey):
            P.act(lambda e: e.activation(sq, psrc, AF.Square), reads=[pskey], writes=[sqkey])

        def headnorm_b(psrc, pskey, gcol, gkey, bank_ms, mskey, sq, sqkey, rs, rskey, extra_mul, emkey,
                       dst, dstkey):
            mm_group(bank_ms, [(blk[:, :], sq)], reads=[sqkey, "blk"], writes=[mskey])
            P.act(lambda e: e.activation(rs, bank_ms, AF.Ln, bias=epsc[:, 0:1]),
                  reads=[mskey], writes=[rskey])
            P.act(lambda e: e.activation(rs, rs, AF.Exp, scale=-0.5), reads=[rskey], writes=[rskey])
            if extra_mul is None:
                P.dve(lambda e: e.scalar_tensor_tensor(dst, psrc, gcol, rs, ALU.mult, ALU.mult),
                      reads=[pskey, rskey, gkey], writes=[dstkey])
            else:
                P.dve(lambda e: e.scalar_tensor_tensor(rs, psrc, gcol, rs, ALU.mult, ALU.mult),
                      reads=[pskey, rskey, gkey], writes=[rskey])
                P.pool(lambda e: e.tensor_tensor(dst, rs, extra_mul, ALU.mult),
                       reads=[rskey, emkey], writes=[dstkey])

        def headnorm(psrc, pskey, gcol, gkey, bank_ms, mskey, sq, sqkey, rs, rskey, extra_mul, emkey,
                     dst, dstkey):
            headnorm_a(psrc, pskey, sq, sqkey)
            headnorm_b(psrc, pskey, gcol, gkey, bank_ms, mskey, sq, sqkey, rs, rskey, extra_mul, emkey, dst, dstkey)

        def phase1a(l, xsrc, wv):
            AR.reset()
            gbc = AR.get([128, D], F32)
            xt = [AR.get([128, D], F32) for _ in range(2)]
            sqs = AR.get([128, D], BF16)
            hb = [AR.get([128, D], BF16) for _ in range(2)]
            st = [AR.get([128, 4], F32) for _ in range(2)]
            hT = [AR.get([128, 8, 512], BF16) for _ in range(2)]
            sq = [AR.get([128, 512], BF16) for _ in range(2)]
            rs = [AR.get([128, 512], F32) for _ in range(2)]
            qn = [AR.get([128, 512], BF16) for _ in range(2)]
            vaug = [AR.get([128, 12, 65], BF16) for _ in range(2)]
            spt = AR.get([6, 512], F32)
            cpos = [AR.get([6, 512], F32) for _ in range(2)]
            rr = AR.get([6, 512], F32)
            qh = [AR.get([6, 512], BF16) for _ in range(3)]
            P.dma("sp", lambda e: e.dma_start(out=gbc, in_=n1g[l, :].partition_broadcast(128)),
                  writes=["gbc"])
            for i in range(2):
                P.pool(lambda e, i=i: e.memset(vaug[i][:, :, 64:65], 1.0), writes=[("vaug", i)])
            itc = [0]

            pbuf = {}

            def prep_norm(g, ti):
                t0 = g * 512 + ti * 128
                b = itc[0] % 2
                itc[0] += 1
                pbuf[(g, ti)] = b
                P.dma("sp", lambda e, b=b, t0=t0: e.dma_start(out=xt[b], in_=xsrc[t0:t0 + 128, :]),
                      writes=[("xt", b)])
                norm_tile(xt[b], ("xt", b), gbc, hb[b], ("hb", b), st[b], ("st", b), sqs, "sqs")

            def prep_T(g, ti):
                b = pbuf[(g, ti)]
                transpose_h(hb[b], ("hb", b), hT[g % 2], ("hT", g % 2), ti * 128, 7, ("bank", 7))

            def prep_tile(g, ti):
                prep_norm(g, ti)
                prep_T(g, ti)

            for ti in range(4):
                prep_tile(0, ti)
            for g in range(NG):
                hb_ = g % 2
                hTg = hT[hb_]
                hk = ("hT", hb_)
                for k in range(8):
                    P.dma("act", lambda e, k=k, g=g, hTg=hTg: e.dma_start(
                        out=hTs[k, :, g * 512:(g + 1) * 512], in_=hTg[:, k, :]),
                        reads=[hk], writes=[("hTs", g)])
                ci = 0
                pend = None

                def tail(args):
                    (bank, pb, gcol, gkey, typ, qk, c, dst_d) = args
                    headnorm_b(bank, ("bank", pb), gcol, gkey, banks[2 + pb], ("bank", 2 + pb),
                               sq[pb], ("sq", pb), rs[pb], ("rs", pb), None, None, qn[pb], ("qn", pb))
                    for hp in range(2):
                        P.dma("sp", lambda e, pb=pb, hp=hp, c=c, g=g, dst_d=dst_d: e.dma_start(
                            out=dst_d[2 * c + hp, 0:64, g * 512:(g + 1) * 512],
                            in_=qn[pb][hp * 64:(hp + 1) * 64, :]),
                            reads=[("qn", pb)], writes=[("qkd", typ, qk, 2 * c + hp, g)])

                for typ, qk, off, dst_d, gci in (("sb", "q", OFF["sq"], qsb, None),
                                                 ("sb", "k", OFF["sk"], ksb, 1),
                                                 ("fx", "q", OFF["fq"], qfx, None),
                                                 ("fx", "k", OFF["fk"], kfx, 3)):
                    for c in range(3):
                        pb = ci % 2
                        ci += 1
                        bank = banks[pb]
                        c0 = off + c * 128
                        mm_group(bank, [(wv[:, k, c0:c0 + 128], hTg[:, k, :]) for k in range(8)],
                                 reads=[hk, "slotA"], writes=[("bank", pb)])
                        if qk == "q":
                            qi = 0 if typ == "sb" else 1
                            gcol = colq[:, l * 2 + qi:l * 2 + qi + 1]
                            gkey = "colq"
                        else:
                            gcol = cols[:, l * 5 + gci:l * 5 + gci + 1]
                            gkey = "cols"
                        headnorm_a(bank, ("bank", pb), sq[pb], ("sq", pb))
                        if pend is not None:
                            tail(pend)
                        pend = (bank, pb, gcol, gkey, typ, qk, c, dst_d)
                        if g + 1 < NG:
                            if ci % 3 == 2:
                                prep_norm(g + 1, ci // 3)
                            if ci % 3 == 0:
                                prep_T(g + 1, ci // 3 - 1)
                tail(pend)
                for ti in range(4):
                    vb = (g * 4 + ti) % 2
                    for vi, off in enumerate((OFF["sv"], OFF["fv"])):
                        bank = banks[4 + vi]
                        mm_group(bank[:, 0:384],
                                 [(hTg[:, k, ti * 128:(ti + 1) * 128], wv[:, k, off:off + 384])
                                  for k in range(8)],
                                 reads=[hk, "slotA"], writes=[("bank", 4 + vi)])
                        src = bank[:, 0:384].rearrange("p (h d) -> p h d", d=64)
                        dst = vaug[vb][:, vi * 6:(vi + 1) * 6, 0:64]
                        if vi == 0:
                            P.act(lambda e, src=src, dst=dst: e.activation(dst, src, AF.Copy),
                                  reads=[("bank", 4 + vi)], writes=[("vaug", vb)])
                        else:
                            P.dve(lambda e, src=src, dst=dst: e.tensor_copy(dst, src),
                                  reads=[("bank", 4 + vi)], writes=[("vaug", vb)])
                    t0 = g * 512 + ti * 128
                    P.dma("sp", lambda e, vb=vb, t0=t0: e.dma_start(
                        out=vsc[t0:t0 + 128, :].rearrange("p (h d) -> p h d", d=65), in_=vaug[vb]),
                        reads=[("vaug", vb)], writes=[("vsc", t0)])
                if os.environ.get("SKIPC") == "1":
                    continue
                mm_group(banks[6][0:6, :], [(wv[:, k, OFF["ff"]:OFF["ff"] + 6], hTg[:, k, :]) for k in range(8)],
                         reads=[hk, "slotA"], writes=[("bank", 6)])
                P.act(lambda e: e.activation(spt, banks[6][0:6, :], AF.Exp, bias=nfb[:, l:l + 1], scale=-1.0),
                      reads=[("bank", 6), "nfb"], writes=["spt"])
                P.act(lambda e: e.activation(spt, spt, AF.Ln, bias=onec[0:6, 0:1]), reads=["spt"], writes=["spt"])
                cb = g % 2
                if g == 0:
                    P.dve(lambda e, cb=cb: e.tensor_tensor_scan(cpos[cb], onesb[0:6, :], spt, 0.0,
                                                                ALU.mult, ALU.add),
                          reads=["spt", "onesb"], writes=[("cpos", cb)])
                else:
                    P.dve(lambda e, cb=cb: e.tensor_tensor_scan(cpos[cb], onesb[0:6, :], spt,
                                                                cpos[1 - cb][:, 511:512], ALU.mult, ALU.add),
                          reads=["spt", "onesb", ("cpos", 1 - cb)], writes=[("cpos", cb)])
                P.dve(lambda e, cb=cb: e.tensor_scalar(qh[0], cpos[cb], -1.0, None, ALU.mult),
                      reads=[("cpos", cb)], writes=["qh0"])
                P.dve(lambda e, cb=cb: e.scalar_tensor_tensor(rr, cpos[cb], -1.0, qh[0], ALU.mult, ALU.subtract),
                      reads=[("cpos", cb), "qh0"], writes=["rr"])
                P.dve(lambda e: e.tensor_copy(qh[1], rr), reads=["rr"], writes=["qh1"])
                P.dve(lambda e: e.tensor_tensor(qh[2], rr, qh[1], ALU.subtract), reads=["rr", "qh1"],
                      writes=["qh2"])
                for i in range(3):
                    P.dma("sp", lambda e, i=i, g=g: e.dma_start(out=qfx[:, 64 + i, g * 512:(g + 1) * 512],
                                                                 in_=qh[i]),
                          reads=["qh%d" % i], writes=[("qfxc", i, g)])
                if os.environ.get("SKIPC3") == "1":
                    continue
                for ti in range(4):
                    P.pe(lambda e, ti=ti, cb=cb: e.matmul(banks[6][:, 0:6], cpos[cb][:, ti * 128:(ti + 1) * 128],
                                                         ident32[0:6, 0:6], start=True, stop=True),
                         reads=[("cpos", cb), "ident32"], writes=[("bank", 6)])
                    P.dve(lambda e, ti=ti, g=g: e.tensor_copy(cposTok[:, g * 4 + ti, :], banks[6][:, 0:6]),
                          reads=[("bank", 6)], writes=["cposTok"])

        def phase1b(l, wv):
            AR.reset()
            hT = [AR.get([128, 8, 512], BF16) for _ in range(2)]
            t1 = [AR.get([128, 512], F32) for _ in range(2)]
            t2 = AR.get([128, 512], F32)
            t3 = AR.get([128, 512], F32)
            t4 = [AR.get([128, 512], F32) for _ in range(2)]
            t5 = AR.get([128, 512], F32)
            gs = [AR.get([128, 512], F32) for _ in range(2)]
            qT = AR.get([128, 2, 512], BF16)
            kTz = [[AR.get([128, 512], BF16) for _ in range(2)] for _ in range(2)]
            khT = AR.get([128, 2, 512], BF16)
            khat = AR.get([128, 4, 256], BF16)
            vt = AR.get([128, 4, 256], BF16)
            vtz = [AR.get([128, 4, 256], BF16) for _ in range(2)]
            atb = [AR.get([128, 4, 128], BF16) for _ in range(2)]
            S32 = [AR.get([128, 2, 64], F32) for _ in range(2)]
            Sbz = [[AR.get([128, 2, 64], BF16) for _ in range(2)] for _ in range(4)]
            sq = AR.get([128, 512], BF16)
            rs = AR.get([128, 512], F32)
            mo = [AR.get([128, 512], BF16) for _ in range(2)]
            for j in range(2):
                for hp in range(2):
                    P.pool(lambda e, j=j, hp=hp: e.memset(kTz[j][hp], 0.0), writes=[("kTz", j)])
            for c in range(2):
                P.pool(lambda e, c=c: e.memset(vtz[c], 0.0), writes=["vtz"])
            for r in range(4):
                for hp in range(2):
                    P.pool(lambda e, r=r, hp=hp: e.memset(Sbz[r][hp], 0.0), writes=[("Sbz", r)])
            P.pool(lambda e: e.memset(S32[0], 0.0), writes=[("S32", 0)])
            sidx = 0
            ring = 0
            for g in range(NG):
                hb_ = g % 2
                hTg = hT[hb_]
                hk = ("hT", hb_)
                for k in range(8):
                    P.dma("sp" if k % 2 == 0 else "act", lambda e, k=k, g=g, hTg=hTg: e.dma_start(
                        out=hTg[:, k, :], in_=hTs[k, :, g * 512:(g + 1) * 512]), writes=[hk])
                for j in range(2):
                    bf_, bq_, bg_ = banks[0], banks[1], banks[2]
                    mm_group(bf_, [(wv[:, k, OFF["hf"] + j * 128:OFF["hf"] + (j + 1) * 128], hTg[:, k, :])
                                   for k in range(8)], reads=[hk, "slotA"], writes=[("bank", 0)])
                    mm_group(bq_, [(wv[:, k, OFF["hq"] + j * 128:OFF["hq"] + (j + 1) * 128], hTg[:, k, :])
                                   for k in range(8)], reads=[hk, "slotA"], writes=[("bank", 1)])
                    mm_group(bg_, [(wv[:, k, OFF["hg"] + j * 128:OFF["hg"] + (j + 1) * 128], hTg[:, k, :])
                                   for k in range(8)], reads=[hk, "slotA"], writes=[("bank", 2)])
                    a = t1[j]
                    ak = ("t1", j)
                    P.act(lambda e, a=a: e.activation(a, bf_, AF.Exp, scale=-1.0), reads=[("bank", 0)], writes=[ak])
                    P.dve(lambda e, a=a: e.tensor_scalar(a, a, 1.0, None, ALU.add), reads=[ak], writes=[ak])
                    P.dve(lambda e, a=a: e.reciprocal(a, a), reads=[ak], writes=[ak])
                    ci = j * depth + l
                    P.dve(lambda e, a=a, ci=ci: e.tensor_scalar(a, a, omlb[:, ci:ci + 1], lbc[:, ci:ci + 1],
                                                                ALU.mult, ALU.add),
                          reads=[ak, "omlb", "lbc"], writes=[ak])
                    P.act(lambda e, a=a: e.activation(t2, a, AF.Ln), reads=[ak], writes=["t2"])
                    P.dve(lambda e: e.tensor_tensor_scan(t3, rmask[:, :], t2, 0.0, ALU.mult, ALU.add),
                          reads=["t2", "rmask"], writes=["t3"])
                    e4 = t4[j]
                    P.act(lambda e, e4=e4: e.activation(e4, t3, AF.Exp), reads=["t3"], writes=[("t4", j)])
                    P.act(lambda e: e.activation(t5, t3, AF.Exp, scale=-1.0), reads=["t3"], writes=["t5"])
                    P.dve(lambda e, j=j, e4=e4: e.tensor_tensor(qT[:, j, :], bq_, e4, ALU.mult),
                          reads=[("bank", 1), ("t4", j)], writes=[("qT", j)])
                    P.pool(lambda e, a=a: e.tensor_scalar(a, a, -1.0, 1.0, ALU.mult, ALU.add),
                           reads=[ak], writes=[ak])
                    P.pool(lambda e, a=a: e.tensor_tensor(t5, a, t5, ALU.mult), reads=[ak, "t5"], writes=["t5"])
                    for hp in range(2):
                        sl = slice(hp * 64, (hp + 1) * 64)
                        P.pool(lambda e, j=j, hp=hp, sl=sl: e.tensor_copy(kTz[j][hp][sl, :], t5[sl, :]),
                               reads=["t5"], writes=[("kTz", j)])
                    e4v = e4.rearrange("p (c t) -> p c t", t=64)[:, :, 63:64].to_broadcast([128, 8, 64])
                    P.dve(lambda e, j=j, e4v=e4v: e.tensor_tensor(
                        khT[:, j, :].rearrange("p (c t) -> p c t", t=64),
                        t5[:, :].rearrange("p (c t) -> p c t", t=64), e4v, ALU.mult),
                        reads=["t5", ("t4", j)], writes=[("khT", j)])
                    gg = gs[j]
                    gk = ("gs", j)
                    P.act(lambda e, gg=gg: e.activation(gg, bg_, AF.Exp, scale=-1.0), reads=[("bank", 2)], writes=[gk])
                    P.dve(lambda e, gg=gg: e.tensor_scalar(gg, gg, 1.0, None, ALU.add), reads=[gk], writes=[gk])
                    P.dve(lambda e, gg=gg: e.reciprocal(gg, gg), reads=[gk], writes=[gk])
                    P.dve(lambda e, gg=gg: e.tensor_tensor(gg, bg_, gg, ALU.mult), reads=[("bank", 2), gk], writes=[gk])
                B1 = int(os.environ.get("B1", "9"))
                if B1 <= 1:
                    continue
                for ti in range(4):
                    tsl = slice(ti * 128, (ti + 1) * 128)
                    mm_group(banks[3][:, 0:256],
                             [(hTg[:, k, tsl], wv[:, k, OFF["hi"]:OFF["hi"] + 256]) for k in range(8)],
                             reads=[hk, "slotA"], writes=[("bank", 3)])
                    P.act(lambda e, ti=ti: e.activation(vt[:, ti, :], banks[3][:, 0:256], AF.Copy),
                          reads=[("bank", 3)], writes=["vt"])
                    for c in range(2):
                        sl = slice(c * 64, (c + 1) * 64)
                        P.dve(lambda e, ti=ti, c=c, sl=sl: e.tensor_copy(vtz[c][sl, ti, :], banks[3][sl, 0:256]),
                              reads=[("bank", 3)], writes=["vtz"])
                    pv = bankbf(3)

                    def fnT(e, tsl=tsl, pv=pv):
                        ins = None
                        for j in range(2):
                            ins = e.transpose(pv[:, 512 + j * 128:512 + (j + 1) * 128], khT[:, j, tsl], ident[:, :])
                        return ins
                    P.pe(fnT, reads=[("khT", 0), ("khT", 1), "ident"], writes=[("bank", 3)])
                    P.act(lambda e, ti=ti, pv=pv: e.activation(khat[:, ti, :], pv[:, 512:768], AF.Copy),
                          reads=[("bank", 3)], writes=["khat"])
                for ti in range(4):
                    if B1 <= 2:
                        continue
                    tsl = slice(ti * 128, (ti + 1) * 128)
                    ab = ti % 2
                    def fnA(e, tsl=tsl):
                        ins = None
                        for h in range(4):
                            j, hp = h // 2, h % 2
                            ins = e.matmul(banks[4][:, h * 128:(h + 1) * 128], kTz[j][hp][:, tsl], qT[:, j, tsl],
                                           start=True, stop=True)
                        return ins
                    P.pe(fnA, reads=[("kTz", 0), ("kTz", 1), ("qT", 0), ("qT", 1)], writes=[("bank", 4)])
                    P.dve(lambda e, ab=ab: e.tensor_tensor(atb[ab], banks[4].rearrange("p (h t) -> p h t", t=128),
                                                           hmask[:, :, :], ALU.mult),
                          reads=[("bank", 4), "hmask"], writes=[("atb", ab)])
                    def fnU(e, ti=ti):
                        ins = None
                        for c in range(2):
                            for h in range(4):
                                j = h // 2
                                ins = e.matmul(banks[5][:, (c * 4 + h) * 64:(c * 4 + h + 1) * 64],
                                               khat[:, ti, j * 128:(j + 1) * 128],
                                               vtz[c][:, ti, h * 64:(h + 1) * 64], start=True, stop=True)
                        return ins
                    P.pe(fnU, reads=["khat", "vtz"], writes=[("bank", 5)])
                    rings = [ring]
                    for c in range(2):
                        chunkcol = (ti * 2 + c) * 64 + 63
                        so, sn = S32[sidx], S32[1 - sidx]
                        for h in range(4):
                            j, hp = h // 2, h % 2
                            sl = slice(hp * 64, (hp + 1) * 64)
                            P.dve(lambda e, so=so, sn=sn, j=j, sl=sl, c=c, h=h, chunkcol=chunkcol:
                                  e.scalar_tensor_tensor(sn[sl, j, :], so[sl, j, :],
                                                         t4[j][sl, chunkcol:chunkcol + 1],
                                                         banks[5][sl, (c * 4 + h) * 64:(c * 4 + h + 1) * 64],
                                                         ALU.mult, ALU.add),
                                  reads=[("S32", sidx), ("t4", j), ("bank", 5)], writes=[("S32", 1 - sidx)])
                        sidx = 1 - sidx
                        ring = (ring + 1) % 4
                        rings.append(ring)
                        for hp in range(2):
                            sl = slice(hp * 64, (hp + 1) * 64)
                            P.act(lambda e, sl=sl, hp=hp, ring=ring, sidx=sidx:
                                  e.activation(Sbz[ring][hp][sl, :, :], S32[sidx][sl, :, :], AF.Copy),
                                  reads=[("S32", sidx)], writes=[("Sbz", ring)])
                    if B1 <= 3:
                        continue
                    def fnO(e, ti=ti, tsl=tsl, ab=ab, rings=tuple(rings)):
                        ins = None
                        for h in range(4):
                            j, hp = h // 2, h % 2
                            osl = slice(hp * 64, (hp + 1) * 64)
                            ob = banks[6 + j]
                            e.matmul(ob[osl, tsl], vt[:, ti, h * 64:(h + 1) * 64], atb[ab][:, h, :],
                                     start=True, stop=False, skip_group_check=True)
                            for c in range(2):
                                csl = slice(ti * 128 + c * 64, ti * 128 + (c + 1) * 64)
                                ins = e.matmul(ob[osl, csl], Sbz[rings[c]][hp][:, j, :], qT[:, j, csl],
                                               start=False, stop=(c == 1), skip_group_check=True)
                        return ins
                    P.pe(fnO, reads=["vt", ("atb", ab), ("Sbz", rings[0]), ("Sbz", rings[1]), ("qT", 0), ("qT", 1)],
                         writes=[("bank", 6), ("bank", 7)])
                for j in range(2):
                    if B1 <= 4:
                        continue
                    gcol = cols[:, l * 5 + 4:l * 5 + 5]
                    headnorm(banks[6 + j], ("bank", 6 + j), gcol, "cols", banks[j], ("bank", j),
                             sq, "sq", rs, "rs", gs[j], ("gs", j), mo[j], ("mo", j))
                    P.dma("sp", lambda e, j=j, g=g: e.dma_start(out=mixT[j, :, g * 512:(g + 1) * 512], in_=mo[j]),
                          reads=[("mo", j)], writes=[("mixT", j, g)])

        def phase2(l):
            AR.reset()
            kT = [AR.get([67, T], BF16) for _ in range(2)]
            qTt = [AR.get([67, T], BF16) for _ in range(2)]
            vv = [AR.get([128, NT, 65], BF16) for _ in range(2)]
            ee = [AR.get([128, 512], F32) for _ in range(2)]
            spb = [AR.get([128, 512], BF16) for _ in range(3)]
            ssum = AR.get([128, 512], F32)
            ssb = [AR.get([128, 512], BF16) for _ in range(3)]
            aa = [AR.get([128, 512], BF16) for _ in range(4)]
            osb = AR.get([65, 512], F32)
            rec = AR.get([65, 512], F32)
            on = [AR.get([64, 512], BF16) for _ in range(2)]

            for b_ in range(2):
                P.pool(lambda e, b_=b_: e.memset(kT[b_][64:67, :], 0.0), writes=[("kT", b_)])
                P.pool(lambda e, b_=b_: e.memset(qTt[b_][64:67, :], 0.0), writes=[("qTt", b_)])

            def load_head(hh):
                typ = "sb" if hh < 6 else "fx"
                h = hh % 6
                hb_ = hh % 2
                nr = 64 if typ == "sb" else 67
                qd, kd = (qsb, ksb) if typ == "sb" else (qfx, kfx)
                P.dma("sp", lambda e: e.dma_start(out=kT[hb_][0:nr, :], in_=kd[h, 0:nr, :]), writes=[("kT", hb_)])
                P.dma("act", lambda e: e.dma_start(out=qTt[hb_][0:nr, :], in_=qd[h, 0:nr, :]), writes=[("qTt", hb_)])
                P.dma("sp", lambda e: e.dma_start(
                    out=vv[hb_], in_=vsc[:, hh * 65:(hh + 1) * 65].rearrange("(n p) d -> p n d", p=128)),
                    writes=[("vv", hb_)])

            items = []
            gi = 0
            for hh in range(12):
                typ = "sb" if hh < 6 else "fx"
                for B in range(NG):
                    na = 4 * B + 4
                    order = list(range(na - 1, -1, -1)) if typ == "sb" else list(range(na))
                    for idx, a in enumerate(order):
                        items.append(dict(hh=hh, typ=typ, h=hh % 6, hb=hh % 2, nr=67,
                                          B=B, a=a, r=a - 4 * B, first=(idx == 0), last=(idx == na - 1),
                                          ob=gi % 2, newhead=(B == 0 and idx == 0)))
                    gi += 1
            n = len(items)
            load_head(0)

            def geom(it):
                r = it["r"]
                c0 = max(r, 0) * 128
                return r, c0, slice(c0, 512), 512 - c0

            def stageA(t, it):
                hb_, nr, a = it["hb"], it["nr"], it["a"]
                r, c0, cs, W = geom(it)
                ks = slice(a * 128, (a + 1) * 128)
                qs = slice(it["B"] * 512 + c0, (it["B"] + 1) * 512)
                zi = t % 3
                zb, zk = banks[zi], ("bank", zi)
                sbt = (it["typ"] == "sb")
                P.pe(lambda e: e.matmul(zb[:, cs], kT[hb_][0:nr, ks], qTt[hb_][0:nr, qs], start=True, stop=(not sbt),
                                        skip_group_check=True),
                     reads=[("kT", hb_), ("qTt", hb_)], writes=[zk])
                if it["typ"] == "fx":
                    ai = t % 4
                    h = it["h"]
                    P.act(lambda e: e.activation(aa[ai][:, cs], zb[:, cs], AF.Exp, bias=cposTok[:, a, h:h + 1]),
                          reads=[zk, "cposTok"], writes=[("aa", ai)])
                    if r >= 0:
                        P.pool(lambda e: e.tensor_tensor(aa[ai][:, cs], aa[ai][:, cs], mfxw[:, 384:384 + W], ALU.mult),
                               reads=[("aa", ai), "mfx"], writes=[("aa", ai)])
                else:
                    ei, si = t % 2, t % 3
                    P.act(lambda e: e.activation(ee[ei][:, cs], zb[:, cs], AF.Exp), reads=[zk], writes=[("ee", ei)])
                    it["_ln"] = (ei, si)

            def stageA2(t, it):
                if it["typ"] != "sb":
                    return
                ei, si = it["_ln"]
                a = it["a"]
                r, c0, cs, W = geom(it)
                P.act(lambda e: e.activation(spb[si][:, cs], ee[ei][:, cs], AF.Ln, bias=onec[:, 0:1]), reads=[("ee", ei)],
                      writes=[("spb", si)])
                if r >= 0:
                    P.pool(lambda e: e.tensor_tensor(spb[si][:, cs], spb[si][:, cs], msbw[:, 384:384 + W], ALU.mult),
                           reads=[("spb", si), "msb"], writes=[("spb", si)])
                if c0 > 0:
                    P.pool(lambda e: e.memset(spb[si][:, 0:c0], 0.0), reads=[("spb", si)], writes=[("spb", si)])
                if a > 0:
                    nsi = (t + 1) % 3
                    if it["first"]:
                        P.dve(lambda e: e.tensor_copy(ssb[nsi], spb[si]), reads=[("spb", si)], writes=[("ssb", nsi)])
                        P.dve(lambda e: e.tensor_copy(ssum, spb[si]), reads=[("spb", si)], writes=["ssum"])
                    else:
                        P.dve(lambda e: e.tensor_tensor(ssb[nsi], ssum, spb[si], ALU.add),
                              reads=["ssum", ("spb", si)], writes=[("ssb", nsi)])
                        P.dve(lambda e: e.tensor_tensor(ssum, ssum, spb[si], ALU.add),
                              reads=["ssum", ("spb", si)], writes=["ssum"])

            def stageB_pe(t, it):
                if it["typ"] != "sb":
                    return
                r, c0, cs, W = geom(it)
                si = t % 3
                li = t % 3
                lb_, lk = banks[li], ("bank", li)
                prs = [(negtri[:, :], spb[si][:, cs])]
                rd = [("spb", si), "negtri"]
                if not it["first"]:
                    prs.append((negones[:, :], ssb[si][:, cs]))
                    rd += [("ssb", si), "negones"]

                def fnB(e, prs=prs, lb_=lb_):
                    ins = None
                    for i_, (lt, r_) in enumerate(prs):
                        ins = e.matmul(lb_[:, cs], lt, r_, start=False, stop=(i_ == len(prs) - 1), skip_group_check=True)
                    return ins
                P.pe(fnB, reads=rd + [lk], writes=[lk])

            def stageB_act(t, it):
                if it["typ"] != "sb":
                    return
                r, c0, cs, W = geom(it)
                ai = t % 4
                li = t % 3
                lb_, lk = banks[li], ("bank", li)
                P.act(lambda e: e.activation(aa[ai][:, cs], lb_[:, cs], AF.Exp), reads=[lk], writes=[("aa", ai)])
                if r >= 0:
                    P.pool(lambda e: e.tensor_tensor(aa[ai][:, cs], aa[ai][:, cs], msbw[:, 384:384 + W], ALU.mult),
                           reads=[("aa", ai), "msb"], writes=[("aa", ai)])

            def stageC(t, it):
                if it["newhead"] and it["hh"] + 1 < 12:
                    load_head(it["hh"] + 1)
                hb_, a = it["hb"], it["a"]
                r, c0, cs, W = geom(it)
                ai = t % 4
                ob = banks[6 + it["ob"]]
                okey = ("bank", 6 + it["ob"])
                first, last = it["first"], it["last"]
                P.pe(lambda e: e.matmul(ob[0:65, cs], vv[hb_][:, a, :], aa[ai][:, cs], start=first, stop=last,
                                        skip_group_check=True),
                     reads=[("vv", hb_), ("aa", ai)], writes=[okey])
                if not last:
                    return
                hh, B = it["hh"], it["B"]
                qs = slice(B * 512, (B + 1) * 512)
                ob_ = it["ob"]
                if it["typ"] == "fx":
                    P.dve(lambda e: e.reciprocal(rec[64:65, :], ob[64:65, :]), reads=[okey], writes=["rec"])
                    P.dve(lambda e: e.tensor_copy(osb[0:64, :], ob[0:64, :]), reads=[okey], writes=["osb"])
                    mm_group(banks[4][0:64, :], [(ones32[64:65, 0:64], rec[64:65, :])], reads=["rec", "ones32"],
                             writes=[("bank", 4)])
                    P.dve(lambda e: e.tensor_tensor(on[ob_], osb[0:64, :], banks[4][0:64, :], ALU.mult),
                          reads=["osb", ("bank", 4)], writes=[("on", ob_)])
                else:
                    P.dve(lambda e: e.tensor_copy(on[ob_], ob[0:64, :]), reads=[okey], writes=[("on", ob_)])
                fch = 2 + hh // 2
                prow = (hh % 2) * 64
                P.dma("sp", lambda e: e.dma_start(out=mixT[fch, prow:prow + 64, qs], in_=on[ob_]),
                      reads=[("on", ob_)], writes=[("mixT", hh, B)])

            for t in range(n + 3):
                if 0 <= t - 2 < n:
                    stageB_act(t - 2, items[t - 2])
                if t < n:
                    stageA(t, items[t])
                if 0 <= t - 1 < n:
                    stageB_pe(t - 1, items[t - 1])
                if t < n:
                    stageA2(t, items[t])
                if 0 <= t - 3 < n:
                    stageC(t - 3, items[t - 3])

        def phase3a(l, xsrc, wo):
            AR.reset()
            mx = [AR.get([128, 8, 512], BF16) for _ in range(2)]
            xt = [AR.get([128, D], F32) for _ in range(3)]
            it = 0
            for g in range(NG):
                mb = g % 2
                for k in range(8):
                    P.dma("sp" if k % 2 == 0 else "act", lambda e, k=k, g=g, mb=mb: e.dma_start(
                        out=mx[mb][:, k, :], in_=mixT[k, :, g * 512:(g + 1) * 512]), writes=[("mx", mb)])
                for ti in range(4):
                    t0 = g * 512 + ti * 128
                    b = it % 3
                    it += 1
                    P.dma("sp", lambda e, b=b, t0=t0: e.dma_start(out=xt[b], in_=xsrc[t0:t0 + 128, :]),
                          writes=[("xt", b)])
                    for hf in range(2):
                        pb = (it * 2 + hf) % 4
                        mm_group(banks[pb], [(mx[mb][:, k, ti * 128:(ti + 1) * 128], wo[:, k, hf * 512:(hf + 1) * 512])
                                             for k in range(8)],
                                 reads=[("mx", mb), "slotB"], writes=[("bank", pb)])
                        P.dve(lambda e, b=b, hf=hf, pb=pb: e.tensor_tensor(
                            xt[b][:, hf * 512:(hf + 1) * 512], xt[b][:, hf * 512:(hf + 1) * 512], banks[pb], ALU.add),
                            reads=[("xt", b), ("bank", pb)], writes=[("xt", b)])
                    P.dma("act", lambda e, b=b, t0=t0: e.dma_start(out=xa[t0:t0 + 128, :], in_=xt[b]),
                          reads=[("xt", b)], writes=[("xa", t0)])

        def phase3f(l, half, src, dst, w1, w2, wkey):
            AR.reset()
            xt = [AR.get([128, D], F32) for _ in range(4)]
            hT = [AR.get([128, 8, 512], BF16) for _ in range(2)]
            hid = AR.get([128, 16, 512], BF16)
            rl = [AR.get([128, 512], F32) for _ in range(2)]
            if half == 0:
                gbc = AR.get([128, D], F32)
                sqs = AR.get([128, D], BF16)
                hb = [AR.get([128, D], BF16) for _ in range(2)]
                st = [AR.get([128, 4], F32) for _ in range(2)]
                P.dma("sp", lambda e: e.dma_start(out=gbc, in_=n2g[l, :].partition_broadcast(128)), writes=["gbc"])
            it = 0
            for g in range(NG):
                hb_ = g % 2
                hTg = hT[hb_]
                hk = ("hT", hb_)
                for ti in range(4):
                    t0 = g * 512 + ti * 128
                    P.dma("sp", lambda e, ti=ti, t0=t0: e.dma_start(out=xt[ti], in_=src[t0:t0 + 128, :]),
                          writes=[("xt", ti)])
                if half == 0:
                    for ti in range(4):
                        b = it % 2
                        it += 1
                        norm_tile(xt[ti], ("xt", ti), gbc, hb[b], ("hb", b), st[b], ("st", b), sqs, "sqs")
                        transpose_h(hb[b], ("hb", b), hTg, hk, ti * 128, 7, ("bank", 7))
                    for k in range(8):
                        P.dma("act", lambda e, k=k, g=g, hTg=hTg: e.dma_start(
                            out=hTs[k, :, g * 512:(g + 1) * 512], in_=hTg[:, k, :]), reads=[hk], writes=[("hTs", g)])
                else:
                    for k in range(8):
                        P.dma("sp" if k % 2 == 0 else "act", lambda e, k=k, g=g, hTg=hTg: e.dma_start(
                            out=hTg[:, k, :], in_=hTs[k, :, g * 512:(g + 1) * 512]), writes=[hk])
                for f in range(16):
                    pb = f % 2
                    mm_group(banks[pb], [(w1[:, k, f * 128:(f + 1) * 128], hTg[:, k, :]) for k in range(8)],
                             reads=[hk, wkey], writes=[("bank", pb)])
                    P.act(lambda e, pb=pb: e.activation(rl[pb], banks[pb], AF.Relu), reads=[("bank", pb)],
                          writes=[("rl", pb)])
                    if f % 2 == 0:
                        P.dve(lambda e, pb=pb, f=f: e.tensor_tensor(hid[:, f, :], rl[pb], rl[pb], ALU.mult),
                              reads=[("rl", pb)], writes=["hid"])
                    else:
                        P.pool(lambda e, pb=pb, f=f: e.tensor_tensor(hid[:, f, :], rl[pb], rl[pb], ALU.mult),
                               reads=[("rl", pb)], writes=["hid"])
                for ti in range(4):
                    t0 = g * 512 + ti * 128
                    for hf in range(2):
                        pb = 2 + (ti * 2 + hf) % 4
                        mm_group(banks[pb], [(hid[:, f, ti * 128:(ti + 1) * 128], w2[:, f, hf * 512:(hf + 1) * 512])
                                             for f in range(16)],
                                 reads=["hid", wkey], writes=[("bank", pb)])
                        P.dve(lambda e, ti=ti, hf=hf, pb=pb: e.tensor_tensor(
                            xt[ti][:, hf * 512:(hf + 1) * 512], xt[ti][:, hf * 512:(hf + 1) * 512], banks[pb],
                            ALU.add), reads=[("xt", ti), ("bank", pb)], writes=[("xt", ti)])
                    P.dma("act", lambda e, ti=ti, t0=t0: e.dma_start(out=dst[t0:t0 + 128, :], in_=xt[ti]),
                          reads=[("xt", ti)], writes=[("dst", t0)])

        def whole():
            setup_consts()
            if stop == -1:
                return
            wv = load_w_in(0)
            P.barrier()
            if stop == 0:
                return
            for l in range(depth):
                xsrc = x_in if l == 0 else xb
                wo = load_w_out(l)
                phase1a(l, xsrc, wv)
                P.barrier()
                if stop == 1:
                    return
                phase1b(l, wv)
                P.barrier()
                if stop == 2:
                    return
                w1a, w2a = load_ffn_half(l, 0, slotA, "slotA")
                phase2(l)
                P.barrier()
                if stop == 3:
                    return
                if dbg and l == 0:
                    for nm, srcd in (("d_mixT", mixT), ("d_qfx", qfx), ("d_kfx", kfx), ("d_qsb", qsb), ("d_ksb", ksb)):
                        dd = dbg_out[nm]
                        for i in range(dd.shape[0]):
                            P.dma("sp", lambda e, dd=dd, srcd=srcd, i=i: e.dma_start(out=dd[i], in_=srcd[i]))
                phase3a(l, xsrc, wo)
                P.barrier()
                if stop == 4:
                    return
                if dbg and l == 0:
                    for t0 in range(0, T, 128):
                        P.dma("sp", lambda e, t0=t0: e.dma_start(out=dbg_out["d_xa"][t0:t0 + 128, :], in_=xa[t0:t0 + 128, :]))
                w1b, w2b = load_ffn_half(l, 1, slotB, "slotB")
                phase3f(l, 0, xa, xp, w1a, w2a, "slotA")
                P.barrier()
                if stop == 5:
                    return
                if l + 1 < depth:
                    wv = load_w_in(l + 1)
                phase3f(l, 1, xp, (y if l == depth - 1 else xb), w1b, w2b, "slotB")
                P.barrier()
        whole()
        P.barrier()
        stats = P.emit()
    return nc, stats


_CACHE = {}


def _host_layout(inputs, depth):
    f = lambda a: np.ascontiguousarray(np.asarray(a, dtype=np.float32))
    cols = np.zeros((128, depth * 5), np.float32)
    for l in range(depth):
        for i, nm in enumerate(("sb_q_norm_g", "sb_k_norm_g", "fox_q_norm_g", "fox_k_norm_g", "hg_norm_g")):
            cols[:, l * 5 + i] = np.tile(f(inputs[nm])[l], 2)
    lb = f(inputs["lb_logits"])
    lbT = np.zeros((128, 2 * depth), np.float32)
    for j in range(2):
        lbT[:, j * depth:(j + 1) * depth] = lb[:, j * 128:(j + 1) * 128].T
    fb = np.ascontiguousarray(f(inputs["fox_f_bias"]).T)
    return cols, lbT, fb


def run(inputs, T, depth, n_cores, dbg=False, stop=99):
    key = (T, depth, dbg, stop)
    if key not in _CACHE:
        _CACHE[key] = build(T, depth, dbg, stop)
    nc, stats = _CACHE[key]
    f = lambda a: np.ascontiguousarray(np.asarray(a, dtype=np.float32))
    cols, lbT, fb = _host_layout(inputs, depth)
    shared = {
        "w_in": f(inputs["w_in"]), "w_out": f(inputs["w_out"]), "w_ff1": f(inputs["w_ff1"]),
        "w_ff2": f(inputs["w_ff2"]), "norm1_g": f(inputs["norm1_g"]), "norm2_g": f(inputs["norm2_g"]),
        "cols": cols, "lbT": lbT, "fb": fb,
    }
    x = f(inputs["x"])
    in_maps = []
    for c in range(n_cores):
        m = dict(shared)
        m["x"] = np.ascontiguousarray(x[c])
        in_maps.append(m)
    res = run_bass_kernel_spmd(nc, in_maps, core_ids=list(range(n_cores)))
    return res


def kernel(**inputs):
    res = run(inputs, 4096, 4, 8)
    return np.stack([np.asarray(r["y"], dtype=np.float32) for r in res.results], axis=0)
```

```python
import contextlib
import os
import numpy as np
import concourse.bass as bass
import concourse.mybir as mybir
from concourse.bass_utils import run_bass_kernel_spmd

F32 = mybir.dt.float32
BF16 = mybir.dt.bfloat16
AF = mybir.ActivationFunctionType
ALU = mybir.AluOpType
AX = mybir.AxisListType

D = 1024
INC = 3334
DFF = 4096
EPS = 1e-6
OFF = dict(hq=0, hf=256, hi=512, hg=768, sq=1024, sk=1408, sv=1792, fq=2176, fk=2560,
           fv=2944, ff=3328)
N_DMA_SEMS = {"hw": 24, "sw": 8}
GROUP_DMA = True


class Op:
    __slots__ = ("eng", "fn", "waits", "signal", "dma_idx", "seq", "count")

    def __init__(self, eng, fn):
        self.eng = eng
        self.fn = fn
        self.waits = []
        self.signal = False
        self.dma_idx = None
        self.seq = None
        self.count = None


class Prog:
    ENGS = ("pe", "act", "dve", "pool", "sp")

    def __init__(self, nc):
        self.nc = nc
        self.ops = {e: [] for e in self.ENGS}
        self.state = {}
        self.waited = {e: {} for e in self.ENGS}
        self.n_dma = {"hw": 0, "sw": 0}

    def _add(self, eng, fn, reads, writes, dma=False):
        op = Op(eng, fn)
        op.seq = len(self.ops[eng])
        if dma:
            pool = "sw" if eng == "pool" else "hw"
            op.dma_idx = (pool, self.n_dma[pool])
            self.n_dma[pool] += 1
            tok = ("d", pool, op.dma_idx[1])
            if op.dma_idx[1] >= N_DMA_SEMS[pool]:
                self._need(op, ("d", pool, op.dma_idx[1] - N_DMA_SEMS[pool]))
        else:
            tok = ("e", eng, op.seq)
        st = self.state
        xb_ = [k for k in reads if isinstance(k, tuple) and k[0] == "bank" and k not in writes]
        if xb_:
            writes = list(writes) + xb_
        for k in reads:
            s = st.get(k)
            if s is not None:
                for w in s[0]:
                    self._need(op, w)
                s[2] = False
        for k in writes:
            s = st.get(k)
            if s is None:
                continue
            if dma and s[2] and GROUP_DMA:
                for w in s[3]:
                    self._need(op, w)
                continue
            for w in s[0]:
                self._need(op, w)
            for rt in s[1].values():
                self._need(op, rt)
        for k in reads:
            s = st.setdefault(k, [[], {}, False, []])
            if dma:
                s[1][tok] = tok
            else:
                s[1][eng] = tok
        for k in writes:
            s = st.get(k)
            if dma and s is not None and s[2] and GROUP_DMA:
                s[0].append(tok)
            else:
                pre = (list(s[0]) + list(s[1].values())) if s is not None else []
                st[k] = [[tok], {}, bool(dma), pre]
        self.ops[eng].append(op)
        return op

    def _need(self, op, tok):
        if tok[0] == "e":
            key, val = tok[1], tok[2]
        else:
            key, val = ("d", tok[1], tok[2] % N_DMA_SEMS[tok[1]]), tok[2]
        wd = self.waited[op.eng]
        if wd.get(key, -1) >= val:
            return
        wd[key] = val
        op.waits.append(tok)
        if tok[0] == "e":
            self.ops[tok[1]][tok[2]].signal = True

    def pe(self, fn, reads=(), writes=()):
        return self._add("pe", fn, reads, writes)

    def act(self, fn, reads=(), writes=()):
        return self._add("act", fn, reads, writes)

    def dve(self, fn, reads=(), writes=()):
        return self._add("dve", fn, reads, writes)

    def pool(self, fn, reads=(), writes=()):
        return self._add("pool", fn, reads, writes)

    def dma(self, q, fn, reads=(), writes=()):
        return self._add(q, fn, reads, writes, dma=True)

    def barrier(self):
        last = {}
        for e in self.ENGS:
            for op in reversed(self.ops[e]):
                if op.dma_idx is None and op.fn is not None:
                    last[e] = ("e", e, op.seq)
                    break
        ndma = dict(self.n_dma)
        for e in self.ENGS:
            op = Op(e, None)
            op.seq = len(self.ops[e])
            for e2, tok in last.items():
                if e2 != e:
                    self._need(op, tok)
            for pool in ("hw", "sw"):
                for i in range(max(0, ndma[pool] - N_DMA_SEMS[pool]), ndma[pool]):
                    self._need(op, ("d", pool, i))
            self.ops[e].append(op)
        self.state = {}

    def emit(self):
        nc = self.nc
        for e in self.ENGS:
            c = 0
            for op in self.ops[e]:
                if op.dma_idx is None and op.signal:
                    c += 1
                op.count = c
        stats = {}
        with contextlib.ExitStack() as es:
            esem = {e: es.enter_context(nc.semaphore("s_" + e)) for e in self.ENGS}
            dsem = {p: [es.enter_context(nc.semaphore("d%s_%d" % (p, i))) for i in range(N_DMA_SEMS[p])]
                    for p in ("hw", "sw")}
            block = es.enter_context(nc.Block())
            engobj = {"pe": block.tensor, "act": block.scalar, "dve": block.vector,
                      "pool": block.gpsimd, "sp": block.sync}
            allops = self.ops
            for e in self.ENGS:
                def body(eng, e=e):
                    nw = 0
                    for op in allops[e]:
                        for tok in op.waits:
                            if tok[0] == "e":
                                eng.wait_ge(esem[tok[1]], allops[tok[1]][tok[2]].count)
                            else:
                                pl, i = tok[1], tok[2]
                                eng.wait_ge(dsem[pl][i % N_DMA_SEMS[pl]], 16 * (i // N_DMA_SEMS[pl] + 1))
                            nw += 1
                        if op.fn is None:
                            continue
                        ins = op.fn(eng)
                        if op.dma_idx is not None:
                            pl, i = op.dma_idx
                            ins.then_inc(dsem[pl][i % N_DMA_SEMS[pl]], 16)
                        elif op.signal:
                            ins.then_inc(esem[e], 1)
                    stats[e] = (len(allops[e]), nw, allops[e][-1].count if allops[e] else 0)
                engobj[e](body)
        return stats


def build(T, depth, dbg=False, stop=99):
    nc = bass.Bass("TRN2", target_bir_lowering=False)
    NT = T // 128
    NG = T // 512
    P = Prog(nc)

    def din(name, shape, dt=F32):
        return nc.dram_tensor(name, list(shape), dt, kind="ExternalInput").ap()

    def dscr(name, shape, dt):
        return nc.dram_tensor(name, list(shape), dt, kind="Internal").ap()

    x_in = din("x", [T, D])
    w_in = din("w_in", [depth, D, INC])
    w_out = din("w_out", [depth, D, D])
    w_ff1 = din("w_ff1", [depth, D, DFF])
    w_ff2 = din("w_ff2", [depth, DFF, D])
    n1g = din("norm1_g", [depth, D])
    n2g = din("norm2_g", [depth, D])
    cols_in = din("cols", [128, depth * 5])
    lbT_in = din("lbT", [128, 2 * depth])
    fb_in = din("fb", [6, depth])
    y = nc.dram_tensor("y", [T, D], F32, kind="ExternalOutput").ap()

    xa = dscr("xa", [T, D], F32)
    xp = dscr("xp", [T, D], F32)
    xb = dscr("xb", [T, D], F32)
    hTs = dscr("hTs", [8, 128, T], BF16)
    qsb = dscr("qsb", [6, 64, T], BF16)
    ksb = dscr("ksb", [6, 64, T], BF16)
    qfx = dscr("qfx", [6, 67, T], BF16)
    kfx = dscr("kfx", [6, 67, T], BF16)
    vsc = dscr("vsc", [T, 12 * 65], BF16)
    mixT = dscr("mixT", [8, 128, T], BF16)
    dbg_out = {}
    if dbg:
        for nm, shp, dt in (("d_mixT", [8, 128, T], BF16), ("d_xa", [T, D], F32),
                            ("d_qfx", [6, 67, T], BF16), ("d_kfx", [6, 67, T], BF16),
                            ("d_qsb", [6, 64, T], BF16), ("d_ksb", [6, 64, T], BF16)):
            dbg_out[nm] = nc.dram_tensor(nm, shp, dt, kind="ExternalOutput").ap()

    es = contextlib.ExitStack()
    with es:
        def sb(name, shape, dt):
            return es.enter_context(nc.sbuf_tensor("sb_" + name, list(shape), dt))

        def ps(name, shape, dt=F32):
            return es.enter_context(nc.psum_tensor("ps_" + name, list(shape), dt))

        slotA = sb("slotA", [128, 32768], BF16)
        slotB = sb("slotB", [128, 32768], BF16)
        ARENA = 63 * 1024
        arena = sb("arena", [128, ARENA], mybir.dt.uint8)
        ident = sb("ident", [128, 128], BF16)
        ident32 = sb("ident32", [128, 128], F32)
        blk = sb("blk", [128, 128], BF16)
        negtri = sb("negtri", [128, 128], BF16)
        negones = sb("negones", [128, 128], BF16)
        msbw = sb("msbw", [128, 896], BF16)
        mfxw = sb("mfxw", [128, 896], BF16)
        hmask = sb("hmask", [128, 4, 128], BF16)
        rmask = sb("rmask", [128, 512], BF16)
        ones32 = sb("ones32", [128, 64], F32)
        onesb = sb("onesb", [128, 512], BF16)
        cols = sb("cols", [128, depth * 5], F32)
        colq = sb("colq", [128, depth * 2], F32)
        lbx = sb("lbx", [128, 2 * depth], F32)
        lbc = sb("lbc", [128, 2 * depth], F32)
        omlb = sb("omlb", [128, 2 * depth], F32)
        lbs = sb("lbs", [128, 2], F32)
        fbc = sb("fbc", [6, depth], F32)
        nfb = sb("nfb", [6, depth], F32)
        cposTok = sb("cposTok", [128, NT, 6], F32)
        epsc = sb("epsc", [128, 1], F32)
        onec = sb("onec", [128, 1], F32)

        banks = [ps("bank%d" % i, [128, 512], F32)[:, :] for i in range(8)]

        class Arena:
            def __init__(self):
                self.off = 0

            def reset(self):
                self.off = 0

            def get(self, shape, dt):
                nb = int(np.prod(shape[1:])) * (4 if dt == F32 else 2)
                nbytes = (nb + 31) // 32 * 32
                assert self.off + nbytes <= ARENA, ("arena overflow", self.off, nbytes)
                ap = arena[0:shape[0], self.off:self.off + nb].bitcast(dt)
                self.off += nbytes
                if len(shape) == 3:
                    ap = ap.rearrange("p (a b) -> p a b", b=shape[2])
                return ap
        AR = Arena()

        def bankbf(i):
            return banks[i][:, :].bitcast(BF16)

        def setup_consts():
            P.pool(lambda e: e.memset(ident[:, :], 1.0), writes=["ident"])
            P.pool(lambda e: e.affine_select(ident[:, :], ident[:, :], [[1, 128]], ALU.is_equal, 0.0,
                                             base=0, channel_multiplier=-1),
                   reads=["ident"], writes=["ident"])
            P.pool(lambda e: e.memset(ident32[:, :], 1.0), writes=["ident32"])
            P.pool(lambda e: e.affine_select(ident32[:, :], ident32[:, :], [[1, 128]], ALU.is_equal,
                                             0.0, base=0, channel_multiplier=-1),
                   reads=["ident32"], writes=["ident32"])
            P.pool(lambda e: e.memset(blk[:, :], 0.0), writes=["blk"])
            P.pool(lambda e: e.memset(blk[0:64, 0:64], 1.0 / 64), reads=["blk"], writes=["blk"])
            P.pool(lambda e: e.memset(blk[64:128, 64:128], 1.0 / 64), reads=["blk"], writes=["blk"])
            P.pool(lambda e: e.memset(negtri[:, :], -1.0), writes=["negtri"])
            P.pool(lambda e: e.affine_select(negtri[:, :], negtri[:, :], [[-1, 128]], ALU.is_ge, 0.0,
                                             base=0, channel_multiplier=1),
                   reads=["negtri"], writes=["negtri"])
            P.pool(lambda e: e.memset(negones[:, :], -1.0), writes=["negones"])
            P.pool(lambda e: e.memset(onesb[:, :], 1.0), writes=["onesb"])
            P.pool(lambda e: e.memset(ones32[:, :], 1.0), writes=["ones32"])
            P.pool(lambda e: e.memset(epsc[:, :], EPS), writes=["epsc"])
            P.pool(lambda e: e.memset(onec[:, :], 1.0), writes=["onec"])
            P.pool(lambda e: e.memset(msbw[:, :], 1.0), writes=["msb"])
            P.pool(lambda e: e.affine_select(msbw[:, :], msbw[:, :], [[1, 896]], ALU.is_gt,
                                             0.0, base=-384, channel_multiplier=-1),
                   reads=["msb"], writes=["msb"])
            P.pool(lambda e: e.memset(mfxw[:, :], 1.0), writes=["mfx"])
            P.pool(lambda e: e.affine_select(mfxw[:, :], mfxw[:, :], [[1, 896]], ALU.is_ge,
                                             0.0, base=-384, channel_multiplier=-1),
                   reads=["mfx"], writes=["mfx"])
            P.pool(lambda e: e.memset(hmask[:, :, :], 0.0), writes=["hmask"])
            for c in range(2):
                P.pool(lambda e, c=c: e.memset(hmask[c * 64:(c + 1) * 64, :, c * 64:(c + 1) * 64], 1.0),
                       reads=["hmask"], writes=["hmask"])
            P.pool(lambda e: e.affine_select(hmask[:, :, :], hmask[:, :, :], [[0, 4], [1, 128]],
                                             ALU.is_ge, 0.0, base=0, channel_multiplier=-1),
                   reads=["hmask"], writes=["hmask"])
            P.pool(lambda e: e.memset(rmask[:, :], 1.0), writes=["rmask"])
            P.pool(lambda e: e.memset(rmask[:, :].rearrange("p (c t) -> p c t", t=64)[:, :, 0:1], 0.0),
                   reads=["rmask"], writes=["rmask"])
            P.dma("sp", lambda e: e.dma_start(out=cols[:, :], in_=cols_in[:, :]), writes=["cols"])
            P.dma("sp", lambda e: e.dma_start(out=lbx[:, :], in_=lbT_in[:, :]), writes=["lbx"])
            P.dma("sp", lambda e: e.dma_start(out=fbc[:, :], in_=fb_in[:, :]), writes=["fbc"])
            P.dve(lambda e: e.tensor_scalar(nfb[:, :], fbc[:, :], -1.0, None, ALU.mult),
                  reads=["fbc"], writes=["nfb"])
            for l in range(depth):
                for i, c in enumerate((0, 2)):
                    P.dve(lambda e, l=l, i=i, c=c: e.tensor_scalar(
                        colq[:, l * 2 + i:l * 2 + i + 1], cols[:, l * 5 + c:l * 5 + c + 1],
                        0.125, None, ALU.mult), reads=["cols"], writes=["colq"])
            P.act(lambda e: e.activation(lbx[:, :], lbx[:, :], AF.Exp), reads=["lbx"], writes=["lbx"])
            lb3 = lbx[:, :].rearrange("p (j l) -> p j l", l=depth)
            P.dve(lambda e: e.tensor_reduce(lbs[:, :], lb3, AX.X, ALU.add), reads=["lbx"], writes=["lbs"])
            P.dve(lambda e: e.reciprocal(lbs[:, :], lbs[:, :]), reads=["lbs"], writes=["lbs"])
            for j in range(2):
                P.dve(lambda e, j=j: e.tensor_scalar(lbx[:, j * depth:(j + 1) * depth],
                                                     lbx[:, j * depth:(j + 1) * depth],
                                                     lbs[:, j:j + 1], None, ALU.mult),
                      reads=["lbx", "lbs"], writes=["lbx"])
                P.dve(lambda e, j=j: e.memset(lbc[:, j * depth:j * depth + 1], 0.0),
                      reads=["lbc"], writes=["lbc"])
                for l in range(1, depth):
                    P.dve(lambda e, j=j, l=l: e.tensor_tensor(
                        lbc[:, j * depth + l:j * depth + l + 1], lbc[:, j * depth + l - 1:j * depth + l],
                        lbx[:, j * depth + l:j * depth + l + 1], ALU.add),
                        reads=["lbc", "lbx"], writes=["lbc"])
            P.dve(lambda e: e.tensor_scalar(omlb[:, :], lbc[:, :], -1.0, 1.0, ALU.mult, ALU.add),
                  reads=["lbc"], writes=["omlb"])
            for h in range(6):
                for t0 in range(0, T, 512):
                    P.dma("sp", lambda e, h=h, t0=t0: e.dma_start(out=kfx[h, 64:67, t0:t0 + 512],
                                                                   in_=onesb[0:3, :]),
                          reads=["onesb"], writes=[("kfx1", h)])

        def load_w_in(l):
            v = slotA[:, 0:8 * INC].rearrange("p (k n) -> p k n", n=INC)
            for k in range(8):
                for hh in range(2):
                    c0 = hh * 1667
                    P.dma("pool", lambda e, k=k, c0=c0: e.dma_start(
                        out=v[:, k, c0:c0 + 1667], in_=w_in[l, k * 128:(k + 1) * 128, c0:c0 + 1667]),
                        writes=["slotA"])
            return v

        def load_w_out(l):
            v = slotB[:, 0:8 * D].rearrange("p (k n) -> p k n", n=D)
            for k in range(8):
                P.dma("pool", lambda e, k=k: e.dma_start(
                    out=v[:, k, :], in_=w_out[l, k * 128:(k + 1) * 128, :]), writes=["slotB"])
            return v

        def load_ffn_half(l, hf, slot, key):
            w1 = slot[:, 0:16384].rearrange("p (k n) -> p k n", n=2048)
            w2 = slot[:, 16384:32768].rearrange("p (k n) -> p k n", n=1024)
            for k in range(8):
                for hh in range(2):
                    c0 = hf * 2048 + hh * 1024
                    P.dma("pool", lambda e, k=k, c0=c0, hh=hh: e.dma_start(
                        out=w1[:, k, hh * 1024:(hh + 1) * 1024],
                        in_=w_ff1[l, k * 128:(k + 1) * 128, c0:c0 + 1024]), writes=[key])
            for fc in range(16):
                r0 = hf * 2048 + fc * 128
                P.dma("pool", lambda e, fc=fc, r0=r0: e.dma_start(
                    out=w2[:, fc, :], in_=w_ff2[l, r0:r0 + 128, :]), writes=[key])
            return w1, w2

        def mm_group(out, pairs, reads, writes):
            def fn(e):
                n = len(pairs)
                ins = None
                for i, (lt, r) in enumerate(pairs):
                    ins = e.matmul(out, lt, r, start=(i == 0), stop=(i == n - 1))
                return ins
            P.pe(fn, reads, writes)

        def norm_tile(xt, xkey, gbc, hb, hkey, st, stkey, sq_scr, sqkey):
            P.act(lambda e: e.activation(sq_scr, xt, AF.Square, accum_out=st[:, 0:1]),
                  reads=[xkey], writes=[sqkey, stkey])
            P.act(lambda e: e.activation(st[:, 1:2], st[:, 0:1], AF.Ln, bias=epsc[:, 0:1], scale=1.0 / D),
                  reads=[stkey], writes=[stkey])
            P.act(lambda e: e.activation(st[:, 2:3], st[:, 1:2], AF.Exp, scale=-0.5),
                  reads=[stkey], writes=[stkey])
            P.dve(lambda e: e.scalar_tensor_tensor(hb, xt, st[:, 2:3], gbc, ALU.mult, ALU.mult),
                  reads=[xkey, stkey, "gbc"], writes=[hkey])

        def transpose_h(hb, hkey, hT, hTkey, tcol, pbank, pkey):
            for half in range(2):
                pv = bankbf(pbank)

                def fn(e, half=half, pv=pv):
                    ins = None
                    for kk in range(4):
                        k = half * 4 + kk
                        ins = e.transpose(pv[:, kk * 128:(kk + 1) * 128], hb[:, k * 128:(k + 1) * 128],
                                          ident[:, :])
                    return ins
                P.pe(fn, reads=[hkey, "ident"], writes=[pkey])
                src = pv[:, 0:512].rearrange("p (k t) -> p k t", t=128)
                dst = hT[:, half * 4:(half + 1) * 4, tcol:tcol + 128]
                if half == 0:
                    P.act(lambda e, src=src, dst=dst: e.activation(dst, src, AF.Copy),
                          reads=[pkey], writes=[hTkey])
                else:
                    P.dve(lambda e, src=src, dst=dst: e.tensor_copy(dst, src),
                          reads=[pkey], writes=[hTkey])

        def headnorm_a(psrc, pskey, sq, sqkey):
            P.act(lambda e: e.activation(sq, psrc, AF.Square), reads=[pskey], writes=[sqkey])

        def headnorm_b(psrc, pskey, gcol, gkey, bank_ms, mskey, sq, sqkey, rs, rskey, extra_mul, emkey,
                       dst, dstkey):
            mm_group(bank_ms, [(blk[:, :], sq)], reads=[sqkey, "blk"], writes=[mskey])
            P.act(lambda e: e.activation(rs, bank_ms, AF.Ln, bias=epsc[:, 0:1]),
                  reads=[mskey], writes=[rskey])
            P.act(lambda e: e.activation(rs, rs, AF.Exp, scale=-0.5), reads=[rskey], writes=[rskey])
            if extra_mul is None:
                P.dve(lambda e: e.scalar_tensor_tensor(dst, psrc, gcol, rs, ALU.mult, ALU.mult),
                      reads=[pskey, rskey, gkey], writes=[dstkey])
            else:
                P.dve(lambda e: e.scalar_tensor_tensor(rs, psrc, gcol, rs, ALU.mult, ALU.mult),
                      reads=[pskey, rskey, gkey], writes=[rskey])
                P.pool(lambda e: e.tensor_tensor(dst, rs, extra_mul, ALU.mult),
                       reads=[rskey, emkey], writes=[dstkey])

        def headnorm(psrc, pskey, gcol, gkey, bank_ms, mskey, sq, sqkey, rs, rskey, extra_mul, emkey,
                     dst, dstkey):
            headnorm_a(psrc, pskey, sq, sqkey)
            headnorm_b(psrc, pskey, gcol, gkey, bank_ms, mskey, sq, sqkey, rs, rskey, extra_mul, emkey, dst, dstkey)

        def phase1a(l, xsrc, wv):
            AR.reset()
            gbc = AR.get([128, D], F32)
            xt = [AR.get([128, D], F32) for _ in range(2)]
            sqs = AR.get([128, D], BF16)
            hb = [AR.get([128, D], BF16) for _ in range(2)]
            st = [AR.get([128, 4], F32) for _ in range(2)]
            hT = [AR.get([128, 8, 512], BF16) for _ in range(2)]
            sq = [AR.get([128, 512], BF16) for _ in range(2)]
            rs = [AR.get([128, 512], F32) for _ in range(2)]
            qn = [AR.get([128, 512], BF16) for _ in range(2)]
            vaug = [AR.get([128, 12, 65], BF16) for _ in range(2)]
            spt = AR.get([6, 512], F32)
            cpos = [AR.get([6, 512], F32) for _ in range(2)]
            rr = AR.get([6, 512], F32)
            qh = [AR.get([6, 512], BF16) for _ in range(3)]
            P.dma("sp", lambda e: e.dma_start(out=gbc, in_=n1g[l, :].partition_broadcast(128)),
                  writes=["gbc"])
            for i in range(2):
                P.pool(lambda e, i=i: e.memset(vaug[i][:, :, 64:65], 1.0), writes=[("vaug", i)])
            itc = [0]

            pbuf = {}

            def prep_norm(g, ti):
                t0 = g * 512 + ti * 128
                b = itc[0] % 2
                itc[0] += 1
                pbuf[(g, ti)] = b
                P.dma("sp", lambda e, b=b, t0=t0: e.dma_start(out=xt[b], in_=xsrc[t0:t0 + 128, :]),
                      writes=[("xt", b)])
                norm_tile(xt[b], ("xt", b), gbc, hb[b], ("hb", b), st[b], ("st", b), sqs, "sqs")

            def prep_T(g, ti):
                b = pbuf[(g, ti)]
                transpose_h(hb[b], ("hb", b), hT[g % 2], ("hT", g % 2), ti * 128, 7, ("bank", 7))

            def prep_tile(g, ti):
                prep_norm(g, ti)
                prep_T(g, ti)

            for ti in range(4):
                prep_tile(0, ti)
            for g in range(NG):
                hb_ = g % 2
                hTg = hT[hb_]
                hk = ("hT", hb_)
                for k in range(8):
                    P.dma("act", lambda e, k=k, g=g, hTg=hTg: e.dma_start(
                        out=hTs[k, :, g * 512:(g + 1) * 512], in_=hTg[:, k, :]),
                        reads=[hk], writes=[("hTs", g)])
                ci = 0
                pend = None

                def tail(args):
                    (bank, pb, gcol, gkey, typ, qk, c, dst_d) = args
                    headnorm_b(bank, ("bank", pb), gcol, gkey, banks[2 + pb], ("bank", 2 + pb),
                               sq[pb], ("sq", pb), rs[pb], ("rs", pb), None, None, qn[pb], ("qn", pb))
                    for hp in range(2):
                        P.dma("sp", lambda e, pb=pb, hp=hp, c=c, g=g, dst_d=dst_d: e.dma_start(
                            out=dst_d[2 * c + hp, 0:64, g * 512:(g + 1) * 512],
                            in_=qn[pb][hp * 64:(hp + 1) * 64, :]),
                            reads=[("qn", pb)], writes=[("qkd", typ, qk, 2 * c + hp, g)])

                for typ, qk, off, dst_d, gci in (("sb", "q", OFF["sq"], qsb, None),
                                                 ("sb", "k", OFF["sk"], ksb, 1),
                                                 ("fx", "q", OFF["fq"], qfx, None),
                                                 ("fx", "k", OFF["fk"], kfx, 3)):
                    for c in range(3):
                        pb = ci % 2
                        ci += 1
                        bank = banks[pb]
                        c0 = off + c * 128
                        mm_group(bank, [(wv[:, k, c0:c0 + 128], hTg[:, k, :]) for k in range(8)],
                                 reads=[hk, "slotA"], writes=[("bank", pb)])
                        if qk == "q":
                            qi = 0 if typ == "sb" else 1
                            gcol = colq[:, l * 2 + qi:l * 2 + qi + 1]
                            gkey = "colq"
                        else:
                            gcol = cols[:, l * 5 + gci:l * 5 + gci + 1]
                            gkey = "cols"
                        headnorm_a(bank, ("bank", pb), sq[pb], ("sq", pb))
                        if pend is not None:
                            tail(pend)
                        pend = (bank, pb, gcol, gkey, typ, qk, c, dst_d)
                        if g + 1 < NG:
                            if ci % 3 == 2:
                                prep_norm(g + 1, ci // 3)
                            if ci % 3 == 0:
                                prep_T(g + 1, ci // 3 - 1)
                tail(pend)
                for ti in range(4):
                    vb = (g * 4 + ti) % 2
                    for vi, off in enumerate((OFF["sv"], OFF["fv"])):
                        bank = banks[4 + vi]
                        mm_group(bank[:, 0:384],
                                 [(hTg[:, k, ti * 128:(ti + 1) * 128], wv[:, k, off:off + 384])
                                  for k in range(8)],
                                 reads=[hk, "slotA"], writes=[("bank", 4 + vi)])
                        src = bank[:, 0:384].rearrange("p (h d) -> p h d", d=64)
                        dst = vaug[vb][:, vi * 6:(vi + 1) * 6, 0:64]
                        if vi == 0:
                            P.act(lambda e, src=src, dst=dst: e.activation(dst, src, AF.Copy),
                                  reads=[("bank", 4 + vi)], writes=[("vaug", vb)])
                        else:
                            P.dve(lambda e, src=src, dst=dst: e.tensor_copy(dst, src),
                                  reads=[("bank", 4 + vi)], writes=[("vaug", vb)])
                    t0 = g * 512 + ti * 128
                    P.dma("sp", lambda e, vb=vb, t0=t0: e.dma_start(
                        out=vsc[t0:t0 + 128, :].rearrange("p (h d) -> p h d", d=65), in_=vaug[vb]),
                        reads=[("vaug", vb)], writes=[("vsc", t0)])
                if os.environ.get("SKIPC") == "1":
                    continue
                mm_group(banks[6][0:6, :], [(wv[:, k, OFF["ff"]:OFF["ff"] + 6], hTg[:, k, :]) for k in range(8)],
                         reads=[hk, "slotA"], writes=[("bank", 6)])
                P.act(lambda e: e.activation(spt, banks[6][0:6, :], AF.Exp, bias=nfb[:, l:l + 1], scale=-1.0),
                      reads=[("bank", 6), "nfb"], writes=["spt"])
                P.act(lambda e: e.activation(spt, spt, AF.Ln, bias=onec[0:6, 0:1]), reads=["spt"], writes=["spt"])
                cb = g % 2
                if g == 0:
                    P.dve(lambda e, cb=cb: e.tensor_tensor_scan(cpos[cb], onesb[0:6, :], spt, 0.0,
                                                                ALU.mult, ALU.add),
                          reads=["spt", "onesb"], writes=[("cpos", cb)])
                else:
                    P.dve(lambda e, cb=cb: e.tensor_tensor_scan(cpos[cb], onesb[0:6, :], spt,
                                                                cpos[1 - cb][:, 511:512], ALU.mult, ALU.add),
                          reads=["spt", "onesb", ("cpos", 1 - cb)], writes=[("cpos", cb)])
                P.dve(lambda e, cb=cb: e.tensor_scalar(qh[0], cpos[cb], -1.0, None, ALU.mult),
                      reads=[("cpos", cb)], writes=["qh0"])
                P.dve(lambda e, cb=cb: e.scalar_tensor_tensor(rr, cpos[cb], -1.0, qh[0], ALU.mult, ALU.subtract),
                      reads=[("cpos", cb), "qh0"], writes=["rr"])
                P.dve(lambda e: e.tensor_copy(qh[1], rr), reads=["rr"], writes=["qh1"])
                P.dve(lambda e: e.tensor_tensor(qh[2], rr, qh[1], ALU.subtract), reads=["rr", "qh1"],
                      writes=["qh2"])
                for i in range(3):
                    P.dma("sp", lambda e, i=i, g=g: e.dma_start(out=qfx[:, 64 + i, g * 512:(g + 1) * 512],
                                                                 in_=qh[i]),
                          reads=["qh%d" % i], writes=[("qfxc", i, g)])
                if os.environ.get("SKIPC3") == "1":
                    continue
                for ti in range(4):
                    P.pe(lambda e, ti=ti, cb=cb: e.matmul(banks[6][:, 0:6], cpos[cb][:, ti * 128:(ti + 1) * 128],
                                                         ident32[0:6, 0:6], start=True, stop=True),
                         reads=[("cpos", cb), "ident32"], writes=[("bank", 6)])
                    P.dve(lambda e, ti=ti, g=g: e.tensor_copy(cposTok[:, g * 4 + ti, :], banks[6][:, 0:6]),
                          reads=[("bank", 6)], writes=["cposTok"])

        def phase1b(l, wv):
            AR.reset()
            hT = [AR.get([128, 8, 512], BF16) for _ in range(2)]
            t1 = [AR.get([128, 512], F32) for _ in range(2)]
            t2 = AR.get([128, 512], F32)
            t3 = AR.get([128, 512], F32)
            t4 = [AR.get([128, 512], F32) for _ in range(2)]
            t5 = AR.get([128, 512], F32)
            gs = [AR.get([128, 512], F32) for _ in range(2)]
            qT = AR.get([128, 2, 512], BF16)
            kTz = [[AR.get([128, 512], BF16) for _ in range(2)] for _ in range(2)]
            khT = AR.get([128, 2, 512], BF16)
            khat = AR.get([128, 4, 256], BF16)
            vt = AR.get([128, 4, 256], BF16)
            vtz = [AR.get([128, 4, 256], BF16) for _ in range(2)]
            atb = [AR.get([128, 4, 128], BF16) for _ in range(2)]
            S32 = [AR.get([128, 2, 64], F32) for _ in range(2)]
            Sbz = [[AR.get([128, 2, 64], BF16) for _ in range(2)] for _ in range(4)]
            sq = AR.get([128, 512], BF16)
            rs = AR.get([128, 512], F32)
            mo = [AR.get([128, 512], BF16) for _ in range(2)]
            for j in range(2):
                for hp in range(2):
                    P.pool(lambda e, j=j, hp=hp: e.memset(kTz[j][hp], 0.0), writes=[("kTz", j)])
            for c in range(2):
                P.pool(lambda e, c=c: e.memset(vtz[c], 0.0), writes=["vtz"])
            for r in range(4):
                for hp in range(2):
                    P.pool(lambda e, r=r, hp=hp: e.memset(Sbz[r][hp], 0.0), writes=[("Sbz", r)])
            P.pool(lambda e: e.memset(S32[0], 0.0), writes=[("S32", 0)])
            sidx = 0
            ring = 0
            for g in range(NG):
                hb_ = g % 2
                hTg = hT[hb_]
                hk = ("hT", hb_)
                for k in range(8):
                    P.dma("sp" if k % 2 == 0 else "act", lambda e, k=k, g=g, hTg=hTg: e.dma_start(
                        out=hTg[:, k, :], in_=hTs[k, :, g * 512:(g + 1) * 512]), writes=[hk])
                for j in range(2):
                    bf_, bq_, bg_ = banks[0], banks[1], banks[2]
                    mm_group(bf_, [(wv[:, k, OFF["hf"] + j * 128:OFF["hf"] + (j + 1) * 128], hTg[:, k, :])
                                   for k in range(8)], reads=[hk, "slotA"], writes=[("bank", 0)])
                    mm_group(bq_, [(wv[:, k, OFF["hq"] + j * 128:OFF["hq"] + (j + 1) * 128], hTg[:, k, :])
                                   for k in range(8)], reads=[hk, "slotA"], writes=[("bank", 1)])
                    mm_group(bg_, [(wv[:, k, OFF["hg"] + j * 128:OFF["hg"] + (j + 1) * 128], hTg[:, k, :])
                                   for k in range(8)], reads=[hk, "slotA"], writes=[("bank", 2)])
                    a = t1[j]
                    ak = ("t1", j)
                    P.act(lambda e, a=a: e.activation(a, bf_, AF.Exp, scale=-1.0), reads=[("bank", 0)], writes=[ak])
                    P.dve(lambda e, a=a: e.tensor_scalar(a, a, 1.0, None, ALU.add), reads=[ak], writes=[ak])
                    P.dve(lambda e, a=a: e.reciprocal(a, a), reads=[ak], writes=[ak])
                    ci = j * depth + l
                    P.dve(lambda e, a=a, ci=ci: e.tensor_scalar(a, a, omlb[:, ci:ci + 1], lbc[:, ci:ci + 1],
                                                                ALU.mult, ALU.add),
                          reads=[ak, "omlb", "lbc"], writes=[ak])
                    P.act(lambda e, a=a: e.activation(t2, a, AF.Ln), reads=[ak], writes=["t2"])
                    P.dve(lambda e: e.tensor_tensor_scan(t3, rmask[:, :], t2, 0.0, ALU.mult, ALU.add),
                          reads=["t2", "rmask"], writes=["t3"])
                    e4 = t4[j]
                    P.act(lambda e, e4=e4: e.activation(e4, t3, AF.Exp), reads=["t3"], writes=[("t4", j)])
                    P.act(lambda e: e.activation(t5, t3, AF.Exp, scale=-1.0), reads=["t3"], writes=["t5"])
                    P.dve(lambda e, j=j, e4=e4: e.tensor_tensor(qT[:, j, :], bq_, e4, ALU.mult),
                          reads=[("bank", 1), ("t4", j)], writes=[("qT", j)])
                    P.pool(lambda e, a=a: e.tensor_scalar(a, a, -1.0, 1.0, ALU.mult, ALU.add),
                           reads=[ak], writes=[ak])
                    P.pool(lambda e, a=a: e.tensor_tensor(t5, a, t5, ALU.mult), reads=[ak, "t5"], writes=["t5"])
                    for hp in range(2):
                        sl = slice(hp * 64, (hp + 1) * 64)
                        P.pool(lambda e, j=j, hp=hp, sl=sl: e.tensor_copy(kTz[j][hp][sl, :], t5[sl, :]),
                               reads=["t5"], writes=[("kTz", j)])
                    e4v = e4.rearrange("p (c t) -> p c t", t=64)[:, :, 63:64].to_broadcast([128, 8, 64])
                    P.dve(lambda e, j=j, e4v=e4v: e.tensor_tensor(
                        khT[:, j, :].rearrange("p (c t) -> p c t", t=64),
                        t5[:, :].rearrange("p (c t) -> p c t", t=64), e4v, ALU.mult),
                        reads=["t5", ("t4", j)], writes=[("khT", j)])
                    gg = gs[j]
                    gk = ("gs", j)
                    P.act(lambda e, gg=gg: e.activation(gg, bg_, AF.Exp, scale=-1.0), reads=[("bank", 2)], writes=[gk])
                    P.dve(lambda e, gg=gg: e.tensor_scalar(gg, gg, 1.0, None, ALU.add), reads=[gk], writes=[gk])
                    P.dve(lambda e, gg=gg: e.reciprocal(gg, gg), reads=[gk], writes=[gk])
                    P.dve(lambda e, gg=gg: e.tensor_tensor(gg, bg_, gg, ALU.mult), reads=[("bank", 2), gk], writes=[gk])
                B1 = int(os.environ.get("B1", "9"))
                if B1 <= 1:
                    continue
                for ti in range(4):
                    tsl = slice(ti * 128, (ti + 1) * 128)
                    mm_group(banks[3][:, 0:256],
                             [(hTg[:, k, tsl], wv[:, k, OFF["hi"]:OFF["hi"] + 256]) for k in range(8)],
                             reads=[hk, "slotA"], writes=[("bank", 3)])
                    P.act(lambda e, ti=ti: e.activation(vt[:, ti, :], banks[3][:, 0:256], AF.Copy),
                          reads=[("bank", 3)], writes=["vt"])
                    for c in range(2):
                        sl = slice(c * 64, (c + 1) * 64)
                        P.dve(lambda e, ti=ti, c=c, sl=sl: e.tensor_copy(vtz[c][sl, ti, :], banks[3][sl, 0:256]),
                              reads=[("bank", 3)], writes=["vtz"])
                    pv = bankbf(3)

                    def fnT(e, tsl=tsl, pv=pv):
                        ins = None
                        for j in range(2):
                            ins = e.transpose(pv[:, 512 + j * 128:512 + (j + 1) * 128], khT[:, j, tsl], ident[:, :])
                        return ins
                    P.pe(fnT, reads=[("khT", 0), ("khT", 1), "ident"], writes=[("bank", 3)])
                    P.act(lambda e, ti=ti, pv=pv: e.activation(khat[:, ti, :], pv[:, 512:768], AF.Copy),
                          reads=[("bank", 3)], writes=["khat"])
                for ti in range(4):
                    if B1 <= 2:
                        continue
                    tsl = slice(ti * 128, (ti + 1) * 128)
                    ab = ti % 2
                    def fnA(e, tsl=tsl):
                        ins = None
                        for h in range(4):
                            j, hp = h // 2, h % 2
                            ins = e.matmul(banks[4][:, h * 128:(h + 1) * 128], kTz[j][hp][:, tsl], qT[:, j, tsl],
                                           start=True, stop=True)
                        return ins
                    P.pe(fnA, reads=[("kTz", 0), ("kTz", 1), ("qT", 0), ("qT", 1)], writes=[("bank", 4)])
                    P.dve(lambda e, ab=ab: e.tensor_tensor(atb[ab], banks[4].rearrange("p (h t) -> p h t", t=128),
                                                           hmask[:, :, :], ALU.mult),
                          reads=[("bank", 4), "hmask"], writes=[("atb", ab)])
                    def fnU(e, ti=ti):
                        ins = None
                        for c in range(2):
                            for h in range(4):
                                j = h // 2
                                ins = e.matmul(banks[5][:, (c * 4 + h) * 64:(c * 4 + h + 1) * 64],
                                               khat[:, ti, j * 128:(j + 1) * 128],
                                               vtz[c][:, ti, h * 64:(h + 1) * 64], start=True, stop=True)
                        return ins
                    P.pe(fnU, reads=["khat", "vtz"], writes=[("bank", 5)])
                    rings = [ring]
                    for c in range(2):
                        chunkcol = (ti * 2 + c) * 64 + 63
                        so, sn = S32[sidx], S32[1 - sidx]
                        for h in range(4):
                            j, hp = h // 2, h % 2
                            sl = slice(hp * 64, (hp + 1) * 64)
                            P.dve(lambda e, so=so, sn=sn, j=j, sl=sl, c=c, h=h, chunkcol=chunkcol:
                                  e.scalar_tensor_tensor(sn[sl, j, :], so[sl, j, :],
                                                         t4[j][sl, chunkcol:chunkcol + 1],
                                                         banks[5][sl, (c * 4 + h) * 64:(c * 4 + h + 1) * 64],
                                                         ALU.mult, ALU.add),
                                  reads=[("S32", sidx), ("t4", j), ("bank", 5)], writes=[("S32", 1 - sidx)])
                        sidx = 1 - sidx
                        ring = (ring + 1) % 4
                        rings.append(ring)
                        for hp in range(2):
                            sl = slice(hp * 64, (hp + 1) * 64)
                            P.act(lambda e, sl=sl, hp=hp, ring=ring, sidx=sidx:
                                  e.activation(Sbz[ring][hp][sl, :, :], S32[sidx][sl, :, :], AF.Copy),
                                  reads=[("S32", sidx)], writes=[("Sbz", ring)])
                    if B1 <= 3:
                        continue
                    def fnO(e, ti=ti, tsl=tsl, ab=ab, rings=tuple(rings)):
                        ins = None
                        for h in range(4):
                            j, hp = h // 2, h % 2
                            osl = slice(hp * 64, (hp + 1) * 64)
                            ob = banks[6 + j]
                            e.matmul(ob[osl, tsl], vt[:, ti, h * 64:(h + 1) * 64], atb[ab][:, h, :],
                                     start=True, stop=False, skip_group_check=True)
                            for c in range(2):
                                csl = slice(ti * 128 + c * 64, ti * 128 + (c + 1) * 64)
                                ins = e.matmul(ob[osl, csl], Sbz[rings[c]][hp][:, j, :], qT[:, j, csl],
                                               start=False, stop=(c == 1), skip_group_check=True)
                        return ins
                    P.pe(fnO, reads=["vt", ("atb", ab), ("Sbz", rings[0]), ("Sbz", rings[1]), ("qT", 0), ("qT", 1)],
                         writes=[("bank", 6), ("bank", 7)])
                for j in range(2):
                    if B1 <= 4:
                        continue
                    gcol = cols[:, l * 5 + 4:l * 5 + 5]
                    headnorm(banks[6 + j], ("bank", 6 + j), gcol, "cols", banks[j], ("bank", j),
                             sq, "sq", rs, "rs", gs[j], ("gs", j), mo[j], ("mo", j))
                    P.dma("sp", lambda e, j=j, g=g: e.dma_start(out=mixT[j, :, g * 512:(g + 1) * 512], in_=mo[j]),
                          reads=[("mo", j)], writes=[("mixT", j, g)])

        def phase2(l):
            AR.reset()
            kT = [AR.get([67, T], BF16) for _ in range(2)]
            qTt = [AR.get([67, T], BF16) for _ in range(2)]
            vv = [AR.get([128, NT, 65], BF16) for _ in range(2)]
            ee = [AR.get([128, 512], F32) for _ in range(2)]
            spb = [AR.get([128, 512], BF16) for _ in range(3)]
            ssum = AR.get([128, 512], F32)
            ssb = [AR.get([128, 512], BF16) for _ in range(3)]
            aa = [AR.get([128, 512], BF16) for _ in range(4)]
            osb = AR.get([65, 512], F32)
            rec = AR.get([65, 512], F32)
            on = [AR.get([64, 512], BF16) for _ in range(2)]

            for b_ in range(2):
                P.pool(lambda e, b_=b_: e.memset(kT[b_][64:67, :], 0.0), writes=[("kT", b_)])
                P.pool(lambda e, b_=b_: e.memset(qTt[b_][64:67, :], 0.0), writes=[("qTt", b_)])

            def load_head(hh):
                typ = "sb" if hh < 6 else "fx"
                h = hh % 6
                hb_ = hh % 2
                nr = 64 if typ == "sb" else 67
                qd, kd = (qsb, ksb) if typ == "sb" else (qfx, kfx)
                P.dma("sp", lambda e: e.dma_start(out=kT[hb_][0:nr, :], in_=kd[h, 0:nr, :]), writes=[("kT", hb_)])
                P.dma("act", lambda e: e.dma_start(out=qTt[hb_][0:nr, :], in_=qd[h, 0:nr, :]), writes=[("qTt", hb_)])
                P.dma("sp", lambda e: e.dma_start(
                    out=vv[hb_], in_=vsc[:, hh * 65:(hh + 1) * 65].rearrange("(n p) d -> p n d", p=128)),
                    writes=[("vv", hb_)])

            items = []
            gi = 0
            for hh in range(12):
                typ = "sb" if hh < 6 else "fx"
                for B in range(NG):
                    na = 4 * B + 4
                    order = list(range(na - 1, -1, -1)) if typ == "sb" else list(range(na))
                    for idx, a in enumerate(order):
                        items.append(dict(hh=hh, typ=typ, h=hh % 6, hb=hh % 2, nr=67,
                                          B=B, a=a, r=a - 4 * B, first=(idx == 0), last=(idx == na - 1),
                                          ob=gi % 2, newhead=(B == 0 and idx == 0)))
                    gi += 1
            n = len(items)
            load_head(0)

            def geom(it):
                r = it["r"]
                c0 = max(r, 0) * 128
                return r, c0, slice(c0, 512), 512 - c0

            def stageA(t, it):
                hb_, nr, a = it["hb"], it["nr"], it["a"]
                r, c0, cs, W = geom(it)
                ks = slice(a * 128, (a + 1) * 128)
                qs = slice(it["B"] * 512 + c0, (it["B"] + 1) * 512)
                zi = t % 3
                zb, zk = banks[zi], ("bank", zi)
                sbt = (it["typ"] == "sb")
                P.pe(lambda e: e.matmul(zb[:, cs], kT[hb_][0:nr, ks], qTt[hb_][0:nr, qs], start=True, stop=(not sbt),
                                        skip_group_check=True),
                     reads=[("kT", hb_), ("qTt", hb_)], writes=[zk])
                if it["typ"] == "fx":
                    ai = t % 4
                    h = it["h"]
                    P.act(lambda e: e.activation(aa[ai][:, cs], zb[:, cs], AF.Exp, bias=cposTok[:, a, h:h + 1]),
                          reads=[zk, "cposTok"], writes=[("aa", ai)])
                    if r >= 0:
                        P.pool(lambda e: e.tensor_tensor(aa[ai][:, cs], aa[ai][:, cs], mfxw[:, 384:384 + W], ALU.mult),
                               reads=[("aa", ai), "mfx"], writes=[("aa", ai)])
                else:
                    ei, si = t % 2, t % 3
                    P.act(lambda e: e.activation(ee[ei][:, cs], zb[:, cs], AF.Exp), reads=[zk], writes=[("ee", ei)])
                    it["_ln"] = (ei, si)

            def stageA2(t, it):
                if it["typ"] != "sb":
                    return
                ei, si = it["_ln"]
                a = it["a"]
                r, c0, cs, W = geom(it)
                P.act(lambda e: e.activation(spb[si][:, cs], ee[ei][:, cs], AF.Ln, bias=1.0), reads=[("ee", ei)],
                      writes=[("spb", si)])
                if r >= 0:
                    P.pool(lambda e: e.tensor_tensor(spb[si][:, cs], spb[si][:, cs], msbw[:, 384:384 + W], ALU.mult),
                           reads=[("spb", si), "msb"], writes=[("spb", si)])
                if c0 > 0:
                    P.pool(lambda e: e.memset(spb[si][:, 0:c0], 0.0), reads=[("spb", si)], writes=[("spb", si)])
                if a > 0:
                    nsi = (t + 1) % 3
                    if it["first"]:
                        P.dve(lambda e: e.tensor_copy(ssb[nsi], spb[si]), reads=[("spb", si)], writes=[("ssb", nsi)])
                        P.dve(lambda e: e.tensor_copy(ssum, spb[si]), reads=[("spb", si)], writes=["ssum"])
                    else:
                        P.dve(lambda e: e.tensor_tensor(ssb[nsi], ssum, spb[si], ALU.add),
                              reads=["ssum", ("spb", si)], writes=[("ssb", nsi)])
                        P.dve(lambda e: e.tensor_tensor(ssum, ssum, spb[si], ALU.add),
                              reads=["ssum", ("spb", si)], writes=["ssum"])

            def stageB_pe(t, it):
                if it["typ"] != "sb":
                    return
                r, c0, cs, W = geom(it)
                si = t % 3
                li = t % 3
                lb_, lk = banks[li], ("bank", li)
                prs = [(negtri[:, :], spb[si][:, cs])]
                rd = [("spb", si), "negtri"]
                if not it["first"]:
                    prs.append((negones[:, :], ssb[si][:, cs]))
                    rd += [("ssb", si), "negones"]

                def fnB(e, prs=prs, lb_=lb_):
                    ins = None
                    for i_, (lt, r_) in enumerate(prs):
                        ins = e.matmul(lb_[:, cs], lt, r_, start=False, stop=(i_ == len(prs) - 1), skip_group_check=True)
                    return ins
                P.pe(fnB, reads=rd + [lk], writes=[lk])

            def stageB_act(t, it):
                if it["typ"] != "sb":
                    return
                r, c0, cs, W = geom(it)
                ai = t % 4
                li = t % 3
                lb_, lk = banks[li], ("bank", li)
                P.act(lambda e: e.activation(aa[ai][:, cs], lb_[:, cs], AF.Exp), reads=[lk], writes=[("aa", ai)])
                if r >= 0:
                    P.pool(lambda e: e.tensor_tensor(aa[ai][:, cs], aa[ai][:, cs], msbw[:, 384:384 + W], ALU.mult),
                           reads=[("aa", ai), "msb"], writes=[("aa", ai)])

            def stageC(t, it):
                if it["newhead"] and it["hh"] + 1 < 12:
                    load_head(it["hh"] + 1)
                hb_, a = it["hb"], it["a"]
                r, c0, cs, W = geom(it)
                ai = t % 4
                ob = banks[6 + it["ob"]]
                okey = ("bank", 6 + it["ob"])
                first, last = it["first"], it["last"]
                P.pe(lambda e: e.matmul(ob[0:65, cs], vv[hb_][:, a, :], aa[ai][:, cs], start=first, stop=last,
                                        skip_group_check=True),
                     reads=[("vv", hb_), ("aa", ai)], writes=[okey])
                if not last:
                    return
                hh, B = it["hh"], it["B"]
                qs = slice(B * 512, (B + 1) * 512)
                ob_ = it["ob"]
                if it["typ"] == "fx":
                    P.dve(lambda e: e.reciprocal(rec[64:65, :], ob[64:65, :]), reads=[okey], writes=["rec"])
                    P.dve(lambda e: e.tensor_copy(osb[0:64, :], ob[0:64, :]), reads=[okey], writes=["osb"])
                    mm_group(banks[4][0:64, :], [(ones32[64:65, 0:64], rec[64:65, :])], reads=["rec", "ones32"],
                             writes=[("bank", 4)])
                    P.dve(lambda e: e.tensor_tensor(on[ob_], osb[0:64, :], banks[4][0:64, :], ALU.mult),
                          reads=["osb", ("bank", 4)], writes=[("on", ob_)])
                else:
                    P.dve(lambda e: e.tensor_copy(on[ob_], ob[0:64, :]), reads=[okey], writes=[("on", ob_)])
                fch = 2 + hh // 2
                prow = (hh % 2) * 64
                P.dma("sp", lambda e: e.dma_start(out=mixT[fch, prow:prow + 64, qs], in_=on[ob_]),
                      reads=[("on", ob_)], writes=[("mixT", hh, B)])

            for t in range(n + 3):
                if 0 <= t - 2 < n:
                    stageB_act(t - 2, items[t - 2])
                if t < n:
                    stageA(t, items[t])
                if 0 <= t - 1 < n:
                    stageB_pe(t - 1, items[t - 1])
                if t < n:
                    stageA2(t, items[t])
                if 0 <= t - 3 < n:
                    stageC(t - 3, items[t - 3])

        def phase3a(l, xsrc, wo):
            AR.reset()
            mx = [AR.get([128, 8, 512], BF16) for _ in range(2)]
            xt = [AR.get([128, D], F32) for _ in range(3)]
            it = 0
            for g in range(NG):
                mb = g % 2
                for k in range(8):
                    P.dma("sp" if k % 2 == 0 else "act", lambda e, k=k, g=g, mb=mb: e.dma_start(
                        out=mx[mb][:, k, :], in_=mixT[k, :, g * 512:(g + 1) * 512]), writes=[("mx", mb)])
                for ti in range(4):
                    t0 = g * 512 + ti * 128
                    b = it % 3
                    it += 1
                    P.dma("sp", lambda e, b=b, t0=t0: e.dma_start(out=xt[b], in_=xsrc[t0:t0 + 128, :]),
                          writes=[("xt", b)])
                    for hf in range(2):
                        pb = (it * 2 + hf) % 4
                        mm_group(banks[pb], [(mx[mb][:, k, ti * 128:(ti + 1) * 128], wo[:, k, hf * 512:(hf + 1) * 512])
                                             for k in range(8)],
                                 reads=[("mx", mb), "slotB"], writes=[("bank", pb)])
                        P.dve(lambda e, b=b, hf=hf, pb=pb: e.tensor_tensor(
                            xt[b][:, hf * 512:(hf + 1) * 512], xt[b][:, hf * 512:(hf + 1) * 512], banks[pb], ALU.add),
                            reads=[("xt", b), ("bank", pb)], writes=[("xt", b)])
                    P.dma("act", lambda e, b=b, t0=t0: e.dma_start(out=xa[t0:t0 + 128, :], in_=xt[b]),
                          reads=[("xt", b)], writes=[("xa", t0)])

        def phase3f(l, half, src, dst, w1, w2, wkey):
            AR.reset()
            xt = [AR.get([128, D], F32) for _ in range(4)]
            hT = [AR.get([128, 8, 512], BF16) for _ in range(2)]
            hid = AR.get([128, 16, 512], BF16)
            rl = [AR.get([128, 512], F32) for _ in range(2)]
            if half == 0:
                gbc = AR.get([128, D], F32)
                sqs = AR.get([128, D], BF16)
                hb = [AR.get([128, D], BF16) for _ in range(2)]
                st = [AR.get([128, 4], F32) for _ in range(2)]
                P.dma("sp", lambda e: e.dma_start(out=gbc, in_=n2g[l, :].partition_broadcast(128)), writes=["gbc"])
            it = 0
            for g in range(NG):
                hb_ = g % 2
                hTg = hT[hb_]
                hk = ("hT", hb_)
                for ti in range(4):
                    t0 = g * 512 + ti * 128
                    P.dma("sp", lambda e, ti=ti, t0=t0: e.dma_start(out=xt[ti], in_=src[t0:t0 + 128, :]),
                          writes=[("xt", ti)])
                if half == 0:
                    for ti in range(4):
                        b = it % 2
                        it += 1
                        norm_tile(xt[ti], ("xt", ti), gbc, hb[b], ("hb", b), st[b], ("st", b), sqs, "sqs")
                        transpose_h(hb[b], ("hb", b), hTg, hk, ti * 128, 7, ("bank", 7))
                    for k in range(8):
                        P.dma("act", lambda e, k=k, g=g, hTg=hTg: e.dma_start(
                            out=hTs[k, :, g * 512:(g + 1) * 512], in_=hTg[:, k, :]), reads=[hk], writes=[("hTs", g)])
                else:
                    for k in range(8):
                        P.dma("sp" if k % 2 == 0 else "act", lambda e, k=k, g=g, hTg=hTg: e.dma_start(
                            out=hTg[:, k, :], in_=hTs[k, :, g * 512:(g + 1) * 512]), writes=[hk])
                for f in range(16):
                    pb = f % 2
                    mm_group(banks[pb], [(w1[:, k, f * 128:(f + 1) * 128], hTg[:, k, :]) for k in range(8)],
                             reads=[hk, wkey], writes=[("bank", pb)])
                    P.act(lambda e, pb=pb: e.activation(rl[pb], banks[pb], AF.Relu), reads=[("bank", pb)],
                          writes=[("rl", pb)])
                    if f % 2 == 0:
                        P.dve(lambda e, pb=pb, f=f: e.tensor_tensor(hid[:, f, :], rl[pb], rl[pb], ALU.mult),
                              reads=[("rl", pb)], writes=["hid"])
                    else:
                        P.pool(lambda e, pb=pb, f=f: e.tensor_tensor(hid[:, f, :], rl[pb], rl[pb], ALU.mult),
                               reads=[("rl", pb)], writes=["hid"])
                for ti in range(4):
                    t0 = g * 512 + ti * 128
                    for hf in range(2):
                        pb = 2 + (ti * 2 + hf) % 4
                        mm_group(banks[pb], [(hid[:, f, ti * 128:(ti + 1) * 128], w2[:, f, hf * 512:(hf + 1) * 512])
                                             for f in range(16)],
                                 reads=["hid", wkey], writes=[("bank", pb)])
                        P.dve(lambda e, ti=ti, hf=hf, pb=pb: e.tensor_tensor(
                            xt[ti][:, hf * 512:(hf + 1) * 512], xt[ti][:, hf * 512:(hf + 1) * 512], banks[pb],
                            ALU.add), reads=[("xt", ti), ("bank", pb)], writes=[("xt", ti)])
                    P.dma("act", lambda e, ti=ti, t0=t0: e.dma_start(out=dst[t0:t0 + 128, :], in_=xt[ti]),
                          reads=[("xt", ti)], writes=[("dst", t0)])

        def whole():
            setup_consts()
            if stop == -1:
                return
            wv = load_w_in(0)
            P.barrier()
            if stop == 0:
                return
            for l in range(depth):
                xsrc = x_in if l == 0 else xb
                wo = load_w_out(l)
                phase1a(l, xsrc, wv)
                P.barrier()
                if stop == 1:
                    return
                phase1b(l, wv)
                P.barrier()
                if stop == 2:
                    return
                w1a, w2a = load_ffn_half(l, 0, slotA, "slotA")
                phase2(l)
                P.barrier()
                if stop == 3:
                    return
                if dbg and l == 0:
                    for nm, srcd in (("d_mixT", mixT), ("d_qfx", qfx), ("d_kfx", kfx), ("d_qsb", qsb), ("d_ksb", ksb)):
                        dd = dbg_out[nm]
                        for i in range(dd.shape[0]):
                            P.dma("sp", lambda e, dd=dd, srcd=srcd, i=i: e.dma_start(out=dd[i], in_=srcd[i]))
                phase3a(l, xsrc, wo)
                P.barrier()
                if stop == 4:
                    return
                if dbg and l == 0:
                    for t0 in range(0, T, 128):
                        P.dma("sp", lambda e, t0=t0: e.dma_start(out=dbg_out["d_xa"][t0:t0 + 128, :], in_=xa[t0:t0 + 128, :]))
                w1b, w2b = load_ffn_half(l, 1, slotB, "slotB")
                phase3f(l, 0, xa, xp, w1a, w2a, "slotA")
                P.barrier()
                if stop == 5:
                    return
                if l + 1 < depth:
                    wv = load_w_in(l + 1)
                phase3f(l, 1, xp, (y if l == depth - 1 else xb), w1b, w2b, "slotB")
                P.barrier()
        whole()
        P.barrier()
        stats = P.emit()
    return nc, stats


_CACHE = {}


def _host_layout(inputs, depth):
    f = lambda a: np.ascontiguousarray(np.asarray(a, dtype=np.float32))
    cols = np.zeros((128, depth * 5), np.float32)
    for l in range(depth):
        for i, nm in enumerate(("sb_q_norm_g", "sb_k_norm_g", "fox_q_norm_g", "fox_k_norm_g", "hg_norm_g")):
            cols[:, l * 5 + i] = np.tile(f(inputs[nm])[l], 2)
    lb = f(inputs["lb_logits"])
    lbT = np.zeros((128, 2 * depth), np.float32)
    for j in range(2):
        lbT[:, j * depth:(j + 1) * depth] = lb[:, j * 128:(j + 1) * 128].T
    fb = np.ascontiguousarray(f(inputs["fox_f_bias"]).T)
    return cols, lbT, fb


def run(inputs, T, depth, n_cores, dbg=False, stop=99):
    key = (T, depth, dbg, stop)
    if key not in _CACHE:
        _CACHE[key] = build(T, depth, dbg, stop)
    nc, stats = _CACHE[key]
    f = lambda a: np.ascontiguousarray(np.asarray(a, dtype=np.float32))
    cols, lbT, fb = _host_layout(inputs, depth)
    shared = {
        "w_in": f(inputs["w_in"]), "w_out": f(inputs["w_out"]), "w_ff1": f(inputs["w_ff1"]),
        "w_ff2": f(inputs["w_ff2"]), "norm1_g": f(inputs["norm1_g"]), "norm2_g": f(inputs["norm2_g"]),
        "cols": cols, "lbT": lbT, "fb": fb,
    }
    x = f(inputs["x"])
    in_maps = []
    for c in range(n_cores):
        m = dict(shared)
        m["x"] = np.ascontiguousarray(x[c])
        in_maps.append(m)
    res = run_bass_kernel_spmd(nc, in_maps, core_ids=list(range(n_cores)))
    return res


def kernel(**inputs):
    res = run(inputs, 4096, 4, 8)
    return np.stack([np.asarray(r["y"], dtype=np.float32) for r in res.results], axis=0)
```
